# Optimizing a Trainium2 kernel written in Bass

```python
import jax, jax.numpy as jnp
from jax import lax
import numpy as np

D_MODEL = 4096
BATCH = 8
SEQ = 2048
DEPTH = 2
DEC_BATCH = 32
DEC_SEQ = 32
PAST_LEN = 1024

CHUNK = 64
Q_BLOCK = 128
EPS = 1e-6
CONV_WIDTH = 1024
CONV_K = 31
CONV_STATE = CONV_K - 1
M_HEADS = 4
M_DK = 128
M_DV = 256
M_WIDTH = M_HEADS * M_DV
FORGET_BIAS = 3.0
A_HEADS = 16
NOPE_DIM = 128
ROPE_DIM = 64
V_DIM = 128
Q_LORA = 1024
KV_LORA = 512
A_WIDTH = A_HEADS * V_DIM
ROPE_THETA = 10000.0
ATTN_SCALE = (NOPE_DIM + ROPE_DIM) ** -0.5
N_BRANCH = 3

IN_SIZES = (CONV_WIDTH, CONV_WIDTH, CONV_WIDTH,
            M_HEADS * M_DK, M_HEADS * M_DK, M_WIDTH,
            M_HEADS, M_HEADS, M_WIDTH, M_WIDTH,
            Q_LORA, KV_LORA, ROPE_DIM, A_WIDTH,
            N_BRANCH * D_MODEL)
N_IN = sum(IN_SIZES)
SPLIT_AT = tuple(int(s) for s in np.cumsum(IN_SIZES)[:-1])
F_OFFSET = sum(IN_SIZES[:7])

kernel_name = "hybrid_conv_mlstm_mla_stream_step"


def rmsnorm(x, g):
    xf = x.astype(jnp.float32)
    y = xf * lax.rsqrt(jnp.mean(xf * xf, axis=-1, keepdims=True) + EPS)
    return (y * g.astype(jnp.float32)).astype(x.dtype)


def layernorm(x, g, b):
    xf = x.astype(jnp.float32)
    mu = jnp.mean(xf, axis=-1, keepdims=True)
    var = jnp.mean(jnp.square(xf - mu), axis=-1, keepdims=True)
    y = (xf - mu) * lax.rsqrt(var + EPS) * g.astype(jnp.float32) + b.astype(jnp.float32)
    return y.astype(x.dtype)


def rope(x, pos):
    half = ROPE_DIM // 2
    freqs = ROPE_THETA ** (-jnp.arange(half, dtype=jnp.float32) / half)
    ang = pos.astype(jnp.float32)[:, None] * freqs
    shape = (1, pos.shape[0]) + (1,) * (x.ndim - 3) + (half,)
    cos = jnp.cos(ang).reshape(shape)
    sin = jnp.sin(ang).reshape(shape)
    xf = x.astype(jnp.float32)
    x1, x2 = xf[..., :half], xf[..., half:]
    return jnp.concatenate([x1 * cos - x2 * sin, x2 * cos + x1 * sin], axis=-1).astype(x.dtype)


def causal_dwconv(u, prev, w, b):
    full = jnp.concatenate([prev.astype(u.dtype), u], axis=1)
    out = lax.conv_general_dilated(full, w[:, None, :].astype(u.dtype), window_strides=(1,), padding='VALID',
                                   dimension_numbers=('NWC', 'WIO', 'NWC'), feature_group_count=u.shape[-1])
    return out + b, full[:, -CONV_STATE:]


def mlstm_chunk(carry, inp):
    C, n, m = carry
    q, k, v, li, lf = inp
    L = q.shape[1]
    bt = jnp.cumsum(lf, axis=1).transpose(0, 2, 1)
    lit = li.transpose(0, 2, 1)
    causal = jnp.tril(jnp.ones((L, L), dtype=bool))
    logw = jnp.where(causal, bt[..., :, None] - bt[..., None, :] + lit[..., None, :], -jnp.inf)
    a = bt + m[..., None]
    m_t = jnp.maximum(a, jnp.max(logw, axis=-1))
    w_inter = jnp.exp(a - m_t)
    sw = jnp.exp(logw - m_t[..., None]) * jnp.einsum('blhk,bshk->bhls', q, k)
    num = (jnp.einsum('bhls,bshv->blhv', sw, v)
           + jnp.einsum('blhk,bhkv->blhv', q, C) * w_inter.transpose(0, 2, 1)[..., None])
    den = jnp.sum(sw, axis=-1) + w_inter * jnp.einsum('blhk,bhk->bhl', q, n)
    denom = jnp.maximum(jnp.abs(den), jnp.exp(-m_t)).transpose(0, 2, 1)[..., None]
    h = num / denom
    b_last = bt[..., -1]
    gs = b_last[..., None] - bt + lit
    m_new = jnp.maximum(b_last + m, jnp.max(gs, axis=-1))
    ws = jnp.exp(gs - m_new[..., None])
    decay = jnp.exp(b_last + m - m_new)
    C_new = decay[..., None, None] * C + jnp.einsum('bhs,bshk,bshv->bhkv', ws, k, v)
    n_new = decay[..., None] * n + jnp.einsum('bhs,bshk->bhk', ws, k)
    return (C_new, n_new, m_new), h


def mlstm(q, k, v, li, lf, C, n, m):
    B, L = q.shape[:2]
    if L <= CHUNK:
        (C, n, m), h = mlstm_chunk((C, n, m), (q, k, v, li, lf))
        return h, C, n, m
    nc = L // CHUNK

    def to_chunks(t):
        return jnp.moveaxis(t.reshape((B, nc, CHUNK) + t.shape[2:]), 1, 0)

    (C, n, m), hs = lax.scan(mlstm_chunk, (C, n, m), (to_chunks(q), to_chunks(k), to_chunks(v), to_chunks(li), to_chunks(lf)))
    h = jnp.moveaxis(hs, 0, 1).reshape((B, L) + hs.shape[3:])
    return h, C, n, m


def chunk_attention(qn, qr, qpos, kn, kr, v, kpos):
    s = (jnp.einsum('bqhd,bkhd->bhqk', qn, kn) + jnp.einsum('bqhd,bkd->bhqk', qr, kr)).astype(jnp.float32) * ATTN_SCALE
    mask = (kpos[None, :] // CHUNK) <= (qpos[:, None] // CHUNK)
    s = jnp.where(mask, s, -jnp.inf)
    p = jax.nn.softmax(s, axis=-1).astype(v.dtype)
    return jnp.einsum('bhqk,bkhd->bqhd', p, v)


def mla_attend(qn, qr, qpos, kn, kr, v, kpos):
    B, L = qn.shape[:2]
    if L <= Q_BLOCK or L % Q_BLOCK:
        return chunk_attention(qn, qr, qpos, kn, kr, v, kpos)
    nb = L // Q_BLOCK

    def blocks(t):
        return jnp.moveaxis(t.reshape((B, nb, Q_BLOCK) + t.shape[2:]), 1, 0)

    out = lax.map(lambda a: chunk_attention(a[0], a[1], a[2], kn, kr, v, kpos),
                  (blocks(qn), blocks(qr), qpos.reshape(nb, Q_BLOCK)))
    return jnp.moveaxis(out, 0, 1).reshape((B, L) + out.shape[3:])


def mixer_layer(x, pos, conv_prev, mC, mn, mm, lat_past, kr_past,
                ln_g, w_in, b_in, conv_w, conv_b, conv_ln_g, conv_ln_b, w_pc, m_norm_g, w_pm,
                cq_g, ckv_g, qn_g, qr_g, kn_g, kr_g, w_uq, w_ukv, w_pa, w_out):
    B, L, _ = x.shape
    h = rmsnorm(x, ln_g)
    proj = h @ w_in + b_in
    (c_a, c_b, c_z, m_q, m_k, m_v, m_i, m_f, m_o, m_z,
     a_cq, a_ckv, a_kr, a_z, gates) = jnp.split(proj, SPLIT_AT, axis=-1)

    u = c_a * jax.nn.sigmoid(c_b)
    cv, conv_new = causal_dwconv(u, conv_prev, conv_w, conv_b)
    cv = jax.nn.silu(layernorm(cv, conv_ln_g, conv_ln_b)) * jax.nn.silu(c_z)
    y_c = cv @ w_pc

    q = m_q.reshape(B, L, M_HEADS, M_DK).astype(jnp.float32)
    k = m_k.reshape(B, L, M_HEADS, M_DK).astype(jnp.float32) * (M_DK ** -0.5)
    v = m_v.reshape(B, L, M_HEADS, M_DV).astype(jnp.float32)
    li = m_i.astype(jnp.float32)
    lf = jax.nn.log_sigmoid(m_f.astype(jnp.float32))
    hm, mC, mn, mm = mlstm(q, k, v, li, lf, mC, mn, mm)
    hm = rmsnorm(hm, m_norm_g.reshape(M_HEADS, M_DV)).reshape(B, L, M_WIDTH).astype(x.dtype)
    hm = hm * jax.nn.sigmoid(m_o) * jax.nn.silu(m_z)
    y_m = hm @ w_pm

    qa = (rmsnorm(a_cq, cq_g) @ w_uq).reshape(B, L, A_HEADS, NOPE_DIM + ROPE_DIM)
    q_nope = rmsnorm(qa[..., :NOPE_DIM], qn_g)
    q_rope = rope(rmsnorm(qa[..., NOPE_DIM:], qr_g), pos)
    ckv = rmsnorm(a_ckv, ckv_g)
    kr_new = rope(rmsnorm(a_kr, kr_g), pos)
    if lat_past is None:
        lat_all, kr_all, kpos = ckv, kr_new, pos
    else:
        lat_all = jnp.concatenate([lat_past.astype(ckv.dtype), ckv], axis=1)
        kr_all = jnp.concatenate([kr_past.astype(kr_new.dtype), kr_new], axis=1)
        kpos = jnp.arange(lat_all.shape[1], dtype=jnp.int32)
    T = lat_all.shape[1]
    kv = (lat_all @ w_ukv).reshape(B, T, A_HEADS, NOPE_DIM + V_DIM)
    k_nope = rmsnorm(kv[..., :NOPE_DIM], kn_g)
    va = kv[..., NOPE_DIM:]
    ao = mla_attend(q_nope, q_rope, pos, k_nope, kr_all, va, kpos).reshape(B, L, A_WIDTH)
    y_a = (ao * jax.nn.silu(a_z)) @ w_pa

    g_c, g_m, g_a = jnp.split(jax.nn.sigmoid(gates), N_BRANCH, axis=-1)
    y = (g_c * y_c + g_m * y_m + g_a * y_a) @ w_out
    return x + y, conv_new, mC, mn, mm, ckv, kr_new


def setup_inputs(seed: int = 0) -> dict:
    key = jax.random.key(seed)
    ks = jax.random.split(key, 32)

    def nrm(k, shape, scale):
        return jax.random.normal(k, shape, jnp.float32) * scale

    b_in = nrm(ks[10], (DEPTH, N_IN), 0.01).at[:, F_OFFSET:F_OFFSET + M_HEADS].add(FORGET_BIAS)
    return {
        "x_prompt": nrm(ks[0], (BATCH, SEQ, D_MODEL), 1.0),
        "x_sample": nrm(ks[1], (DEC_BATCH, DEC_SEQ, D_MODEL), 1.0),
        "cache_kv_latent": nrm(ks[2], (DEPTH, DEC_BATCH, PAST_LEN, KV_LORA), 1.0),
        "cache_k_rope": nrm(ks[3], (DEPTH, DEC_BATCH, PAST_LEN, ROPE_DIM), 1.0),
        "state_conv": nrm(ks[4], (DEPTH, DEC_BATCH, CONV_STATE, CONV_WIDTH), 0.5),
        "state_mlstm_C": nrm(ks[5], (DEPTH, DEC_BATCH, M_HEADS, M_DK, M_DV), 0.1),
        "state_mlstm_n": nrm(ks[6], (DEPTH, DEC_BATCH, M_HEADS, M_DK), 0.1),
        "state_mlstm_m": nrm(ks[7], (DEPTH, DEC_BATCH, M_HEADS), 1.0),
        "ln_g": 1.0 + nrm(ks[8], (DEPTH, D_MODEL), 0.02),
        "w_in": nrm(ks[9], (DEPTH, D_MODEL, N_IN), D_MODEL ** -0.5),
        "b_in": b_in,
        "conv_w": nrm(ks[11], (DEPTH, CONV_K, CONV_WIDTH), CONV_K ** -0.5),
        "conv_b": nrm(ks[12], (DEPTH, CONV_WIDTH), 0.01),
        "conv_ln_g": 1.0 + nrm(ks[13], (DEPTH, CONV_WIDTH), 0.02),
        "conv_ln_b": nrm(ks[14], (DEPTH, CONV_WIDTH), 0.01),
        "w_pc": nrm(ks[15], (DEPTH, CONV_WIDTH, D_MODEL), CONV_WIDTH ** -0.5),
        "m_norm_g": 1.0 + nrm(ks[16], (DEPTH, M_WIDTH), 0.02),
        "w_pm": nrm(ks[17], (DEPTH, M_WIDTH, D_MODEL), M_WIDTH ** -0.5),
        "cq_g": 1.0 + nrm(ks[18], (DEPTH, Q_LORA), 0.02),
        "ckv_g": 1.0 + nrm(ks[19], (DEPTH, KV_LORA), 0.02),
        "qn_g": 1.0 + nrm(ks[20], (DEPTH, NOPE_DIM), 0.02),
        "qr_g": 1.0 + nrm(ks[21], (DEPTH, ROPE_DIM), 0.02),
        "kn_g": 1.0 + nrm(ks[22], (DEPTH, NOPE_DIM), 0.02),
        "kr_g": 1.0 + nrm(ks[23], (DEPTH, ROPE_DIM), 0.02),
        "w_uq": nrm(ks[24], (DEPTH, Q_LORA, A_HEADS * (NOPE_DIM + ROPE_DIM)), Q_LORA ** -0.5),
        "w_ukv": nrm(ks[25], (DEPTH, KV_LORA, A_HEADS * (NOPE_DIM + V_DIM)), KV_LORA ** -0.5),
        "w_pa": nrm(ks[26], (DEPTH, A_WIDTH, D_MODEL), A_WIDTH ** -0.5),
        "w_out": nrm(ks[27], (DEPTH, D_MODEL, D_MODEL), D_MODEL ** -0.5),
    }


def reference(x_prompt, x_sample, cache_kv_latent, cache_k_rope, state_conv, state_mlstm_C, state_mlstm_n,
              state_mlstm_m, ln_g, w_in, b_in, conv_w, conv_b, conv_ln_g, conv_ln_b, w_pc, m_norm_g, w_pm,
              cq_g, ckv_g, qn_g, qr_g, kn_g, kr_g, w_uq, w_ukv, w_pa, w_out):
    Bp, Lp, _ = x_prompt.shape
    Bs, Ls, _ = x_sample.shape
    past = cache_kv_latent.shape[2]
    pos_p = jnp.arange(Lp, dtype=jnp.int32)
    pos_s = past + jnp.arange(Ls, dtype=jnp.int32)
    xp, xs = x_prompt, x_sample
    cp_l, cs_l, Cp_l, np_l, mp_l, Cs_l, ns_l, ms_l, latp_l, krp_l, lats_l, krs_l = ([] for _ in range(12))
    for l in range(DEPTH):
        p = (ln_g[l], w_in[l], b_in[l], conv_w[l], conv_b[l], conv_ln_g[l], conv_ln_b[l], w_pc[l], m_norm_g[l],
             w_pm[l], cq_g[l], ckv_g[l], qn_g[l], qr_g[l], kn_g[l], kr_g[l], w_uq[l], w_ukv[l], w_pa[l], w_out[l])
        xp, cst, Cn, nn_, mn_, lat, kr = mixer_layer(
            xp, pos_p, jnp.zeros((Bp, CONV_STATE, CONV_WIDTH), xp.dtype),
            jnp.zeros((Bp, M_HEADS, M_DK, M_DV), jnp.float32), jnp.zeros((Bp, M_HEADS, M_DK), jnp.float32),
            jnp.zeros((Bp, M_HEADS), jnp.float32), None, None, *p)
        cp_l.append(cst); Cp_l.append(Cn); np_l.append(nn_); mp_l.append(mn_); latp_l.append(lat); krp_l.append(kr)
        xs, cst, Cn, nn_, mn_, lat, kr = mixer_layer(
            xs, pos_s, state_conv[l], state_mlstm_C[l].astype(jnp.float32), state_mlstm_n[l].astype(jnp.float32),
            state_mlstm_m[l].astype(jnp.float32), cache_kv_latent[l], cache_k_rope[l], *p)
        cs_l.append(cst); Cs_l.append(Cn); ns_l.append(nn_); ms_l.append(mn_); lats_l.append(lat); krs_l.append(kr)
    return (xp, xs,
            jnp.stack(cp_l), jnp.stack(cs_l),
            jnp.stack(Cp_l), jnp.stack(np_l), jnp.stack(mp_l),
            jnp.stack(Cs_l), jnp.stack(ns_l), jnp.stack(ms_l),
            jnp.stack(latp_l), jnp.stack(krp_l),
            jnp.stack(lats_l), jnp.stack(krs_l))
```

```python
import numpy as np
from contextlib import ExitStack
import concourse.bass as bass
import concourse.mybir as mybir
from concourse.bass_utils import run_bass_kernel_spmd

F32 = mybir.dt.float32
BF16 = mybir.dt.bfloat16
AF = mybir.ActivationFunctionType
ALU = mybir.AluOpType
AX = mybir.AxisListType

D = 4096
NIN = 23112
SEQ = 2048
TT = 256
NPT = SEQ // TT
PAST = 1024
EPS = 1e-6
ATTN_SCALE = 192.0 ** -0.5
O_CA, O_CB, O_CZ, O_MQ, O_MK, O_MV, O_MIF, O_MO, O_MZ = 0, 1024, 2048, 3072, 3584, 4096, 5120, 5128, 6152
O_ACQ, O_ACKV, O_AKR, O_AZ, O_G = 7176, 8200, 8712, 8776, 10824
NBCH = 182


def cid(col):
    if col < 5120:
        return col // 128
    if col == 5120:
        return 40
    if col < 8712:
        return 41 + (col - 5128) // 128
    if col == 8712:
        return 69
    return 70 + (col - 8776) // 128


ENGS = ("pe", "act", "dve", "pool", "sp")
SAME_ENGINE_SYNC = True
EPOCH = 30000


class Res:
    __slots__ = ("name", "last_w", "readers", "sem", "semcnt", "arena")

    def __init__(self, name, arena=False):
        self.name = name
        self.last_w = None
        self.readers = {}
        self.sem = None
        self.semcnt = 0
        self.arena = arena


_SEQ = [0]


class Op:
    __slots__ = ("eng", "fn", "deps", "dma", "flag", "sem", "val", "seq")

    def __init__(self, eng, fn, dma):
        _SEQ[0] += 1
        self.seq = _SEQ[0]
        self.eng = eng
        self.fn = fn
        self.deps = []
        self.dma = dma
        self.flag = False
        self.sem = None
        self.val = 0


class _Rec:
    def __getattr__(self, name):
        return lambda *a, **k: (name, a, k)


_REC = _Rec()


class Prog:
    def __init__(self, nc, stack):
        self.nc = nc
        self.stack = stack
        self.ops = {e: [] for e in ENGS}
        self.out_dmas = []
        self.nres = 0
        self.token = Res("arena_token")
        self.dsem = {}
        self.verbose = False

    def res(self, name=None, arena=False):
        self.nres += 1
        return Res(name or f"r{self.nres}", arena)

    def _newsem(self, name):
        return self.stack.enter_context(self.nc.semaphore(name))

    def op(self, eng, fn, r=(), w=(), dma=False, out=False, semres=None):
        o = Op(eng, fn(_REC) if fn is not None else None, dma)
        deps = o.deps
        r = list(r)
        w = list(w)
        if any(R.arena for R in r) or any(R.arena for R in w):
            r.append(self.token)
        if dma:
            R0 = semres if semres is not None else (w + r)[0]
            ent = self.dsem.get(R0.name)
            if ent is None:
                ent = [self._newsem("d" + R0.name), 0]
                self.dsem[R0.name] = ent
            ent[1] += 16
            o.sem = ent[0]
            o.val = ent[1]
            if out:
                self.out_dmas.append(o)
        o_sem_key = o.sem
        for R in r:
            if R.last_w is not None:
                deps.append((R.last_w, 0))
        for R in w:
            if R.last_w is not None:
                deps.append((R.last_w, 1))
            for rd in R.readers.values():
                deps.append((rd, 2))
        key = (eng, id(o_sem_key)) if dma else (eng, None)
        for R in r:
            R.readers[key] = o
        for R in w:
            R.last_w = o
            R.readers = {}
        self.ops[eng].append(o)
        return o

    def barrier(self):
        self.op("act", lambda e: e.nop(), w=[self.token])

    def _need(self, o, d, kind):
        if d.dma:
            return True
        if d.eng == o.eng and not o.dma:
            if d.eng == "pe":
                return False
            return kind == 0 and SAME_ENGINE_SYNC
        return True

    def emit(self):
        nc = self.nc
        for e in ENGS:
            for o in self.ops[e]:
                for (d, kind) in o.deps:
                    if not d.dma and self._need(o, d, kind):
                        d.flag = True
        fin = Op("sp", None, False)
        fin.deps = [(d, 0) for d in self.out_dmas]
        self.ops["sp"].append(fin)
        for e in ENGS:
            cnt = 0
            sems = {}
            for o in self.ops[e]:
                if o.dma or not o.flag:
                    continue
                ep = cnt // EPOCH
                if ep not in sems:
                    sems[ep] = self._newsem(f"e{e}{ep}")
                o.sem = sems[ep]
                o.val = cnt % EPOCH + 1
                cnt += 1

        if self.verbose:
            mx = {e: max([o.val for o in self.ops[e] if o.sem is not None] + [0]) for e in ENGS}
            print("ops per engine", {e: len(self.ops[e]) for e in ENGS}, "max sem val per engine", mx,
                  "dma sems", len(self.dsem), "max dma val", max([v[1] for v in self.dsem.values()] + [0]), flush=True)

        def run(engname):
            def body(eng):
                waited = {}
                for o in self.ops[engname]:
                    for (d, kind) in o.deps:
                        if not self._need(o, d, kind):
                            continue
                        k = id(d.sem)
                        if waited.get(k, 0) >= d.val:
                            continue
                        eng.wait_ge(d.sem, d.val)
                        waited[k] = d.val
                    if o.fn is None:
                        continue
                    name, a, k = o.fn
                    ins = getattr(eng, name)(*a, **k)
                    if o.dma:
                        ins.then_inc(o.sem, 16)
                    elif o.flag:
                        ins.then_inc(o.sem, 1)
            return body

        with nc.Block() as block:
            block.tensor(run("pe"))
            block.scalar(run("act"))
            block.vector(run("dve"))
            block.gpsimd(run("pool"))
            block.sync(run("sp"))


def build(cfg):
    NTILES_P = cfg.get("ntiles_p", NPT)
    DO_SAMPLE = cfg.get("sample", True)
    NLAYER = cfg.get("nlayer", 2)
    USE_WC = cfg.get("use_wc", True)
    nc = bass.Bass("TRN2", target_bir_lowering=False)

    def din(name, shape, dt=F32):
        return nc.dram_tensor(name, list(shape), dt, kind="ExternalInput").ap()

    def dout(name, shape):
        return nc.dram_tensor(name, list(shape), F32, kind="ExternalOutput").ap()

    xp = din("xp", [SEQ, D]); xs = din("xs", [128, D])
    latp_d = din("latc", [2, 4, PAST, 512]); krp_d = din("krc", [2, 4, PAST, 64])
    stconv = din("stconv", [2, 4, 30, 1024]); stC = din("stC", [2, 4, 4, 128, 256])
    stn = din("stn", [2, 4, 4, 128]); stm = din("stm", [2, 4, 4])
    ln_g = din("ln_g", [2, D]); w_in = din("w_in", [2, D, NIN]); b_in = din("b_in", [2, NIN])
    conv_w = din("conv_w", [2, 31, 1024]); conv_b = din("conv_b", [2, 1024])
    conv_ln_g = din("conv_ln_g", [2, 1024]); conv_ln_b = din("conv_ln_b", [2, 1024])
    w_pc = din("w_pc", [2, 1024, D]); m_norm_g = din("m_norm_g", [2, 1024]); w_pm = din("w_pm", [2, 1024, D])
    cq_g = din("cq_g", [2, 1024]); ckv_g = din("ckv_g", [2, 512]); qn_g = din("qn_g", [2, 128])
    qr_g = din("qr_g", [2, 64]); kn_g = din("kn_g", [2, 128]); kr_g = din("kr_g", [2, 64])
    w_uq = din("w_uq", [2, 1024, 3072]); w_ukv = din("w_ukv", [2, 512, D]); w_pa = din("w_pa", [2, 2048, D])
    w_out = din("w_out", [2, D, D])
    c_ident = din("c_ident", [128, 128]); c_rot = din("c_rot", [64, 64]); c_utri = din("c_utri", [64, 64])
    c_csp = din("c_csp", [64, 2, SEQ]); c_css = din("c_css", [64, 2, 128]); c_am = din("c_am", [128, 2, TT])
    c_eye4 = din("c_eye4", [4, 4])

    y_p = dout("y_p", [SEQ, D]); y_s = dout("y_s", [128, D])
    cs_p = dout("cs_p", [2, 30, 1024]); cs_s = dout("cs_s", [2, 4, 30, 1024])
    C_p = dout("C_p", [2, 4, 128, 256]); n_p = dout("n_p", [2, 4, 128]); m_p = dout("m_p", [2, 4])
    C_s = dout("C_s", [2, 4, 4, 128, 256]); n_s = dout("n_s", [2, 4, 4, 128]); m_s = dout("m_s", [2, 4, 4])
    lat_p = dout("lat_p", [2, SEQ, 512]); kr_p = dout("kr_p", [2, SEQ, 64])
    lat_s = dout("lat_s", [2, 128, 512]); kr_s = dout("kr_s", [2, 128, 64])
    KC = nc.dram_tensor("KC", [2, 16, 128, SEQ], BF16).ap()
    VC = nc.dram_tensor("VC", [2, 16, SEQ, 128], BF16).ap()
    KRC = nc.dram_tensor("KRC", [2, 64, SEQ], BF16).ap()

    st = ExitStack()
    P = Prog(nc, st)
    P.verbose = cfg.get("verbose", False)

    def sbt(name, shape, dt=F32):
        return st.enter_context(nc.sbuf_tensor(name, list(shape), dt))

    class TB:
        def __init__(self, t, res):
            self.t = t
            self.r = res

    def pers(name, shape, dt=F32):
        return TB(sbt(name, shape, dt), P.res(name))

    NSUB = TT // 128
    Xs = [pers(f"X{j}", [128, D]) for j in range(NSUB)]
    HT = pers("HT", [128, 32, TT], BF16)
    CVG = pers("CVG", [128, 8, TT], BF16)
    HMG = pers("HMG", [128, 8, TT], BF16)
    AOG = pers("AOG", [128, 16, TT], BF16)
    NW = 4
    WS = [pers(f"WS{i}", [128, 4096], BF16) for i in range(NW)]
    IDF = pers("IDF", [128, 128]); IDB = pers("IDB", [128, 128], BF16)
    ONF = pers("ONF", [128, 128]); ONB = pers("ONB", [128, 128], BF16)
    ROTF = pers("ROTF", [64, 64]); ROTB = pers("ROTB", [64, 64], BF16)
    UTF = pers("UTF", [64, 64]); UTB = pers("UTB", [64, 64], BF16)
    EYE4 = pers("EYE4", [4, 4])
    AMF = pers("AMF", [128, 2, TT]); AMB = pers("AMB", [128, 2, TT], BF16)
    CS = pers("CS", [64, 2, TT])
    LNG = pers("LNG", [128, 2, 32]); BIN = pers("BIN", [128, 2, NBCH])
    CW = pers("CW", [128, 2, 8, 31]); CB_ = pers("CB", [128, 2, 8]); CLG = pers("CLG", [128, 2, 8]); CLB = pers("CLB", [128, 2, 8])
    MNG = pers("MNG", [128, 2, 8]); CQG = pers("CQG", [128, 2, 8]); CKVG = pers("CKVG", [128, 2, 4])
    QNG = pers("QNG", [128, 2]); KNG = pers("KNG", [128, 2]); QRG = pers("QRG", [64, 2]); KRG = pers("KRG", [64, 2])
    CT = [pers(f"CT{l}", [128, 8, 30]) for l in range(2)]
    CNP = [pers(f"CNP{l}", [128, 4, 257]) for l in range(2)]
    MRP = [pers(f"MRP{l}", [4, 1]) for l in range(2)]
    ARENA_BYTES = 72 * 1024
    ARENA = sbt("ARENA", [128, ARENA_BYTES // 2], BF16)
    astate = dict(lo=0, hi=ARENA_BYTES, side=0, phase=0)
    alive = []
    ancient = Res("ancient")

    def _merge(dst, k, op):
        cur = dst.get(k)
        if cur is None or cur.seq < op.seq:
            dst[k] = op

    def _inherit(new, old):
        for k, op in old.readers.items():
            _merge(new.readers, k, op)
        lw = old.last_w
        if lw is not None:
            _merge(new.readers, (lw.eng, id(lw.sem) if lw.dma else None), lw)

    def areset():
        astate["phase"] += 1
        astate["side"] ^= 1
        astate["lo"] = 0
        astate["hi"] = ARENA_BYTES
        ph = astate["phase"]
        keep = []
        for ent in alive:
            if ent[3] < ph - 4:
                _inherit(ancient, ent[2])
            else:
                keep.append(ent)
        alive[:] = keep

    def aal(name, shape, dt=F32, parts=128):
        esz = 4 if dt == F32 else 2
        n = int(np.prod(shape))
        nb = (n * esz + 63) // 64 * 64
        if astate["side"] == 0:
            o = astate["lo"]
            astate["lo"] = o + nb
        else:
            astate["hi"] -= nb
            o = astate["hi"]
        assert astate["lo"] <= astate["hi"], (name, astate)
        ap = ARENA[0:parts, o // 2:(o + n * esz) // 2]
        if dt == F32:
            ap = ap.bitcast(F32)
        if len(shape) == 2:
            ap = ap.rearrange("p (a b) -> p a b", a=shape[0])
        elif len(shape) == 3:
            ap = ap.rearrange("p (a b c) -> p a b c", a=shape[0], b=shape[1])
        R = P.res(name)
        _inherit(R, ancient)
        for (s0, e0, R0, ph0) in alive:
            if s0 < o + nb and o < e0:
                _inherit(R, R0)
        alive.append((o, o + nb, R, astate["phase"]))
        return TB(ap, R)

    banks = [TB(st.enter_context(nc.psum_tensor(f"pb{i}", [128, 512], F32)), P.res(f"pb{i}")) for i in range(8)]
    bcnt = [0]

    def bank():
        b = banks[bcnt[0] % 6]
        bcnt[0] += 1
        return b

    def abank(i):
        return banks[6 + i]

    def bfv(b):
        return b.t.bitcast(BF16)

    wcnt = [0]
    NIMG = 720
    WPER = 180
    WCS = [nc.dram_tensor(f"WC{i}", [WPER, 128, 4096], BF16).ap() for i in range(NIMG // WPER)]
    wc_index = {}
    WST = [P.res(f"WSst{i}") for i in range(NW)]

    def wload(src, nk, ncols, key):
        i = wcnt[0] % NW
        s = WS[i]
        wcnt[0] += 1
        n = nk * ncols
        v = s.t[:, 0:n].rearrange("p (k c) -> p k c", k=nk)
        ent = wc_index.get(key)
        if ent is None:
            idx = len(wc_index)
            assert idx < NIMG
            R = P.res(f"wc{idx}")
            wc_index[key] = (idx, R)
            P.op("pool", lambda e: e.dma_start(out=v, in_=src.rearrange("(k p) c -> p k c", p=128)), w=[s.r], dma=True)
            if USE_WC:
                P.op("sp", lambda e: e.dma_start(out=WCS[idx // WPER][idx % WPER][:, 0:n], in_=s.t[:, 0:n]), r=[s.r], w=[R], dma=True, semres=WST[i])
        else:
            idx, R = ent
            if USE_WC:
                P.op("sp", lambda e: e.dma_start(out=s.t[:, 0:n], in_=WCS[idx // WPER][idx % WPER][:, 0:n]), r=[R], w=[s.r], dma=True)
            else:
                P.op("pool", lambda e: e.dma_start(out=v, in_=src.rearrange("(k p) c -> p k c", p=128)), w=[s.r], dma=True)
        return v, s.r

    def dma_in(dst, src, R, slow=False, eng="sp"):
        P.op(eng, lambda e: e.dma_start(out=dst, in_=src, allow_slow_non_contiguous=slow), w=[R], dma=True)

    def dma_out(dst, src, R, slow=False, final=True, extra_w=()):
        P.op("sp", lambda e: e.dma_start(out=dst, in_=src, allow_slow_non_contiguous=slow), r=[R], w=list(extra_w),
             dma=True, out=final, semres=R)

    A = lambda eng, fn, r=(), w=(): P.op(eng, fn, r=r, w=w)
    DBG = cfg.get("dbg", False)

    def dbg(name, ap, R, shape, dt=F32):
        if DBG:
            d = nc.dram_tensor("dbg_" + name, list(shape), dt, kind="ExternalOutput").ap()
            dma_out(d, ap, R, slow=True)

    def mm(out, lhsT, rhs, start, stop, r, w):
        P.op("pe", lambda e: e.matmul(out, lhsT, rhs, start=start, stop=stop), r=r, w=w)

    def tr(out, in_, ident, r, w):
        P.op("pe", lambda e: e.transpose(out, in_, ident), r=r, w=w)

    dma_in(IDF.t[:], c_ident, IDF.r); dma_in(ROTF.t[:], c_rot, ROTF.r); dma_in(UTF.t[:], c_utri, UTF.r)
    dma_in(EYE4.t[:], c_eye4, EYE4.r); dma_in(AMF.t[:], c_am, AMF.r)
    A("dve", lambda e: e.tensor_copy(out=IDB.t[:], in_=IDF.t[:]), r=[IDF.r], w=[IDB.r])
    A("dve", lambda e: e.tensor_copy(out=ROTB.t[:], in_=ROTF.t[:]), r=[ROTF.r], w=[ROTB.r])
    A("dve", lambda e: e.tensor_copy(out=UTB.t[:], in_=UTF.t[:]), r=[UTF.r], w=[UTB.r])
    A("dve", lambda e: e.tensor_copy(out=AMB.t[:], in_=AMF.t[:]), r=[AMF.r], w=[AMB.r])
    A("dve", lambda e: e.memset(ONF.t[:], 1.0), w=[ONF.r])
    A("dve", lambda e: e.memset(ONB.t[:], 1.0), w=[ONB.r])
    A("dve", lambda e: e.memset(BIN.t[:], 0.0), w=[BIN.r])
    for l in range(2):
        dma_in(LNG.t[:, l, :], ln_g[l].rearrange("(c p) -> p c", p=128), LNG.r, slow=True)
        dma_in(BIN.t[:, l, 0:40], b_in[l, 0:5120].rearrange("(c p) -> p c", p=128), BIN.r, slow=True)
        dma_in(BIN.t[0:8, l, 40:41], b_in[l, 5120:5128].rearrange("(c p) -> p c", p=8), BIN.r, slow=True)
        dma_in(BIN.t[:, l, 41:69], b_in[l, 5128:8712].rearrange("(c p) -> p c", p=128), BIN.r, slow=True)
        dma_in(BIN.t[0:64, l, 69:70], b_in[l, 8712:8776].rearrange("(c p) -> p c", p=64), BIN.r, slow=True)
        dma_in(BIN.t[:, l, 70:182], b_in[l, 8776:NIN].rearrange("(c p) -> p c", p=128), BIN.r, slow=True)
        for cc in range(8):
            dma_in(CW.t[:, l, cc, :], conv_w[l, :, cc * 128:(cc + 1) * 128].rearrange("k p -> p k"), CW.r, slow=True)
        for (dst, src, n) in ((CB_, conv_b, 8), (CLG, conv_ln_g, 8), (CLB, conv_ln_b, 8), (MNG, m_norm_g, 8),
                              (CQG, cq_g, 8), (CKVG, ckv_g, 4)):
            dma_in(dst.t[:, l, 0:n], src[l].rearrange("(c p) -> p c", p=128), dst.r, slow=True)
        for (dst, src, n) in ((QNG, qn_g, 128), (KNG, kn_g, 128), (QRG, qr_g, 64), (KRG, kr_g, 64)):
            dma_in(dst.t[0:n, l:l + 1], src[l].rearrange("(c p) -> p c", p=n), dst.r, slow=True)
        A("dve", lambda e, l=l: e.memset(CT[l].t[:], 0.0), w=[CT[l].r])
        A("dve", lambda e, l=l: e.memset(CNP[l].t[:], 0.0), w=[CNP[l].r])
        A("dve", lambda e, l=l: e.memset(MRP[l].t[:], 0.0), w=[MRP[l].r])

    def bias(l, col, n=128):
        return BIN.t[0:n, l, cid(col):cid(col) + 1]

    def rstd_from(dst, src_ap, scale, R_src, parts=128):
        A("dve", lambda e: e.tensor_scalar(out=dst.t, in0=src_ap, scalar1=scale, scalar2=EPS, op0=ALU.mult, op1=ALU.add),
          r=[R_src], w=[dst.r])
        A("act", lambda e: e.activation(out=dst.t, in_=dst.t, func=AF.Sqrt), r=[dst.r], w=[dst.r])
        A("dve", lambda e: e.reciprocal(out=dst.t, in_=dst.t), r=[dst.r], w=[dst.r])

    def process(tile, l):
        kind = tile["kind"]
        ntok = tile["ntok"]
        nsub = ntok // 128
        pos0 = tile["pos0"]
        ti = tile["idx"]
        last = tile["last"]
        L = 64 if kind == "p" else 32
        nch = 4
        TK = slice(0, ntok)

        def w_in_cols(c0, ncols):
            return wload(w_in[l][:, c0:c0 + ncols], 32, ncols, ("w_in", l, c0, ncols))

        def projW(c0, ncols):
            wv, wr = w_in_cols(c0, ncols)
            b = bank()
            for kc in range(32):
                mm(b.t[0:ncols, TK], wv[:, kc, :], HT.t[:, kc, TK], kc == 0, kc == 31, [wr, HT.r], [b.r])
            return b

        areset()
        XS = aal("XS", [D], BF16)
        SSQ = aal("SSQ", [4]); RSTD0 = aal("RSTD0", [4])
        for j in range(nsub):
            Xj = Xs[j]
            if l == 0:
                src = (xp[pos0 + j * 128: pos0 + (j + 1) * 128, :] if kind == "p" else xs[:, :])
                dma_in(Xj.t[:], src, Xj.r)
            A("act", lambda e, Xj=Xj, j=j: e.activation(out=XS.t, in_=Xj.t[:], func=AF.Square, accum_out=SSQ.t[:, j:j + 1]),
              r=[Xj.r], w=[XS.r, SSQ.r])
            A("dve", lambda e, j=j: e.tensor_scalar(out=RSTD0.t[:, j:j + 1], in0=SSQ.t[:, j:j + 1], scalar1=1.0 / D, scalar2=EPS,
                                                    op0=ALU.mult, op1=ALU.add), r=[SSQ.r], w=[RSTD0.r])
            A("act", lambda e, j=j: e.activation(out=RSTD0.t[:, j:j + 1], in_=RSTD0.t[:, j:j + 1], func=AF.Sqrt), r=[RSTD0.r], w=[RSTD0.r])
            A("dve", lambda e, j=j: e.reciprocal(out=RSTD0.t[:, j:j + 1], in_=RSTD0.t[:, j:j + 1]), r=[RSTD0.r], w=[RSTD0.r])
            A("act", lambda e, Xj=Xj, j=j: e.activation(out=XS.t, in_=Xj.t[:], func=AF.Copy, scale=RSTD0.t[:, j:j + 1]),
              r=[Xj.r, RSTD0.r], w=[XS.r])
            for g in range(8):
                b = bank()
                for q in range(4):
                    c = 4 * g + q
                    tr(bfv(b)[:, q * 128:(q + 1) * 128], XS.t[:, c * 128:(c + 1) * 128], IDB.t[:], [XS.r, IDB.r], [b.r])
                A("dve", lambda e, b=b, g=g, j=j: e.tensor_tensor(
                    out=HT.t[:, 4 * g:4 * g + 4, j * 128:(j + 1) * 128],
                    in0=bfv(b)[:, 0:512].rearrange("p (q t) -> p q t", q=4),
                    in1=LNG.t[:, l, 4 * g:4 * g + 4].unsqueeze(2).broadcast_to([128, 4, 128]), op=ALU.mult),
                  r=[b.r, LNG.r], w=[HT.r])

        areset()
        UF = aal("UF", [8, 30 + TT])
        ACC = aal("ACC", [8, TT])
        SIG = aal("SIG", [TT]); SQC = aal("SQC", [TT]); MEAN = aal("MEAN", [TT]); RSTDC = aal("RSTDC", [TT]); SZC = aal("SZC", [TT])
        CSO = aal("CSO", [1024], F32, parts=32)
        STG = aal("STG", [1024], F32, parts=32)
        if kind == "p":
            def ufnew(cc): return UF.t[:, cc, 30:30 + TT]
            def ufwin(cc, k): return UF.t[:, cc, k:k + TT]
            def accv(cc): return ACC.t[:, cc, 0:TT]
            def psv(ap): return ap
            def v2(t): return t.t[:, 0:TT]
            A("dve", lambda e: e.tensor_copy(out=UF.t[:, :, 0:30], in_=CT[l].t[:]), r=[CT[l].r], w=[UF.r])
        else:
            UFs = UF.t[:, :, 0:248].rearrange("p c (s t) -> p c s t", s=4)
            def ufnew(cc): return UFs[:, cc, :, 30:62]
            def ufwin(cc, k): return UFs[:, cc, :, k:k + 32]
            def accv(cc): return ACC.t[:, cc, 0:128].rearrange("p (s t) -> p s t", s=4)
            def psv(ap): return ap.rearrange("p (s t) -> p s t", s=4)
            def v2(t): return t.t[:, 0:128].rearrange("p (s t) -> p s t", s=4)
            for s in range(4):
                dma_in(STG.t[0:30, :], stconv[l, s], STG.r)
                for g in range(2):
                    b = bank()
                    for q in range(4):
                        cc = 4 * g + q
                        tr(b.t[:, q * 32:q * 32 + 30], STG.t[0:30, cc * 128:(cc + 1) * 128], IDF.t[0:30, 0:30], [STG.r, IDF.r], [b.r])
                    A("dve", lambda e, b=b, g=g, s=s: e.tensor_copy(
                        out=UFs[:, 4 * g:4 * g + 4, s, 0:30],
                        in_=b.t[:, 0:128].rearrange("p (q t) -> p q t", q=4)[:, :, 0:30]), r=[b.r], w=[UF.r])
        bS1 = abank(0); bS2 = abank(1)
        for cc in range(8):
            ba = projW(O_CA + cc * 128, 128)
            bb = projW(O_CB + cc * 128, 128)
            A("act", lambda e, bb=bb, cc=cc: e.activation(out=SIG.t[:, TK], in_=bb.t[:, TK], func=AF.Sigmoid,
                                                          bias=bias(l, O_CB + cc * 128), scale=1.0), r=[bb.r, BIN.r], w=[SIG.r])
            A("dve", lambda e, ba=ba, cc=cc: e.scalar_tensor_tensor(out=ufnew(cc), in0=psv(ba.t[:, TK]), scalar=bias(l, O_CA + cc * 128),
                                                                    in1=psv(SIG.t[:, TK]), op0=ALU.add, op1=ALU.mult),
              r=[ba.r, SIG.r, BIN.r], w=[UF.r])
            A("dve", lambda e, cc=cc: e.tensor_scalar(out=accv(cc), in0=ufwin(cc, 0), scalar1=CW.t[:, l, cc, 0:1],
                                                      scalar2=CB_.t[:, l, cc:cc + 1], op0=ALU.mult, op1=ALU.add),
              r=[UF.r, CW.r, CB_.r], w=[ACC.r])
            for k in range(1, 31):
                A("dve", lambda e, cc=cc, k=k: e.scalar_tensor_tensor(out=accv(cc), in0=ufwin(cc, k), scalar=CW.t[:, l, cc, k:k + 1],
                                                                      in1=accv(cc), op0=ALU.mult, op1=ALU.add),
                  r=[UF.r, CW.r, ACC.r], w=[ACC.r])
            A("act", lambda e, cc=cc: e.activation(out=SQC.t[:, TK], in_=ACC.t[:, cc, TK], func=AF.Square), r=[ACC.r], w=[SQC.r])
            mm(bS1.t[:, TK], ONF.t[:], ACC.t[:, cc, TK], cc == 0, cc == 7, [ONF.r, ACC.r], [bS1.r])
            mm(bS2.t[:, TK], ONF.t[:], SQC.t[:, TK], cc == 0, cc == 7, [ONF.r, SQC.r], [bS2.r])
        A("dve", lambda e: e.tensor_scalar(out=MEAN.t[:, TK], in0=bS1.t[:, TK], scalar1=1.0 / 1024, scalar2=None, op0=ALU.mult),
          r=[bS1.r], w=[MEAN.r])
        A("dve", lambda e: e.tensor_tensor(out=SQC.t[:, TK], in0=MEAN.t[:, TK], in1=MEAN.t[:, TK], op=ALU.mult), r=[MEAN.r], w=[SQC.r])
        A("dve", lambda e: e.scalar_tensor_tensor(out=RSTDC.t[:, TK], in0=bS2.t[:, TK], scalar=1.0 / 1024, in1=SQC.t[:, TK],
                                                  op0=ALU.mult, op1=ALU.subtract), r=[bS2.r, SQC.r], w=[RSTDC.r])
        A("dve", lambda e: e.tensor_scalar(out=RSTDC.t[:, TK], in0=RSTDC.t[:, TK], scalar1=EPS, scalar2=None, op0=ALU.add),
          r=[RSTDC.r], w=[RSTDC.r])
        A("act", lambda e: e.activation(out=RSTDC.t[:, TK], in_=RSTDC.t[:, TK], func=AF.Sqrt), r=[RSTDC.r], w=[RSTDC.r])
        A("dve", lambda e: e.reciprocal(out=RSTDC.t[:, TK], in_=RSTDC.t[:, TK]), r=[RSTDC.r], w=[RSTDC.r])
        for cc in range(8):
            bz = projW(O_CZ + cc * 128, 128)
            A("act", lambda e, bz=bz, cc=cc: e.activation(out=SZC.t[:, TK], in_=bz.t[:, TK], func=AF.Silu,
                                                          bias=bias(l, O_CZ + cc * 128), scale=1.0), r=[bz.r, BIN.r], w=[SZC.r])
            A("dve", lambda e, cc=cc: e.tensor_tensor(out=ACC.t[:, cc, TK], in0=ACC.t[:, cc, TK], in1=MEAN.t[:, TK], op=ALU.subtract),
              r=[ACC.r, MEAN.r], w=[ACC.r])
            A("dve", lambda e, cc=cc: e.tensor_tensor(out=ACC.t[:, cc, TK], in0=ACC.t[:, cc, TK], in1=RSTDC.t[:, TK], op=ALU.mult),
              r=[ACC.r, RSTDC.r], w=[ACC.r])
            A("act", lambda e, cc=cc: e.activation(out=ACC.t[:, cc, TK], in_=ACC.t[:, cc, TK], func=AF.Silu,
                                                   bias=CLB.t[:, l, cc:cc + 1], scale=CLG.t[:, l, cc:cc + 1]),
              r=[ACC.r, CLB.r, CLG.r], w=[ACC.r])
            A("dve", lambda e, cc=cc: e.tensor_tensor(out=CVG.t[:, cc, TK], in0=ACC.t[:, cc, TK], in1=SZC.t[:, TK], op=ALU.mult),
              r=[ACC.r, SZC.r], w=[CVG.r])
        if kind == "p":
            A("dve", lambda e: e.tensor_copy(out=CT[l].t[:], in_=UF.t[:, :, TT:TT + 30]), r=[UF.r], w=[CT[l].r])
            if last:
                for g in range(2):
                    b = bank()
                    for q in range(4):
                        cc = 4 * g + q
                        tr(b.t[0:30, q * 128:(q + 1) * 128], UF.t[:, cc, TT:TT + 30], IDF.t[:], [UF.r, IDF.r], [b.r])
                    A("act", lambda e, b=b, g=g: e.copy(out=CSO.t[0:30, g * 512:(g + 1) * 512], in_=b.t[0:30, :]), r=[b.r], w=[CSO.r])
                dma_out(cs_p[l], CSO.t[0:30, :], CSO.r)
        else:
            for s in range(4):
                for g in range(2):
                    b = bank()
                    for q in range(4):
                        cc = 4 * g + q
                        tr(b.t[0:30, q * 128:(q + 1) * 128], UFs[:, cc, s, 32:62], IDF.t[:], [UF.r, IDF.r], [b.r])
                    A("act", lambda e, b=b, g=g: e.copy(out=CSO.t[0:30, g * 512:(g + 1) * 512], in_=b.t[0:30, :]), r=[b.r], w=[CSO.r])
                dma_out(cs_s[l, s], CSO.t[0:30, :], CSO.r)

        areset()
        QT = aal("QT", [4, TT], BF16); KT = aal("KT", [4, TT], BF16); VT = aal("VT", [8, TT], BF16)
        OG = aal("OG", [8, TT], BF16)
        T1 = aal("T1", [TT]); T2 = aal("T2", [TT])
        GT = aal("GT", [TT], F32, parts=8)
        GTOK = aal("GTOK", [4, 8], F32, parts=64)
        SP_ = aal("SPt", [4, 4], F32, parts=64)
        NBT = aal("NBT", [4, 4], F32, parts=64)
        CTOK = aal("CTOK", [4, 4], F32, parts=64)
        WSK = aal("WSK", [4, 4], F32, parts=64); ETOK = aal("ETOK", [4, 4], F32, parts=64)
        CTR = aal("CTR", [TT], F32, parts=4)
        CMX = aal("CMX", [4], F32, parts=4); BL = aal("BL", [4], F32, parts=4)
        MPV = aal("MPV", [5], F32, parts=4); MCT = aal("MCT", [4], F32, parts=4); DA = aal("DA", [4], F32, parts=4)
        MNEW = aal("MNEW", [4], F32, parts=4)
        ZZ = aal("ZZ", [4, 4], F32, parts=4)
        DEC = aal("DEC", [16], F32)
        VS = aal("VS", [4, 257], BF16, parts=64); KTOK = aal("KTOK", [512], BF16, parts=64)
        QKM = aal("QKM", [4, 64], BF16, parts=64); HN = aal("HN", [4, 256], BF16, parts=64)
        CBF = aal("CBF", [4, 257], BF16)
        AD = aal("AD", [4], F32, parts=64); RD = aal("RD", [4], F32, parts=64); SS = aal("SS", [4], F32, parts=64)
        SC = aal("SC", [4], F32, parts=64); JK = aal("JK", [256], BF16, parts=64)
        HTMP = aal("HTMP", [8, 64], F32)
        CNS = aal("CNS", [4, 257], F32)
        for h in range(4):
            b = projW(O_MQ + h * 128, 128)
            A("act", lambda e, b=b, h=h: e.activation(out=QT.t[:, h, TK], in_=b.t[:, TK], func=AF.Identity,
                                                      bias=bias(l, O_MQ + h * 128), scale=1.0), r=[b.r, BIN.r], w=[QT.r])
            b = projW(O_MK + h * 128, 128)
            A("dve", lambda e, b=b, h=h: e.tensor_scalar(out=KT.t[:, h, TK], in0=b.t[:, TK], scalar1=bias(l, O_MK + h * 128),
                                                         scalar2=128.0 ** -0.5, op0=ALU.add, op1=ALU.mult), r=[b.r, BIN.r], w=[KT.r])
        for c in range(8):
            b = projW(O_MV + c * 128, 128)
            A("act", lambda e, b=b, c=c: e.activation(out=VT.t[:, c, TK], in_=b.t[:, TK], func=AF.Identity,
                                                      bias=bias(l, O_MV + c * 128), scale=1.0), r=[b.r, BIN.r], w=[VT.r])
            bo = projW(O_MO + c * 128, 128)
            bz = projW(O_MZ + c * 128, 128)
            A("act", lambda e, bo=bo, c=c: e.activation(out=T1.t[:, TK], in_=bo.t[:, TK], func=AF.Sigmoid,
                                                        bias=bias(l, O_MO + c * 128), scale=1.0), r=[bo.r, BIN.r], w=[T1.r])
            A("act", lambda e, bz=bz, c=c: e.activation(out=T2.t[:, TK], in_=bz.t[:, TK], func=AF.Silu,
                                                        bias=bias(l, O_MZ + c * 128), scale=1.0), r=[bz.r, BIN.r], w=[T2.r])
            A("dve", lambda e, c=c: e.tensor_tensor(out=OG.t[:, c, TK], in0=T1.t[:, TK], in1=T2.t[:, TK], op=ALU.mult),
              r=[T1.r, T2.r], w=[OG.r])
        b = projW(O_MIF, 8)
        A("act", lambda e, b=b: e.activation(out=GT.t[0:8, TK], in_=b.t[0:8, TK], func=AF.Identity, bias=bias(l, O_MIF, 8), scale=1.0),
          r=[b.r, BIN.r], w=[GT.r])
        b = bank()
        for j in range(nch):
            tr(b.t[0:L, j * 8:(j + 1) * 8], GT.t[0:8, j * L:(j + 1) * L], IDF.t[0:8, 0:8], [GT.r, IDF.r], [b.r])
        A("dve", lambda e, b=b: e.tensor_copy(out=GTOK.t[0:L], in_=b.t[0:L, 0:32].rearrange("p (j g) -> p j g", j=4)), r=[b.r], w=[GTOK.r])
        A("act", lambda e: e.activation(out=SP_.t[0:L], in_=GTOK.t[0:L, :, 4:8], func=AF.Exp, scale=-1.0), r=[GTOK.r], w=[SP_.r])
        A("act", lambda e: e.activation(out=SP_.t[0:L], in_=SP_.t[0:L], func=AF.Ln, bias=1.0, scale=1.0), r=[SP_.r], w=[SP_.r])
        b1 = bank()
        mm(b1.t[0:L, 0:16], UTF.t[0:L, 0:L], SP_.t[0:L].rearrange("p j h -> p (j h)"), True, True, [UTF.r, SP_.r], [b1.r])
        A("dve", lambda e: e.tensor_copy(out=NBT.t[0:L].rearrange("p j h -> p (j h)"), in_=b1.t[0:L, 0:16]), r=[b1.r], w=[NBT.r])
        A("dve", lambda e: e.tensor_tensor(out=CTOK.t[0:L], in0=GTOK.t[0:L, :, 0:4], in1=NBT.t[0:L], op=ALU.add),
          r=[GTOK.r, NBT.r], w=[CTOK.r])
        b2 = bank()
        for j in range(nch):
            mm(b2.t[0:4, j * L:(j + 1) * L], SP_.t[0:L, j, :], UTF.t[0:L, 0:L], True, True, [SP_.r, UTF.r], [b2.r])
        A("dve", lambda e: e.tensor_tensor(out=CTR.t[0:4, TK], in0=GT.t[0:4, TK], in1=b2.t[0:4, TK], op=ALU.add), r=[GT.r, b2.r], w=[CTR.r])
        A("dve", lambda e: e.tensor_reduce(out=CMX.t[0:4, :], in_=CTR.t[0:4, TK].rearrange("p (j t) -> p j t", j=4), axis=AX.X, op=ALU.max),
          r=[CTR.r], w=[CMX.r])
        A("dve", lambda e: e.tensor_scalar(out=BL.t[0:4, :], in0=b2.t[0:4, TK].rearrange("p (j t) -> p j t", j=4)[:, :, L - 1],
                                           scalar1=-1.0, scalar2=None, op0=ALU.mult), r=[b2.r], w=[BL.r])
        if kind == "p":
            A("dve", lambda e: e.tensor_copy(out=MPV.t[0:4, 0:1], in_=MRP[l].t[:]), r=[MRP[l].r], w=[MPV.r])
            for j in range(nch):
                A("dve", lambda e, j=j: e.tensor_tensor(out=MCT.t[0:4, j:j + 1], in0=MPV.t[0:4, j:j + 1], in1=CMX.t[0:4, j:j + 1], op=ALU.max),
                  r=[MPV.r, CMX.r], w=[MCT.r])
                A("dve", lambda e, j=j: e.tensor_tensor(out=MPV.t[0:4, j + 1:j + 2], in0=BL.t[0:4, j:j + 1], in1=MCT.t[0:4, j:j + 1], op=ALU.add),
                  r=[BL.r, MCT.r], w=[MPV.r])
            A("dve", lambda e: e.tensor_copy(out=MRP[l].t[:], in_=MPV.t[0:4, 4:5]), r=[MPV.r], w=[MRP[l].r])
            if last:
                dma_out(m_p[l].rearrange("(h o) -> h o", o=1), MRP[l].t[:], MRP[l].r, slow=True)
        else:
            dma_in(MPV.t[0:4, 0:4], stm[l].rearrange("s h -> h s"), MPV.r, slow=True)
            A("dve", lambda e: e.tensor_tensor(out=MCT.t[0:4, :], in0=MPV.t[0:4, 0:4], in1=CMX.t[0:4, :], op=ALU.max),
              r=[MPV.r, CMX.r], w=[MCT.r])
            A("dve", lambda e: e.tensor_tensor(out=MNEW.t[0:4, :], in0=BL.t[0:4, :], in1=MCT.t[0:4, :], op=ALU.add),
              r=[BL.r, MCT.r], w=[MNEW.r])
            dma_out(m_s[l].rearrange("s h -> h s"), MNEW.t[0:4, :], MNEW.r, slow=True)
        A("dve", lambda e: e.tensor_tensor(out=DA.t[0:4, :], in0=MPV.t[0:4, 0:4], in1=MCT.t[0:4, :], op=ALU.subtract),
          r=[MPV.r, MCT.r], w=[DA.r])
        A("act", lambda e: e.activation(out=DA.t[0:4, :], in_=DA.t[0:4, :], func=AF.Exp), r=[DA.r], w=[DA.r])

        def bcast_rows(src_tb, nparts):
            A("dve", lambda e: e.tensor_tensor(out=ZZ.t[0:4], in0=src_tb.t[0:4, :].unsqueeze(2).broadcast_to([4, 4, 4]),
                                               in1=EYE4.t[0:4, :].unsqueeze(1).broadcast_to([4, 4, 4]), op=ALU.mult),
              r=[src_tb.r, EYE4.r], w=[ZZ.r])
            bb = bank()
            mm(bb.t[0:nparts, 0:16], ONF.t[0:4, 0:nparts], ZZ.t[0:4].rearrange("p j h -> p (j h)"), True, True, [ONF.r, ZZ.r], [bb.r])
            return bb

        if kind == "s" and l == 0:
            dbg("GT", GT.t[0:8, TK], GT.r, [8, 128]); dbg("GTOK", GTOK.t[0:L], GTOK.r, [32, 4, 8]); dbg("SP", SP_.t[0:L], SP_.r, [32, 4, 4])
            dbg("NBT", NBT.t[0:L], NBT.r, [32, 4, 4]); dbg("CTR", CTR.t[0:4, TK], CTR.r, [4, 128]); dbg("CMX", CMX.t[0:4, :], CMX.r, [4, 4])
            dbg("BL", BL.t[0:4, :], BL.r, [4, 4]); dbg("MCT", MCT.t[0:4, :], MCT.r, [4, 4]); dbg("MPV", MPV.t[0:4, 0:4], MPV.r, [4, 4])
        bm = bcast_rows(MCT, 64)
        A("dve", lambda e: e.tensor_tensor(out=WSK.t[0:L].rearrange("p j h -> p (j h)"), in0=CTOK.t[0:L].rearrange("p j h -> p (j h)"),
                                           in1=bm.t[0:L, 0:16], op=ALU.subtract), r=[CTOK.r, bm.r], w=[WSK.r])
        A("act", lambda e: e.activation(out=WSK.t[0:L], in_=WSK.t[0:L], func=AF.Exp), r=[WSK.r], w=[WSK.r])
        A("dve", lambda e: e.tensor_tensor(out=ETOK.t[0:L].rearrange("p j h -> p (j h)"), in0=NBT.t[0:L].rearrange("p j h -> p (j h)"),
                                           in1=bm.t[0:L, 0:16], op=ALU.subtract), r=[NBT.r, bm.r], w=[ETOK.r])
        A("act", lambda e: e.activation(out=ETOK.t[0:L], in_=ETOK.t[0:L], func=AF.Exp), r=[ETOK.r], w=[ETOK.r])
        bd = bcast_rows(DA, 128)
        A("dve", lambda e: e.tensor_copy(out=DEC.t[:, :], in_=bd.t[:, 0:16]), r=[bd.r], w=[DEC.r])

        for j in range(nch):
            cs_ = slice(j * L, (j + 1) * L)
            if kind == "p":
                S = CNP[l]
            else:
                S = CNS
                dma_in(S.t[:, :, 0:256], stC[l, j].rearrange("h k v -> k h v"), S.r)
                dma_in(S.t[:, :, 256], stn[l, j].rearrange("h k -> k h"), S.r, slow=True)
            A("dve", lambda e, S=S, j=j: e.tensor_tensor(out=S.t[:], in0=S.t[:],
                                                         in1=DEC.t[:, j * 4:(j + 1) * 4].unsqueeze(2).broadcast_to([128, 4, 257]), op=ALU.mult),
              r=[S.r, DEC.r], w=[S.r])
            A("act", lambda e, S=S: e.copy(out=CBF.t[:], in_=S.t[:]), r=[S.r], w=[CBF.r])
            bq = bank()
            for h in range(4):
                mm(bq.t[0:L, h * L:(h + 1) * L], KT.t[:, h, cs_], QT.t[:, h, cs_], True, True, [KT.r, QT.r], [bq.r])
            A("dve", lambda e, bq=bq: e.tensor_tensor(out=QKM.t[0:L, :, 0:L], in0=bq.t[0:L, 0:4 * L].rearrange("p (h t) -> p h t", h=4),
                                                      in1=UTF.t[0:L, 0:L].unsqueeze(1).broadcast_to([L, 4, L]), op=ALU.mult),
              r=[bq.r, UTF.r], w=[QKM.r])
            bv = bank()
            for c in range(8):
                tr(bfv(bv)[0:L, c * 128:(c + 1) * 128], VT.t[:, c, cs_], IDB.t[:], [VT.r, IDB.r], [bv.r])
            A("dve", lambda e, bv=bv, j=j: e.tensor_tensor(out=VS.t[0:L, :, 0:256], in0=bfv(bv)[0:L, 0:1024].rearrange("p (h v) -> p h v", h=4),
                                                           in1=WSK.t[0:L, j, :].unsqueeze(2).broadcast_to([L, 4, 256]), op=ALU.mult),
              r=[bv.r, WSK.r], w=[VS.r])
            A("dve", lambda e, j=j: e.tensor_copy(out=VS.t[0:L, :, 256], in_=WSK.t[0:L, j, :]), r=[WSK.r], w=[VS.r])
            bk = bank()
            for h in range(4):
                tr(bfv(bk)[0:L, h * 128:(h + 1) * 128], KT.t[:, h, cs_], IDB.t[:], [KT.r, IDB.r], [bk.r])
            A("act", lambda e, bk=bk: e.copy(out=KTOK.t[0:L, :], in_=bfv(bk)[0:L, 0:512]), r=[bk.r], w=[KTOK.r])
            bn = [bank(), bank()]
            bden = bank()
            for h in range(4):
                o_ = bn[h // 2].t[0:L, (h % 2) * 256:(h % 2) * 256 + 256]
                mm(o_, QKM.t[0:L, h, 0:L], VS.t[0:L, h, 0:256], True, False, [QKM.r, VS.r], [bn[h // 2].r])
                mm(o_, QT.t[:, h, cs_], CBF.t[:, h, 0:256], False, True, [QT.r, CBF.r], [bn[h // 2].r])
                mm(bden.t[0:L, h:h + 1], QKM.t[0:L, h, 0:L], VS.t[0:L, h, 256:257], True, False, [QKM.r, VS.r], [bden.r])
                mm(bden.t[0:L, h:h + 1], QT.t[:, h, cs_], CBF.t[:, h, 256:257], False, True, [QT.r, CBF.r], [bden.r])
            A("act", lambda e, bden=bden: e.activation(out=AD.t[0:L, :], in_=bden.t[0:L, 0:4], func=AF.Abs), r=[bden.r], w=[AD.r])
            A("dve", lambda e, j=j: e.tensor_tensor(out=AD.t[0:L, :], in0=AD.t[0:L, :], in1=ETOK.t[0:L, j, :], op=ALU.max),
              r=[AD.r, ETOK.r], w=[AD.r])
            A("dve", lambda e: e.reciprocal(out=RD.t[0:L, :], in_=AD.t[0:L, :]), r=[AD.r], w=[RD.r])
            for h in range(4):
                A("act", lambda e, h=h, bn=bn: e.activation(out=JK.t[0:L, :], in_=bn[h // 2].t[0:L, (h % 2) * 256:(h % 2) * 256 + 256],
                                                            func=AF.Square, scale=RD.t[0:L, h:h + 1], accum_out=SS.t[0:L, h:h + 1]),
                  r=[bn[h // 2].r, RD.r], w=[JK.r, SS.r])
            A("dve", lambda e: e.tensor_scalar(out=SS.t[0:L, :], in0=SS.t[0:L, :], scalar1=1.0 / 256, scalar2=EPS, op0=ALU.mult, op1=ALU.add),
              r=[SS.r], w=[SS.r])
            A("act", lambda e: e.activation(out=SS.t[0:L, :], in_=SS.t[0:L, :], func=AF.Sqrt), r=[SS.r], w=[SS.r])
            A("dve", lambda e: e.reciprocal(out=SS.t[0:L, :], in_=SS.t[0:L, :]), r=[SS.r], w=[SS.r])
            A("dve", lambda e: e.tensor_tensor(out=SC.t[0:L, :], in0=SS.t[0:L, :], in1=RD.t[0:L, :], op=ALU.mult), r=[SS.r, RD.r], w=[SC.r])
            for g in range(2):
                A("dve", lambda e, g=g, bn=bn: e.tensor_tensor(out=HN.t[0:L, 2 * g:2 * g + 2, :],
                                                               in0=bn[g].t[0:L, 0:512].rearrange("p (h v) -> p h v", h=2),
                                                               in1=SC.t[0:L, 2 * g:2 * g + 2].unsqueeze(2).broadcast_to([L, 2, 256]), op=ALU.mult),
                  r=[bn[g].r, SC.r], w=[HN.r])
            bt_ = bank()
            HNf = HN.t[0:L].rearrange("p h v -> p (h v)")
            for c in range(8):
                tr(bfv(bt_)[:, c * L:(c + 1) * L], HNf[:, c * 128:(c + 1) * 128], IDB.t[0:L, 0:L], [HN.r, IDB.r], [bt_.r])
            A("dve", lambda e, bt_=bt_: e.tensor_tensor(out=HTMP.t[:, :, 0:L], in0=bfv(bt_)[:, 0:8 * L].rearrange("p (c t) -> p c t", c=8),
                                                        in1=MNG.t[:, l, :].unsqueeze(2).broadcast_to([128, 8, L]), op=ALU.mult),
              r=[bt_.r, MNG.r], w=[HTMP.r])
            A("dve", lambda e, cs_=cs_: e.tensor_tensor(out=HMG.t[:, :, cs_], in0=HTMP.t[:, :, 0:L], in1=OG.t[:, :, cs_], op=ALU.mult),
              r=[HTMP.r, OG.r], w=[HMG.r])
            bs = [bank(), bank()]
            bsn = bank()
            for h in range(4):
                mm(bs[h // 2].t[:, (h % 2) * 256:(h % 2) * 256 + 256], KTOK.t[0:L, h * 128:(h + 1) * 128], VS.t[0:L, h, 0:256], True, True,
                   [KTOK.r, VS.r], [bs[h // 2].r])
                mm(bsn.t[:, h:h + 1], KTOK.t[0:L, h * 128:(h + 1) * 128], VS.t[0:L, h, 256:257], True, True, [KTOK.r, VS.r], [bsn.r])
            for g in range(2):
                A("dve", lambda e, g=g, bs=bs, S=S: e.tensor_tensor(out=S.t[:, 2 * g:2 * g + 2, 0:256], in0=S.t[:, 2 * g:2 * g + 2, 0:256],
                                                                    in1=bs[g].t[:, 0:512].rearrange("p (h v) -> p h v", h=2), op=ALU.add),
                  r=[S.r, bs[g].r], w=[S.r])
            A("dve", lambda e, bsn=bsn, S=S: e.tensor_tensor(out=S.t[:, :, 256], in0=S.t[:, :, 256], in1=bsn.t[:, 0:4], op=ALU.add),
              r=[S.r, bsn.r], w=[S.r])
            if kind == "s":
                dma_out(C_s[l, j].rearrange("h k v -> k h v"), S.t[:, :, 0:256], S.r)
                dma_out(n_s[l, j].rearrange("h k -> k h"), S.t[:, :, 256], S.r, slow=True)
        if kind == "p" and last:
            dma_out(C_p[l].rearrange("h k v -> k h v"), CNP[l].t[:, :, 0:256], CNP[l].r)
            dma_out(n_p[l].rearrange("h k -> k h"), CNP[l].t[:, :, 256], CNP[l].r, slow=True)

        areset()
        ACQ = aal("ACQ", [8, TT], BF16)
        CKV32 = aal("CKV32", [4, TT]); CKVB = aal("CKVB", [4, TT], BF16)
        KR32 = aal("KR32", [TT], F32, parts=64); KRB = aal("KRB", [TT], BF16, parts=64)
        SQ = aal("SQ", [512], BF16); SQF = aal("SQF", [TT], F32, parts=64)
        R1 = aal("R1", [512]); R2 = aal("R2", [TT], F32, parts=64)
        TA = aal("TA", [TT], F32, parts=64); TBb = aal("TBb", [TT], F32, parts=64)
        LATO = aal("LATO", [512]); KRO = aal("KRO", [64])
        QNB = aal("QNB", [TT], BF16); QR32 = aal("QR32", [TT], F32, parts=64); QRNB = aal("QRNB", [TT], BF16, parts=64)
        QRB = aal("QRB", [TT], BF16, parts=64)
        KNB = aal("KNB", [TT], BF16); VB = aal("VB", [4, 128], BF16)
        SZ = aal("SZ", [TT]); PTs = [aal(f"PT{i}", [TT], BF16) for i in range(3)]
        RDEN = aal("RDEN", [TT]); TO = aal("TO", [TT])
        KP = aal("KP", [SEQ], BF16); VP = aal("VP", [16, 128], BF16); KRP = aal("KRP", [SEQ], BF16, parts=64)
        if kind == "s":
            LPB = aal("LPB", [8, 512], BF16); LATT = aal("LATT", [4, PAST], BF16); KRPB = aal("KRPB", [8, 64], BF16)
        if kind == "p":
            dma_in(CS.t[:, :, :], c_csp[:, :, pos0:pos0 + TT], CS.r)
        else:
            dma_in(CS.t[:, :, 0:128], c_css[:, :, :], CS.r)
        COS = CS.t[:, 0, TK]; SIN = CS.t[:, 1, TK]

        def rms_bcast(src_bank, nparts, n, gain_ap, dst, dst_r, ncols=TK, fp32sq=False):
            sq = SQF if fp32sq else SQ
            A("act", lambda e: e.activation(out=sq.t[0:nparts, ncols], in_=src_bank.t[0:nparts, ncols], func=AF.Square), r=[src_bank.r], w=[sq.r])
            b2_ = bank()
            on = ONF if fp32sq else ONB
            mm(b2_.t[0:nparts, ncols], on.t[0:nparts, 0:nparts], sq.t[0:nparts, ncols], True, True, [on.r, sq.r], [b2_.r])
            rr = R2 if nparts == 64 else R1
            A("dve", lambda e: e.tensor_scalar(out=rr.t[0:nparts, ncols], in0=b2_.t[0:nparts, ncols], scalar1=1.0 / n, scalar2=EPS,
                                               op0=ALU.mult, op1=ALU.add), r=[b2_.r], w=[rr.r])
            A("act", lambda e: e.activation(out=rr.t[0:nparts, ncols], in_=rr.t[0:nparts, ncols], func=AF.Sqrt), r=[rr.r], w=[rr.r])
            A("dve", lambda e: e.reciprocal(out=rr.t[0:nparts, ncols], in_=rr.t[0:nparts, ncols]), r=[rr.r], w=[rr.r])
            A("dve", lambda e: e.scalar_tensor_tensor(out=dst, in0=src_bank.t[0:nparts, ncols], scalar=gain_ap, in1=rr.t[0:nparts, ncols],
                                                      op0=ALU.mult, op1=ALU.mult), r=[src_bank.r, rr.r], w=[dst_r])

        bR = abank(0)
        for c in range(8):
            b = projW(O_ACQ + c * 128, 128)
            A("act", lambda e, b=b, c=c: e.activation(out=ACQ.t[:, c, TK], in_=b.t[:, TK], func=AF.Identity,
                                                      bias=bias(l, O_ACQ + c * 128), scale=1.0), r=[b.r, BIN.r], w=[ACQ.r])
            A("act", lambda e, b=b, c=c: e.activation(out=SQ.t[:, TK], in_=b.t[:, TK], func=AF.Square,
                                                      bias=bias(l, O_ACQ + c * 128), scale=1.0), r=[b.r, BIN.r], w=[SQ.r])
            mm(bR.t[:, TK], ONB.t[:], SQ.t[:, TK], c == 0, c == 7, [ONB.r, SQ.r], [bR.r])
        rstd_from(TB(R1.t[:, TK], R1.r), bR.t[:, TK], 1.0 / 1024, bR.r)
        for c in range(8):
            A("dve", lambda e, c=c: e.scalar_tensor_tensor(out=ACQ.t[:, c, TK], in0=ACQ.t[:, c, TK], scalar=CQG.t[:, l, c:c + 1],
                                                           in1=R1.t[:, TK], op0=ALU.mult, op1=ALU.mult), r=[ACQ.r, R1.r, CQG.r], w=[ACQ.r])
        bR = abank(1)
        for c in range(4):
            b = projW(O_ACKV + c * 128, 128)
            A("act", lambda e, b=b, c=c: e.activation(out=CKV32.t[:, c, TK], in_=b.t[:, TK], func=AF.Identity,
                                                      bias=bias(l, O_ACKV + c * 128), scale=1.0), r=[b.r, BIN.r], w=[CKV32.r])
            A("act", lambda e, c=c: e.activation(out=SQ.t[:, TK], in_=CKV32.t[:, c, TK], func=AF.Square), r=[CKV32.r], w=[SQ.r])
            mm(bR.t[:, TK], ONB.t[:], SQ.t[:, TK], c == 0, c == 3, [ONB.r, SQ.r], [bR.r])
        rstd_from(TB(R1.t[:, TK], R1.r), bR.t[:, TK], 1.0 / 512, bR.r)
        for c in range(4):
            A("dve", lambda e, c=c: e.scalar_tensor_tensor(out=CKV32.t[:, c, TK], in0=CKV32.t[:, c, TK], scalar=CKVG.t[:, l, c:c + 1],
                                                           in1=R1.t[:, TK], op0=ALU.mult, op1=ALU.mult), r=[CKV32.r, R1.r, CKVG.r], w=[CKV32.r])
        A("act", lambda e: e.copy(out=CKVB.t[:, :, TK], in_=CKV32.t[:, :, TK]), r=[CKV32.r], w=[CKVB.r])
        for j in range(nsub):
            b = bank()
            for c in range(4):
                tr(b.t[:, c * 128:(c + 1) * 128], CKV32.t[:, c, j * 128:(j + 1) * 128], IDF.t[:], [CKV32.r, IDF.r], [b.r])
            A("act", lambda e, b=b: e.copy(out=LATO.t[:, :], in_=b.t[:, :]), r=[b.r], w=[LATO.r])
            dst = lat_p[l, pos0 + j * 128:pos0 + (j + 1) * 128, :] if kind == "p" else lat_s[l]
            dma_out(dst, LATO.t[:, :], LATO.r)
        b = projW(O_AKR, 64)
        A("act", lambda e, b=b: e.activation(out=KR32.t[0:64, TK], in_=b.t[0:64, TK], func=AF.Identity, bias=bias(l, O_AKR, 64), scale=1.0),
          r=[b.r, BIN.r], w=[KR32.r])
        A("act", lambda e: e.activation(out=SQF.t[0:64, TK], in_=KR32.t[0:64, TK], func=AF.Square), r=[KR32.r], w=[SQF.r])
        b2 = bank()
        mm(b2.t[0:64, TK], ONF.t[0:64, 0:64], SQF.t[0:64, TK], True, True, [ONF.r, SQF.r], [b2.r])
        rstd_from(TB(R2.t[0:64, TK], R2.r), b2.t[0:64, TK], 1.0 / 64, b2.r)
        A("dve", lambda e: e.scalar_tensor_tensor(out=KR32.t[0:64, TK], in0=KR32.t[0:64, TK], scalar=KRG.t[:, l:l + 1], in1=R2.t[0:64, TK],
                                                  op0=ALU.mult, op1=ALU.mult), r=[KR32.r, R2.r, KRG.r], w=[KR32.r])
        b3 = bank()
        mm(b3.t[0:64, TK], ROTF.t[:], KR32.t[0:64, TK], True, True, [ROTF.r, KR32.r], [b3.r])
        A("dve", lambda e: e.tensor_tensor(out=TA.t[0:64, TK], in0=KR32.t[0:64, TK], in1=COS, op=ALU.mult), r=[KR32.r, CS.r], w=[TA.r])
        A("dve", lambda e: e.tensor_tensor(out=TBb.t[0:64, TK], in0=b3.t[0:64, TK], in1=SIN, op=ALU.mult), r=[b3.r, CS.r], w=[TBb.r])
        A("dve", lambda e: e.tensor_tensor(out=KR32.t[0:64, TK], in0=TA.t[0:64, TK], in1=TBb.t[0:64, TK], op=ALU.add), r=[TA.r, TBb.r], w=[KR32.r])
        A("act", lambda e: e.copy(out=KRB.t[0:64, TK], in_=KR32.t[0:64, TK]), r=[KR32.r], w=[KRB.r])
        for j in range(nsub):
            b = bank()
            tr(b.t[:, 0:64], KR32.t[0:64, j * 128:(j + 1) * 128], IDF.t[0:64, 0:64], [KR32.r, IDF.r], [b.r])
            A("act", lambda e, b=b: e.copy(out=KRO.t[:, :], in_=b.t[:, 0:64]), r=[b.r], w=[KRO.r])
            dst = kr_p[l, pos0 + j * 128:pos0 + (j + 1) * 128, :] if kind == "p" else kr_s[l]
            dma_out(dst, KRO.t[:, :], KRO.r)
        RKR = krc_res[l]
        if kind == "p":
            if not last:
                dma_out(KRC[l][:, pos0:pos0 + TT], KRB.t[0:64, TK], KRB.r, final=False, extra_w=[RKR])
            if ti > 0:
                P.op("sp", lambda e: e.dma_start(out=KRP.t[0:64, 0:pos0], in_=KRC[l][:, 0:pos0]), r=[RKR], w=[KRP.r], dma=True, semres=KRP.r)

        def attend(h, blocks, qs, nq, first_full=True):
            bO = abank(0); bD = abank(1)
            nb = len(blocks)
            for i, (kap, krap, vap, nk, q0, mk, rds) in enumerate(blocks):
                qsl = slice(qs + q0, qs + nq)
                osl = slice(q0, nq)
                bS = bank()
                mm(bS.t[0:nk, osl], kap, QNB.t[:, qsl], True, False, rds + [QNB.r], [bS.r])
                mm(bS.t[0:nk, osl], krap, QRB.t[0:64, qsl], False, True, rds + [QRB.r], [bS.r])
                PT = PTs[i % 3]
                A("act", lambda e, bS=bS, PT=PT, nk=nk, osl=osl: e.activation(out=PT.t[0:nk, osl], in_=bS.t[0:nk, osl], func=AF.Exp, scale=ATTN_SCALE),
                  r=[bS.r], w=[PT.r])
                if mk is not None:
                    A("dve", lambda e, PT=PT, nk=nk, osl=osl, mk=mk: e.tensor_tensor(out=PT.t[0:nk, osl], in0=PT.t[0:nk, osl], in1=mk, op=ALU.mult),
                      r=[PT.r, AMB.r], w=[PT.r])
                mm(bO.t[:, osl], vap, PT.t[0:nk, osl], i == 0, i == nb - 1, rds + [PT.r], [bO.r])
                mm(bD.t[:, osl], ONB.t[0:nk, :], PT.t[0:nk, osl], i == 0, i == nb - 1, [ONB.r, PT.r], [bD.r])
            osl = slice(0, nq)
            A("dve", lambda e: e.reciprocal(out=RDEN.t[:, osl], in_=bD.t[:, osl]), r=[bD.r], w=[RDEN.r])
            A("dve", lambda e: e.tensor_tensor(out=TO.t[:, osl], in0=bO.t[:, osl], in1=RDEN.t[:, osl], op=ALU.mult), r=[bO.r, RDEN.r], w=[TO.r])
            A("dve", lambda e: e.tensor_tensor(out=AOG.t[:, h, qs:qs + nq], in0=TO.t[:, osl], in1=SZ.t[:, qs:qs + nq], op=ALU.mult),
              r=[TO.r, SZ.r], w=[AOG.r])

        def head(h, cols, seg):
            nq = cols.stop - cols.start
            uq, uqr = wload(w_uq[l][:, h * 192:(h + 1) * 192], 8, 192, ("w_uq", l, h))
            ukv, ukvr = wload(w_ukv[l][:, h * 256:(h + 1) * 256], 4, 256, ("w_ukv", l, h))
            COSc = CS.t[:, 0, cols]; SINc = CS.t[:, 1, cols]
            b = bank()
            for kc in range(8):
                mm(b.t[:, cols], uq[:, kc, 0:128], ACQ.t[:, kc, cols], kc == 0, kc == 7, [uqr, ACQ.r], [b.r])
            rms_bcast(b, 128, 128, QNG.t[:, l:l + 1], QNB.t[:, cols], QNB.r, ncols=cols)
            b = bank()
            for kc in range(8):
                mm(b.t[0:64, cols], uq[:, kc, 128:192], ACQ.t[:, kc, cols], kc == 0, kc == 7, [uqr, ACQ.r], [b.r])
            rms_bcast(b, 64, 64, QRG.t[:, l:l + 1], QR32.t[0:64, cols], QR32.r, ncols=cols, fp32sq=True)
            A("act", lambda e: e.copy(out=QRNB.t[0:64, cols], in_=QR32.t[0:64, cols]), r=[QR32.r], w=[QRNB.r])
            b3 = bank()
            mm(b3.t[0:64, cols], ROTB.t[:], QRNB.t[0:64, cols], True, True, [ROTB.r, QRNB.r], [b3.r])
            A("dve", lambda e: e.tensor_tensor(out=TA.t[0:64, cols], in0=QR32.t[0:64, cols], in1=COSc, op=ALU.mult), r=[QR32.r, CS.r], w=[TA.r])
            A("dve", lambda e: e.tensor_tensor(out=TBb.t[0:64, cols], in0=b3.t[0:64, cols], in1=SINc, op=ALU.mult), r=[b3.r, CS.r], w=[TBb.r])
            A("dve", lambda e: e.tensor_tensor(out=QRB.t[0:64, cols], in0=TA.t[0:64, cols], in1=TBb.t[0:64, cols], op=ALU.add),
              r=[TA.r, TBb.r], w=[QRB.r])
            b = bank()
            for kc in range(4):
                mm(b.t[:, cols], ukv[:, kc, 0:128], CKVB.t[:, kc, cols], kc == 0, kc == 3, [ukvr, CKVB.r], [b.r])
            rms_bcast(b, 128, 128, KNG.t[:, l:l + 1], KNB.t[:, cols], KNB.r, ncols=cols)
            b = bank()
            if kind == "p":
                for j in range(nsub):
                    for kc in range(4):
                        mm(b.t[:, j * 128:(j + 1) * 128], CKVB.t[:, kc, j * 128:(j + 1) * 128], ukv[:, kc, 128:256], kc == 0, kc == 3,
                           [ukvr, CKVB.r], [b.r])
                A("act", lambda e: e.copy(out=VB.t[:, 0:nsub, :], in_=b.t[:, 0:nsub * 128].rearrange("p (j v) -> p j v", j=nsub)),
                  r=[b.r], w=[VB.r])
            else:
                for kc in range(4):
                    mm(b.t[0:32, 0:128], CKVB.t[:, kc, cols], ukv[:, kc, 128:256], kc == 0, kc == 3, [ukvr, CKVB.r], [b.r])
                A("act", lambda e: e.copy(out=VB.t[0:32, 0, :], in_=b.t[0:32, 0:128]), r=[b.r], w=[VB.r])
            bz = projW(O_AZ + h * 128, 128)
            A("act", lambda e: e.activation(out=SZ.t[:, TK], in_=bz.t[:, TK], func=AF.Silu, bias=bias(l, O_AZ + h * 128), scale=1.0),
              r=[bz.r, BIN.r], w=[SZ.r])
            blocks = []
            if kind == "p":
                RC = kvc_res[l][h]
                if not last:
                    dma_out(KC[l, h][:, pos0:pos0 + TT], KNB.t[:, TK], KNB.r, final=False, extra_w=[RC])
                    dma_out(VC[l, h][pos0:pos0 + TT, :].rearrange("(j p) v -> p j v", p=128), VB.t[:, 0:nsub, :], VB.r, final=False, extra_w=[RC])
                if ti > 0:
                    P.op("sp", lambda e: e.dma_start(out=KP.t[:, 0:pos0], in_=KC[l, h][:, 0:pos0]), r=[RC], w=[KP.r], dma=True, semres=KP.r)
                    P.op("sp", lambda e: e.dma_start(out=VP.t[:, 0:pos0 // 128, :],
                                                     in_=VC[l, h][0:pos0, :].rearrange("(j p) v -> p j v", p=128)),
                         r=[RC], w=[VP.r], dma=True, semres=VP.r)
                    for pb in range(pos0 // 128):
                        blocks.append((KP.t[:, pb * 128:(pb + 1) * 128], KRP.t[0:64, pb * 128:(pb + 1) * 128], VP.t[:, pb, :], 128, 0, None,
                                       [KP.r, KRP.r, VP.r]))
                for kb in range(nsub):
                    blocks.append((KNB.t[:, kb * 128:(kb + 1) * 128], KRB.t[0:64, kb * 128:(kb + 1) * 128], VB.t[:, kb, :], 128, kb * 128,
                                   AMB.t[:, kb, kb * 128:TT], [KNB.r, KRB.r, VB.r]))
            else:
                for half in range(2):
                    b = bank()
                    for kc in range(4):
                        mm(b.t[:, :], ukv[:, kc, 0:128], LATT.t[:, kc, half * 512:(half + 1) * 512], kc == 0, kc == 3, [ukvr, LATT.r], [b.r])
                    rms_bcast(b, 128, 128, KNG.t[:, l:l + 1], KP.t[:, half * 512:(half + 1) * 512], KP.r, ncols=slice(0, 512))
                for g in range(2):
                    b = bank()
                    for q in range(4):
                        kb = 4 * g + q
                        for kc in range(4):
                            mm(b.t[:, q * 128:(q + 1) * 128], LATT.t[:, kc, kb * 128:(kb + 1) * 128], ukv[:, kc, 128:256], kc == 0, kc == 3,
                               [ukvr, LATT.r], [b.r])
                    A("act", lambda e, b=b, g=g: e.copy(out=VP.t[:, 4 * g:4 * g + 4, :], in_=b.t[:, :].rearrange("p (q v) -> p q v", q=4)),
                      r=[b.r], w=[VP.r])
                for pb in range(8):
                    blocks.append((KP.t[:, pb * 128:(pb + 1) * 128], KRP.t[0:64, pb * 128:(pb + 1) * 128], VP.t[:, pb, :], 128, 0, None,
                                   [KP.r, KRP.r, VP.r]))
                blocks.append((KNB.t[:, cols], KRB.t[0:64, cols], VB.t[0:32, 0, :], 32, 0, None, [KNB.r, KRB.r, VB.r]))
            attend(h, blocks, cols.start, nq)

        if kind == "p":
            for h in range(16):
                head(h, TK, None)
        else:
            for s in range(4):
                P.op("pool", lambda e, s=s: e.dma_start(out=LPB.t[:, :, :], in_=latp_d[l, s].rearrange("(b p) f -> p b f", p=128)),
                     w=[LPB.r], dma=True)
                P.op("pool", lambda e, s=s: e.dma_start(out=KRPB.t[:, :, :], in_=krp_d[l, s].rearrange("(b p) f -> p b f", p=128)),
                     w=[KRPB.r], dma=True)
                for c in range(4):
                    b = bank()
                    for kb in range(8):
                        tr(bfv(b)[:, kb * 128:(kb + 1) * 128], LPB.t[:, kb, c * 128:(c + 1) * 128], IDB.t[:], [LPB.r, IDB.r], [b.r])
                    A("act", lambda e, b=b, c=c: e.copy(out=LATT.t[:, c, :], in_=bfv(b)[:, 0:1024]), r=[b.r], w=[LATT.r])
                b = bank()
                for kb in range(8):
                    tr(bfv(b)[0:64, kb * 128:(kb + 1) * 128], KRPB.t[:, kb, :], IDB.t[:], [KRPB.r, IDB.r], [b.r])
                A("act", lambda e, b=b: e.copy(out=KRP.t[0:64, 0:PAST], in_=bfv(b)[0:64, 0:1024]), r=[b.r], w=[KRP.r])
                for h in range(16):
                    head(h, slice(s * 32, (s + 1) * 32), s)

        areset()
        MT = aal("MT", [32, TT], BF16)
        SG = [aal(f"SG{i}", [TT]) for i in range(3)]
        TM = aal("TM", [TT]); TM2 = aal("TM2", [TT])
        for dc in range(32):
            bg = []
            for br in range(3):
                b = projW(O_G + br * D + dc * 128, 128)
                A("act", lambda e, b=b, br=br, dc=dc: e.activation(out=SG[br].t[:, TK], in_=b.t[:, TK], func=AF.Sigmoid,
                                                                   bias=bias(l, O_G + br * D + dc * 128), scale=1.0),
                  r=[b.r, BIN.r], w=[SG[br].r])
            ys = []
            for (wi, (wsrc, nk, act)) in enumerate(((w_pc, 8, CVG), (w_pm, 8, HMG), (w_pa, 16, AOG))):
                wv, wr = wload(wsrc[l][:, dc * 128:(dc + 1) * 128], nk, 128, ("w_p", wi, l, dc))
                b = bank()
                for kc in range(nk):
                    mm(b.t[:, TK], wv[:, kc, :], act.t[:, kc, TK], kc == 0, kc == nk - 1, [wr, act.r], [b.r])
                ys.append(b)
            A("dve", lambda e, ys=ys: e.tensor_tensor(out=TM.t[:, TK], in0=ys[0].t[:, TK], in1=SG[0].t[:, TK], op=ALU.mult), r=[ys[0].r, SG[0].r], w=[TM.r])
            A("dve", lambda e, ys=ys: e.tensor_tensor(out=TM2.t[:, TK], in0=ys[1].t[:, TK], in1=SG[1].t[:, TK], op=ALU.mult), r=[ys[1].r, SG[1].r], w=[TM2.r])
            A("dve", lambda e: e.tensor_tensor(out=TM.t[:, TK], in0=TM.t[:, TK], in1=TM2.t[:, TK], op=ALU.add), r=[TM.r, TM2.r], w=[TM.r])
            A("dve", lambda e, ys=ys: e.tensor_tensor(out=TM2.t[:, TK], in0=ys[2].t[:, TK], in1=SG[2].t[:, TK], op=ALU.mult), r=[ys[2].r, SG[2].r], w=[TM2.r])
            A("dve", lambda e, dc=dc: e.tensor_tensor(out=MT.t[:, dc, TK], in0=TM.t[:, TK], in1=TM2.t[:, TK], op=ALU.add), r=[TM.r, TM2.r], w=[MT.r])

        if kind == "s" and l == 0:
            dbg("CVG", CVG.t[:, :, TK], CVG.r, [128, 8, 128], BF16); dbg("HMG", HMG.t[:, :, TK], HMG.r, [128, 8, 128], BF16)
            dbg("AOG", AOG.t[:, :, TK], AOG.r, [128, 16, 128], BF16); dbg("MT", MT.t[:, :, TK], MT.r, [128, 32, 128], BF16)
        for cg in range(8):
            bo = [abank(j) for j in range(nsub)]
            for kq in range(4):
                wv, wr = wload(w_out[l][kq * 1024:(kq + 1) * 1024, cg * 512:(cg + 1) * 512], 8, 512, ("w_out", l, kq, cg))
                for j in range(nsub):
                    for k8 in range(8):
                        kc = kq * 8 + k8
                        mm(bo[j].t[:, :], MT.t[:, kc, j * 128:(j + 1) * 128], wv[:, k8, :], kc == 0, kc == 31, [wr, MT.r], [bo[j].r])
            for j in range(nsub):
                A("dve", lambda e, j=j, bo=bo, cg=cg: e.tensor_tensor(out=Xs[j].t[:, cg * 512:(cg + 1) * 512], in0=Xs[j].t[:, cg * 512:(cg + 1) * 512],
                                                                      in1=bo[j].t[:, :], op=ALU.add), r=[Xs[j].r, bo[j].r], w=[Xs[j].r])
        if l == NLAYER - 1:
            for j in range(nsub):
                dst = y_p[pos0 + j * 128:pos0 + (j + 1) * 128, :] if kind == "p" else y_s[:, :]
                dma_out(dst, Xs[j].t[:], Xs[j].r)

    kvc_res = [[P.res(f"kvc{l}_{h}") for h in range(16)] for l in range(2)]
    krc_res = [P.res(f"krc{l}") for l in range(2)]
    tiles = []
    for i in range(NTILES_P):
        tiles.append(dict(kind="p", ntok=TT, pos0=i * TT, idx=i, last=(i == NPT - 1)))
    if DO_SAMPLE:
        tiles.append(dict(kind="s", ntok=128, pos0=PAST, idx=0, last=True))
    for tile in tiles:
        for l in range(NLAYER):
            process(tile, l)
    P.emit()
    st.close()
    return nc


def host_consts():
    ident = np.eye(128, dtype=np.float32)
    rot = np.zeros((64, 64), np.float32)
    for i in range(32):
        rot[i + 32, i] = -1.0
        rot[i, i + 32] = 1.0
    utri = np.triu(np.ones((64, 64), np.float32))
    half = 32
    freqs = (np.float32(10000.0) ** (-np.arange(half, dtype=np.float32) / np.float32(half))).astype(np.float32)

    def cs(pos):
        ang = pos.astype(np.float32)[None, :] * freqs[:, None]
        c = np.cos(ang).astype(np.float32)
        s = np.sin(ang).astype(np.float32)
        return np.stack([np.concatenate([c, c], 0), np.concatenate([s, s], 0)], axis=1)
    csp = cs(np.arange(SEQ))
    css = np.tile(cs(PAST + np.arange(32)), (1, 1, 4))
    am = np.zeros((128, 2, TT), np.float32)
    for kb in range(2):
        for p in range(128):
            kk = kb * 128 + p
            am[p, kb, :] = ((np.arange(TT) // 64) >= (kk // 64)).astype(np.float32)
    return dict(c_ident=ident, c_rot=rot, c_utri=utri, c_csp=np.ascontiguousarray(csp), c_css=np.ascontiguousarray(css),
                c_am=am, c_eye4=np.eye(4, dtype=np.float32))


_WNAMES = ["ln_g", "w_in", "b_in", "conv_w", "conv_b", "conv_ln_g", "conv_ln_b", "w_pc", "m_norm_g", "w_pm", "cq_g", "ckv_g",
           "qn_g", "qr_g", "kn_g", "kr_g", "w_uq", "w_ukv", "w_pa", "w_out"]


def kernel(**inp):
    cfg = {}
    nc = build(cfg)
    consts = host_consts()
    f = lambda a: np.ascontiguousarray(np.asarray(a, dtype=np.float32))
    W = {k: f(inp[k]) for k in _WNAMES}
    in_maps = []
    for c in range(8):
        sl = slice(4 * c, 4 * c + 4)
        m = dict(W)
        m.update(consts)
        m["xp"] = f(inp["x_prompt"][c])
        m["xs"] = f(inp["x_sample"][sl]).reshape(128, D)
        m["latc"] = f(inp["cache_kv_latent"][:, sl])
        m["krc"] = f(inp["cache_k_rope"][:, sl])
        m["stconv"] = f(inp["state_conv"][:, sl])
        m["stC"] = f(inp["state_mlstm_C"][:, sl])
        m["stn"] = f(inp["state_mlstm_n"][:, sl])
        m["stm"] = f(inp["state_mlstm_m"][:, sl])
        in_maps.append(m)
    res = run_bass_kernel_spmd(nc, in_maps, core_ids=list(range(8)))
    R = res.results
    cat = lambda k, ax: np.concatenate([np.asarray(r[k]) for r in R], axis=ax)
    stk = lambda k, ax: np.stack([np.asarray(r[k]) for r in R], axis=ax)
    y_prompt = stk("y_p", 0)
    y_sample = cat("y_s", 0).reshape(32, 32, D)
    return (y_prompt, y_sample,
            stk("cs_p", 1), cat("cs_s", 1),
            stk("C_p", 1), stk("n_p", 1), stk("m_p", 1),
            cat("C_s", 1), cat("n_s", 1), cat("m_s", 1),
            stk("lat_p", 1), stk("kr_p", 1),
            cat("lat_s", 1).reshape(2, 32, 32, 512), cat("kr_s", 1).reshape(2, 32, 32, 64))
```

```python
import numpy as np
from contextlib import ExitStack
import concourse.bass as bass
import concourse.mybir as mybir
from concourse.bass_utils import run_bass_kernel_spmd

F32 = mybir.dt.float32
BF16 = mybir.dt.bfloat16
AF = mybir.ActivationFunctionType
ALU = mybir.AluOpType
AX = mybir.AxisListType

D = 4096
NIN = 23112
SEQ = 2048
TT = 256
NPT = SEQ // TT
PAST = 1024
EPS = 1e-6
ATTN_SCALE = 192.0 ** -0.5
O_CA, O_CB, O_CZ, O_MQ, O_MK, O_MV, O_MIF, O_MO, O_MZ = 0, 1024, 2048, 3072, 3584, 4096, 5120, 5128, 6152
O_ACQ, O_ACKV, O_AKR, O_AZ, O_G = 7176, 8200, 8712, 8776, 10824
NBCH = 182


def cid(col):
    if col < 5120:
        return col // 128
    if col == 5120:
        return 40
    if col < 8712:
        return 41 + (col - 5128) // 128
    if col == 8712:
        return 69
    return 70 + (col - 8776) // 128


ENGS = ("pe", "act", "dve", "pool", "sp")
SAME_ENGINE_SYNC = True
EPOCH = 30000


class Res:
    __slots__ = ("name", "last_w", "readers", "sem", "semcnt", "arena")

    def __init__(self, name, arena=False):
        self.name = name
        self.last_w = None
        self.readers = {}
        self.sem = None
        self.semcnt = 0
        self.arena = arena


_SEQ = [0]


class Op:
    __slots__ = ("eng", "fn", "deps", "dma", "flag", "sem", "val", "seq")

    def __init__(self, eng, fn, dma):
        _SEQ[0] += 1
        self.seq = _SEQ[0]
        self.eng = eng
        self.fn = fn
        self.deps = []
        self.dma = dma
        self.flag = False
        self.sem = None
        self.val = 0


class _Rec:
    def __getattr__(self, name):
        return lambda *a, **k: (name, a, k)


_REC = _Rec()


class Prog:
    def __init__(self, nc, stack):
        self.nc = nc
        self.stack = stack
        self.ops = {e: [] for e in ENGS}
        self.out_dmas = []
        self.nres = 0
        self.token = Res("arena_token")
        self.dsem = {}
        self.verbose = False

    def res(self, name=None, arena=False):
        self.nres += 1
        return Res(name or f"r{self.nres}", arena)

    def _newsem(self, name):
        return self.stack.enter_context(self.nc.semaphore(name))

    def op(self, eng, fn, r=(), w=(), dma=False, out=False, semres=None):
        o = Op(eng, fn(_REC) if fn is not None else None, dma)
        deps = o.deps
        r = list(r)
        w = list(w)
        if any(R.arena for R in r) or any(R.arena for R in w):
            r.append(self.token)
        if dma:
            R0 = semres if semres is not None else (w + r)[0]
            ent = self.dsem.get(R0.name)
            if ent is None:
                ent = [self._newsem("d" + R0.name), 0]
                self.dsem[R0.name] = ent
            ent[1] += 16
            o.sem = ent[0]
            o.val = ent[1]
            if out:
                self.out_dmas.append(o)
        o_sem_key = o.sem
        for R in r:
            if R.last_w is not None:
                deps.append((R.last_w, 0))
        for R in w:
            if R.last_w is not None:
                deps.append((R.last_w, 1))
            for rd in R.readers.values():
                deps.append((rd, 2))
        key = (eng, id(o_sem_key)) if dma else (eng, None)
        for R in r:
            R.readers[key] = o
        for R in w:
            R.last_w = o
            R.readers = {}
        self.ops[eng].append(o)
        return o

    def barrier(self):
        self.op("act", lambda e: e.nop(), w=[self.token])

    def _need(self, o, d, kind):
        if d.dma:
            return True
        if d.eng == o.eng and not o.dma:
            if d.eng == "pe":
                return False
            return kind == 0 and SAME_ENGINE_SYNC
        return True

    def emit(self):
        nc = self.nc
        for e in ENGS:
            for o in self.ops[e]:
                for (d, kind) in o.deps:
                    if not d.dma and self._need(o, d, kind):
                        d.flag = True
        fin = Op("sp", None, False)
        fin.deps = [(d, 0) for d in self.out_dmas]
        self.ops["sp"].append(fin)
        for e in ENGS:
            cnt = 0
            sems = {}
            for o in self.ops[e]:
                if o.dma or not o.flag:
                    continue
                ep = cnt // EPOCH
                if ep not in sems:
                    sems[ep] = self._newsem(f"e{e}{ep}")
                o.sem = sems[ep]
                o.val = cnt % EPOCH + 1
                cnt += 1

        if self.verbose:
            mx = {e: max([o.val for o in self.ops[e] if o.sem is not None] + [0]) for e in ENGS}
            print("ops per engine", {e: len(self.ops[e]) for e in ENGS}, "max sem val per engine", mx,
                  "dma sems", len(self.dsem), "max dma val", max([v[1] for v in self.dsem.values()] + [0]), flush=True)

        def run(engname):
            def body(eng):
                waited = {}
                for o in self.ops[engname]:
                    for (d, kind) in o.deps:
                        if not self._need(o, d, kind):
                            continue
                        k = id(d.sem)
                        if waited.get(k, 0) >= d.val:
                            continue
                        eng.wait_ge(d.sem, d.val)
                        waited[k] = d.val
                    if o.fn is None:
                        continue
                    name, a, k = o.fn
                    ins = getattr(eng, name)(*a, **k)
                    if o.dma:
                        ins.then_inc(o.sem, 16)
                    elif o.flag:
                        ins.then_inc(o.sem, 1)
            return body

        with nc.Block() as block:
            block.tensor(run("pe"))
            block.scalar(run("act"))
            block.vector(run("dve"))
            block.gpsimd(run("pool"))
            block.sync(run("sp"))


def build(cfg):
    NTILES_P = cfg.get("ntiles_p", NPT)
    DO_SAMPLE = cfg.get("sample", True)
    NLAYER = cfg.get("nlayer", 2)
    USE_WC = cfg.get("use_wc", True)
    nc = bass.Bass("TRN2", target_bir_lowering=False)

    def din(name, shape, dt=F32):
        return nc.dram_tensor(name, list(shape), dt, kind="ExternalInput").ap()

    def dout(name, shape):
        return nc.dram_tensor(name, list(shape), F32, kind="ExternalOutput").ap()

    xp = din("xp", [SEQ, D]); xs = din("xs", [128, D])
    latp_d = din("latc", [2, 4, PAST, 512]); krp_d = din("krc", [2, 4, PAST, 64])
    stconv = din("stconv", [2, 4, 30, 1024]); stC = din("stC", [2, 4, 4, 128, 256])
    stn = din("stn", [2, 4, 4, 128]); stm = din("stm", [2, 4, 4])
    w_in = din("w_in", [2, D, NIN])
    w_pc = din("w_pc", [2, 1024, D]); w_pm = din("w_pm", [2, 1024, D])
    w_uq = din("w_uq", [2, 1024, 3072]); w_ukv = din("w_ukv", [2, 512, D]); w_pa = din("w_pa", [2, 2048, D])
    w_out = din("w_out", [2, D, D])
    h_lng = din("h_lng", [128, 2, 32]); h_bin = din("h_bin", [128, 2, NBCH]); h_cw = din("h_cw", [128, 2, 8, 31])
    h_cb = din("h_cb", [128, 2, 8]); h_clg = din("h_clg", [128, 2, 8]); h_clb = din("h_clb", [128, 2, 8]); h_mng = din("h_mng", [128, 2, 8])
    h_cqg = din("h_cqg", [128, 2, 8]); h_ckvg = din("h_ckvg", [128, 2, 4]); h_qng = din("h_qng", [128, 2]); h_kng = din("h_kng", [128, 2])
    h_qrg = din("h_qrg", [64, 2]); h_krg = din("h_krg", [64, 2])
    c_ident = din("c_ident", [128, 128]); c_rot = din("c_rot", [64, 64]); c_utri = din("c_utri", [64, 64])
    c_csp = din("c_csp", [64, 2, SEQ]); c_css = din("c_css", [64, 2, 128]); c_am = din("c_am", [128, 2, TT])
    c_eye4 = din("c_eye4", [4, 4])

    y_p = dout("y_p", [SEQ, D]); y_s = dout("y_s", [128, D])
    cs_p = dout("cs_p", [2, 30, 1024]); cs_s = dout("cs_s", [2, 4, 30, 1024])
    C_p = dout("C_p", [2, 4, 128, 256]); n_p = dout("n_p", [2, 4, 128]); m_p = dout("m_p", [2, 4])
    C_s = dout("C_s", [2, 4, 4, 128, 256]); n_s = dout("n_s", [2, 4, 4, 128]); m_s = dout("m_s", [2, 4, 4])
    lat_p = dout("lat_p", [2, SEQ, 512]); kr_p = dout("kr_p", [2, SEQ, 64])
    lat_s = dout("lat_s", [2, 128, 512]); kr_s = dout("kr_s", [2, 128, 64])
    KC = nc.dram_tensor("KC", [2, 16, 128, SEQ], BF16).ap()
    VC = nc.dram_tensor("VC", [2, 16, SEQ, 128], BF16).ap()
    KRC = nc.dram_tensor("KRC", [2, 64, SEQ], BF16).ap()

    st = ExitStack()
    P = Prog(nc, st)
    P.verbose = cfg.get("verbose", False)

    def sbt(name, shape, dt=F32):
        return st.enter_context(nc.sbuf_tensor(name, list(shape), dt))

    class TB:
        def __init__(self, t, res):
            self.t = t
            self.r = res

    def pers(name, shape, dt=F32):
        return TB(sbt(name, shape, dt), P.res(name))

    NSUB = TT // 128
    Xs = [pers(f"X{j}", [128, D]) for j in range(NSUB)]
    HT = pers("HT", [128, 32, TT], BF16)
    CVG = pers("CVG", [128, 8, TT], BF16)
    HMG = pers("HMG", [128, 8, TT], BF16)
    AOG = pers("AOG", [128, 16, TT], BF16)
    NW = 4
    WS = [pers(f"WS{i}", [128, 4096], BF16) for i in range(NW)]
    IDF = pers("IDF", [128, 128]); IDB = pers("IDB", [128, 128], BF16)
    ONF = pers("ONF", [128, 128]); ONB = pers("ONB", [128, 128], BF16)
    ROTF = pers("ROTF", [64, 64]); ROTB = pers("ROTB", [64, 64], BF16)
    UTF = pers("UTF", [64, 64]); UTB = pers("UTB", [64, 64], BF16)
    EYE4 = pers("EYE4", [4, 4])
    AMF = pers("AMF", [128, 2, TT]); AMB = pers("AMB", [128, 2, TT], BF16)
    CS = pers("CS", [64, 2, TT])
    LNG = pers("LNG", [128, 2, 32]); BIN = pers("BIN", [128, 2, NBCH])
    CW = pers("CW", [128, 2, 8, 31]); CB_ = pers("CB", [128, 2, 8]); CLG = pers("CLG", [128, 2, 8]); CLB = pers("CLB", [128, 2, 8])
    MNG = pers("MNG", [128, 2, 8]); CQG = pers("CQG", [128, 2, 8]); CKVG = pers("CKVG", [128, 2, 4])
    QNG = pers("QNG", [128, 2]); KNG = pers("KNG", [128, 2]); QRG = pers("QRG", [64, 2]); KRG = pers("KRG", [64, 2])
    CT = [pers(f"CT{l}", [128, 8, 30]) for l in range(2)]
    CNP = [pers(f"CNP{l}", [128, 4, 257]) for l in range(2)]
    MRP = [pers(f"MRP{l}", [4, 1]) for l in range(2)]
    ARENA_BYTES = 72 * 1024
    ARENA = sbt("ARENA", [128, ARENA_BYTES // 2], BF16)
    astate = dict(lo=0, hi=ARENA_BYTES, side=0, phase=0)
    alive = []
    ancient = Res("ancient")

    def _merge(dst, k, op):
        cur = dst.get(k)
        if cur is None or cur.seq < op.seq:
            dst[k] = op

    def _inherit(new, old):
        for k, op in old.readers.items():
            _merge(new.readers, k, op)
        lw = old.last_w
        if lw is not None:
            _merge(new.readers, (lw.eng, id(lw.sem) if lw.dma else None), lw)

    def areset():
        astate["phase"] += 1
        astate["side"] ^= 1
        astate["lo"] = 0
        astate["hi"] = ARENA_BYTES
        ph = astate["phase"]
        keep = []
        for ent in alive:
            if ent[3] < ph - 4:
                _inherit(ancient, ent[2])
            else:
                keep.append(ent)
        alive[:] = keep

    def aal(name, shape, dt=F32, parts=128):
        esz = 4 if dt == F32 else 2
        n = int(np.prod(shape))
        nb = (n * esz + 63) // 64 * 64
        if astate["side"] == 0:
            o = astate["lo"]
            astate["lo"] = o + nb
        else:
            astate["hi"] -= nb
            o = astate["hi"]
        assert astate["lo"] <= astate["hi"], (name, astate)
        ap = ARENA[0:parts, o // 2:(o + n * esz) // 2]
        if dt == F32:
            ap = ap.bitcast(F32)
        if len(shape) == 2:
            ap = ap.rearrange("p (a b) -> p a b", a=shape[0])
        elif len(shape) == 3:
            ap = ap.rearrange("p (a b c) -> p a b c", a=shape[0], b=shape[1])
        R = P.res(name)
        _inherit(R, ancient)
        for (s0, e0, R0, ph0) in alive:
            if s0 < o + nb and o < e0:
                _inherit(R, R0)
        alive.append((o, o + nb, R, astate["phase"]))
        return TB(ap, R)

    banks = [TB(st.enter_context(nc.psum_tensor(f"pb{i}", [128, 512], F32)), P.res(f"pb{i}")) for i in range(8)]
    bcnt = [0]

    pinned = set()

    def bank(pin=False):
        while True:
            i = bcnt[0] % 6
            bcnt[0] += 1
            if i not in pinned:
                break
        if pin:
            pinned.add(i)
        return banks[i]

    def unpin(b):
        pinned.discard(banks.index(b))

    def abank(i):
        return banks[6 + i]

    def bfv(b):
        return b.t.bitcast(BF16)

    wcnt = [0]
    NIMG = 720
    WPER = 180
    WCS = [nc.dram_tensor(f"WC{i}", [WPER, 128, 4096], BF16).ap() for i in range(NIMG // WPER)]
    wc_index = {}
    WST = [P.res(f"WSst{i}") for i in range(NW)]

    def wload(src, nk, ncols, key):
        i = wcnt[0] % NW
        s = WS[i]
        wcnt[0] += 1
        n = nk * ncols
        v = s.t[:, 0:n].rearrange("p (k c) -> p k c", k=nk)
        ent = wc_index.get(key)
        if ent is None:
            idx = len(wc_index)
            assert idx < NIMG
            R = P.res(f"wc{idx}")
            wc_index[key] = (idx, R)
            P.op("pool", lambda e: e.dma_start(out=v, in_=src.rearrange("(k p) c -> p k c", p=128)), w=[s.r], dma=True)
            if USE_WC:
                P.op("sp", lambda e: e.dma_start(out=WCS[idx // WPER][idx % WPER][:, 0:n], in_=s.t[:, 0:n]), r=[s.r], w=[R], dma=True, semres=WST[i])
        else:
            idx, R = ent
            if USE_WC:
                P.op("sp", lambda e: e.dma_start(out=s.t[:, 0:n], in_=WCS[idx // WPER][idx % WPER][:, 0:n]), r=[R], w=[s.r], dma=True)
            else:
                P.op("pool", lambda e: e.dma_start(out=v, in_=src.rearrange("(k p) c -> p k c", p=128)), w=[s.r], dma=True)
        return v, s.r

    def dma_in(dst, src, R, slow=False, eng="sp"):
        P.op(eng, lambda e: e.dma_start(out=dst, in_=src, allow_slow_non_contiguous=slow), w=[R], dma=True)

    def dma_out(dst, src, R, slow=False, final=True, extra_w=()):
        P.op("sp", lambda e: e.dma_start(out=dst, in_=src, allow_slow_non_contiguous=slow), r=[R], w=list(extra_w),
             dma=True, out=final, semres=R)

    A = lambda eng, fn, r=(), w=(): P.op(eng, fn, r=r, w=w)
    DBG = cfg.get("dbg", False)

    def dbg(name, ap, R, shape, dt=F32):
        if DBG:
            d = nc.dram_tensor("dbg_" + name, list(shape), dt, kind="ExternalOutput").ap()
            dma_out(d, ap, R, slow=True)

    def mm(out, lhsT, rhs, start, stop, r, w):
        P.op("pe", lambda e: e.matmul(out, lhsT, rhs, start=start, stop=stop), r=r, w=w)

    def tr(out, in_, ident, r, w):
        P.op("pe", lambda e: e.transpose(out, in_, ident), r=r, w=w)

    dma_in(IDF.t[:], c_ident, IDF.r); dma_in(ROTF.t[:], c_rot, ROTF.r); dma_in(UTF.t[:], c_utri, UTF.r)
    dma_in(EYE4.t[:], c_eye4, EYE4.r); dma_in(AMF.t[:], c_am, AMF.r)
    A("dve", lambda e: e.tensor_copy(out=IDB.t[:], in_=IDF.t[:]), r=[IDF.r], w=[IDB.r])
    A("dve", lambda e: e.tensor_copy(out=ROTB.t[:], in_=ROTF.t[:]), r=[ROTF.r], w=[ROTB.r])
    A("dve", lambda e: e.tensor_copy(out=UTB.t[:], in_=UTF.t[:]), r=[UTF.r], w=[UTB.r])
    A("dve", lambda e: e.tensor_copy(out=AMB.t[:], in_=AMF.t[:]), r=[AMF.r], w=[AMB.r])
    A("dve", lambda e: e.memset(ONF.t[:], 1.0), w=[ONF.r])
    A("dve", lambda e: e.memset(ONB.t[:], 1.0), w=[ONB.r])
    for (dst, src) in ((LNG, h_lng), (BIN, h_bin), (CW, h_cw), (CB_, h_cb), (CLG, h_clg), (CLB, h_clb), (MNG, h_mng),
                       (CQG, h_cqg), (CKVG, h_ckvg), (QNG, h_qng), (KNG, h_kng), (QRG, h_qrg), (KRG, h_krg)):
        dma_in(dst.t[:], src, dst.r)
    for l in range(2):
        A("dve", lambda e, l=l: e.memset(CT[l].t[:], 0.0), w=[CT[l].r])
        A("dve", lambda e, l=l: e.memset(CNP[l].t[:], 0.0), w=[CNP[l].r])
        A("dve", lambda e, l=l: e.memset(MRP[l].t[:], 0.0), w=[MRP[l].r])

    def bias(l, col, n=128):
        return BIN.t[0:n, l, cid(col):cid(col) + 1]

    def rstd_from(dst, src_ap, scale, R_src, parts=128):
        A("dve", lambda e: e.tensor_scalar(out=dst.t, in0=src_ap, scalar1=scale, scalar2=EPS, op0=ALU.mult, op1=ALU.add),
          r=[R_src], w=[dst.r])
        A("act", lambda e: e.activation(out=dst.t, in_=dst.t, func=AF.Sqrt), r=[dst.r], w=[dst.r])
        A("dve", lambda e: e.reciprocal(out=dst.t, in_=dst.t), r=[dst.r], w=[dst.r])

    def process(tile, l):
        kind = tile["kind"]
        ntok = tile["ntok"]
        nsub = ntok // 128
        pos0 = tile["pos0"]
        ti = tile["idx"]
        last = tile["last"]
        L = 64 if kind == "p" else 32
        nch = 4
        TK = slice(0, ntok)

        def w_in_cols(c0, ncols):
            return wload(w_in[l][:, c0:c0 + ncols], 32, ncols, ("w_in", l, c0, ncols))

        def projW(c0, ncols):
            wv, wr = w_in_cols(c0, ncols)
            b = bank()
            for kc in range(32):
                mm(b.t[0:ncols, TK], wv[:, kc, :], HT.t[:, kc, TK], kc == 0, kc == 31, [wr, HT.r], [b.r])
            return b

        areset()
        XS = aal("XS", [D], BF16)
        SSQ = aal("SSQ", [4]); RSTD0 = aal("RSTD0", [4])
        for j in range(nsub):
            Xj = Xs[j]
            if l == 0:
                src = (xp[pos0 + j * 128: pos0 + (j + 1) * 128, :] if kind == "p" else xs[:, :])
                dma_in(Xj.t[:], src, Xj.r)
            A("act", lambda e, Xj=Xj, j=j: e.activation(out=XS.t, in_=Xj.t[:], func=AF.Square, accum_out=SSQ.t[:, j:j + 1]),
              r=[Xj.r], w=[XS.r, SSQ.r])
            A("dve", lambda e, j=j: e.tensor_scalar(out=RSTD0.t[:, j:j + 1], in0=SSQ.t[:, j:j + 1], scalar1=1.0 / D, scalar2=EPS,
                                                    op0=ALU.mult, op1=ALU.add), r=[SSQ.r], w=[RSTD0.r])
            A("act", lambda e, j=j: e.activation(out=RSTD0.t[:, j:j + 1], in_=RSTD0.t[:, j:j + 1], func=AF.Sqrt), r=[RSTD0.r], w=[RSTD0.r])
            A("dve", lambda e, j=j: e.reciprocal(out=RSTD0.t[:, j:j + 1], in_=RSTD0.t[:, j:j + 1]), r=[RSTD0.r], w=[RSTD0.r])
            A("act", lambda e, Xj=Xj, j=j: e.activation(out=XS.t, in_=Xj.t[:], func=AF.Copy, scale=RSTD0.t[:, j:j + 1]),
              r=[Xj.r, RSTD0.r], w=[XS.r])
            for g in range(8):
                b = bank()
                for q in range(4):
                    c = 4 * g + q
                    tr(bfv(b)[:, q * 128:(q + 1) * 128], XS.t[:, c * 128:(c + 1) * 128], IDB.t[:], [XS.r, IDB.r], [b.r])
                A("dve", lambda e, b=b, g=g, j=j: e.tensor_tensor(
                    out=HT.t[:, 4 * g:4 * g + 4, j * 128:(j + 1) * 128],
                    in0=bfv(b)[:, 0:512].rearrange("p (q t) -> p q t", q=4),
                    in1=LNG.t[:, l, 4 * g:4 * g + 4].unsqueeze(2).broadcast_to([128, 4, 128]), op=ALU.mult),
                  r=[b.r, LNG.r], w=[HT.r])

        areset()
        UF = aal("UF", [8, 30 + TT])
        ACC = aal("ACC", [8, TT])
        SIG = aal("SIG", [TT]); SQC = aal("SQC", [TT]); MEAN = aal("MEAN", [TT]); RSTDC = aal("RSTDC", [TT]); SZC = aal("SZC", [TT])
        CSO = aal("CSO", [1024], F32, parts=32)
        STG = aal("STG", [1024], F32, parts=32)
        if kind == "p":
            def ufnew(cc): return UF.t[:, cc, 30:30 + TT]
            def ufwin(cc, k): return UF.t[:, cc, k:k + TT]
            def accv(cc): return ACC.t[:, cc, 0:TT]
            def psv(ap): return ap
            def v2(t): return t.t[:, 0:TT]
            A("dve", lambda e: e.tensor_copy(out=UF.t[:, :, 0:30], in_=CT[l].t[:]), r=[CT[l].r], w=[UF.r])
        else:
            UFs = UF.t[:, :, 0:248].rearrange("p c (s t) -> p c s t", s=4)
            def ufnew(cc): return UFs[:, cc, :, 30:62]
            def ufwin(cc, k): return UFs[:, cc, :, k:k + 32]
            def accv(cc): return ACC.t[:, cc, 0:128].rearrange("p (s t) -> p s t", s=4)
            def psv(ap): return ap.rearrange("p (s t) -> p s t", s=4)
            def v2(t): return t.t[:, 0:128].rearrange("p (s t) -> p s t", s=4)
            for s in range(4):
                dma_in(STG.t[0:30, :], stconv[l, s], STG.r)
                for g in range(2):
                    b = bank()
                    for q in range(4):
                        cc = 4 * g + q
                        tr(b.t[:, q * 32:q * 32 + 30], STG.t[0:30, cc * 128:(cc + 1) * 128], IDF.t[0:30, 0:30], [STG.r, IDF.r], [b.r])
                    A("dve", lambda e, b=b, g=g, s=s: e.tensor_copy(
                        out=UFs[:, 4 * g:4 * g + 4, s, 0:30],
                        in_=b.t[:, 0:128].rearrange("p (q t) -> p q t", q=4)[:, :, 0:30]), r=[b.r], w=[UF.r])
        bS1 = abank(0); bS2 = abank(1)
        UFr = [P.res(f"UF{c}") for c in range(8)]
        ACCr = [P.res(f"ACC{c}") for c in range(8)]
        for c in range(8):
            _inherit(UFr[c], UF.r)
            _inherit(ACCr[c], ACC.r)
        for ent in list(alive):
            if ent[2] is UF.r:
                alive.extend((ent[0], ent[1], R_, ent[3]) for R_ in UFr)
            if ent[2] is ACC.r:
                alive.extend((ent[0], ent[1], R_, ent[3]) for R_ in ACCr)
        SQCs = [SQC, aal("SQC2", [TT])]
        SIGs = [SIG, aal("SIG2", [TT])]
        for cp in range(4):
            pair = (2 * cp, 2 * cp + 1)
            for i_, cc in enumerate(pair):
                ba = projW(O_CA + cc * 128, 128)
                bb = projW(O_CB + cc * 128, 128)
                SG_ = SIGs[i_]
                A("act", lambda e, bb=bb, cc=cc, SG_=SG_: e.activation(out=SG_.t[:, TK], in_=bb.t[:, TK], func=AF.Sigmoid,
                                                                        bias=bias(l, O_CB + cc * 128), scale=1.0), r=[bb.r, BIN.r], w=[SG_.r])
                A("dve", lambda e, ba=ba, cc=cc, SG_=SG_: e.scalar_tensor_tensor(out=ufnew(cc), in0=psv(ba.t[:, TK]), scalar=bias(l, O_CA + cc * 128),
                                                                                 in1=psv(SG_.t[:, TK]), op0=ALU.add, op1=ALU.mult),
                  r=[ba.r, SG_.r, BIN.r], w=[UFr[cc]])
                A("dve", lambda e, cc=cc: e.tensor_scalar(out=accv(cc), in0=ufwin(cc, 0), scalar1=CW.t[:, l, cc, 0:1],
                                                          scalar2=CB_.t[:, l, cc:cc + 1], op0=ALU.mult, op1=ALU.add),
                  r=[UFr[cc], CW.r, CB_.r], w=[ACCr[cc]])
            for k in range(1, 31):
                for cc in pair:
                    A("dve", lambda e, cc=cc, k=k: e.scalar_tensor_tensor(out=accv(cc), in0=ufwin(cc, k), scalar=CW.t[:, l, cc, k:k + 1],
                                                                          in1=accv(cc), op0=ALU.mult, op1=ALU.add),
                      r=[UFr[cc], CW.r, ACCr[cc]], w=[ACCr[cc]])
            for i_, cc in enumerate(pair):
                SQ_ = SQCs[i_]
                A("act", lambda e, cc=cc, SQ_=SQ_: e.activation(out=SQ_.t[:, TK], in_=ACC.t[:, cc, TK], func=AF.Square), r=[ACCr[cc]], w=[SQ_.r])
                mm(bS1.t[:, TK], ONF.t[:], ACC.t[:, cc, TK], cc == 0, cc == 7, [ONF.r, ACCr[cc]], [bS1.r])
                mm(bS2.t[:, TK], ONF.t[:], SQ_.t[:, TK], cc == 0, cc == 7, [ONF.r, SQ_.r], [bS2.r])
        A("dve", lambda e: e.tensor_scalar(out=MEAN.t[:, TK], in0=bS1.t[:, TK], scalar1=1.0 / 1024, scalar2=None, op0=ALU.mult),
          r=[bS1.r], w=[MEAN.r])
        A("dve", lambda e: e.tensor_tensor(out=SQC.t[:, TK], in0=MEAN.t[:, TK], in1=MEAN.t[:, TK], op=ALU.mult), r=[MEAN.r], w=[SQC.r])
        A("dve", lambda e: e.scalar_tensor_tensor(out=RSTDC.t[:, TK], in0=bS2.t[:, TK], scalar=1.0 / 1024, in1=SQC.t[:, TK],
                                                  op0=ALU.mult, op1=ALU.subtract), r=[bS2.r, SQC.r], w=[RSTDC.r])
        A("dve", lambda e: e.tensor_scalar(out=RSTDC.t[:, TK], in0=RSTDC.t[:, TK], scalar1=EPS, scalar2=None, op0=ALU.add),
          r=[RSTDC.r], w=[RSTDC.r])
        A("act", lambda e: e.activation(out=RSTDC.t[:, TK], in_=RSTDC.t[:, TK], func=AF.Sqrt), r=[RSTDC.r], w=[RSTDC.r])
        A("dve", lambda e: e.reciprocal(out=RSTDC.t[:, TK], in_=RSTDC.t[:, TK]), r=[RSTDC.r], w=[RSTDC.r])
        for cc in range(8):
            bz = projW(O_CZ + cc * 128, 128)
            A("act", lambda e, bz=bz, cc=cc: e.activation(out=SZC.t[:, TK], in_=bz.t[:, TK], func=AF.Silu,
                                                          bias=bias(l, O_CZ + cc * 128), scale=1.0), r=[bz.r, BIN.r], w=[SZC.r])
            A("dve", lambda e, cc=cc: e.tensor_tensor(out=ACC.t[:, cc, TK], in0=ACC.t[:, cc, TK], in1=MEAN.t[:, TK], op=ALU.subtract),
              r=[ACCr[cc], MEAN.r], w=[ACCr[cc]])
            A("dve", lambda e, cc=cc: e.tensor_tensor(out=ACC.t[:, cc, TK], in0=ACC.t[:, cc, TK], in1=RSTDC.t[:, TK], op=ALU.mult),
              r=[ACCr[cc], RSTDC.r], w=[ACCr[cc]])
            A("act", lambda e, cc=cc: e.activation(out=ACC.t[:, cc, TK], in_=ACC.t[:, cc, TK], func=AF.Silu,
                                                   bias=CLB.t[:, l, cc:cc + 1], scale=CLG.t[:, l, cc:cc + 1]),
              r=[ACCr[cc], CLB.r, CLG.r], w=[ACCr[cc]])
            A("dve", lambda e, cc=cc: e.tensor_tensor(out=CVG.t[:, cc, TK], in0=ACC.t[:, cc, TK], in1=SZC.t[:, TK], op=ALU.mult),
              r=[ACCr[cc], SZC.r], w=[CVG.r])
        if kind == "p":
            A("dve", lambda e: e.tensor_copy(out=CT[l].t[:], in_=UF.t[:, :, TT:TT + 30]), r=UFr, w=[CT[l].r])
            if last:
                for g in range(2):
                    b = bank()
                    for q in range(4):
                        cc = 4 * g + q
                        tr(b.t[0:30, q * 128:(q + 1) * 128], UF.t[:, cc, TT:TT + 30], IDF.t[:], [UFr[cc], IDF.r], [b.r])
                    A("act", lambda e, b=b, g=g: e.copy(out=CSO.t[0:30, g * 512:(g + 1) * 512], in_=b.t[0:30, :]), r=[b.r], w=[CSO.r])
                dma_out(cs_p[l], CSO.t[0:30, :], CSO.r)
        else:
            for s in range(4):
                for g in range(2):
                    b = bank()
                    for q in range(4):
                        cc = 4 * g + q
                        tr(b.t[0:30, q * 128:(q + 1) * 128], UFs[:, cc, s, 32:62], IDF.t[:], [UFr[cc], IDF.r], [b.r])
                    A("act", lambda e, b=b, g=g: e.copy(out=CSO.t[0:30, g * 512:(g + 1) * 512], in_=b.t[0:30, :]), r=[b.r], w=[CSO.r])
                dma_out(cs_s[l, s], CSO.t[0:30, :], CSO.r)

        areset()
        QT = aal("QT", [4, TT], BF16); KT = aal("KT", [4, TT], BF16); VT = aal("VT", [8, TT], BF16)
        OG = aal("OG", [8, TT], BF16)
        T1 = aal("T1", [TT]); T2 = aal("T2", [TT])
        GT = aal("GT", [TT], F32, parts=8)
        GTOK = aal("GTOK", [4, 8], F32, parts=64)
        SP_ = aal("SPt", [4, 4], F32, parts=64)
        NBT = aal("NBT", [4, 4], F32, parts=64)
        CTOK = aal("CTOK", [4, 4], F32, parts=64)
        WSK = aal("WSK", [4, 4], F32, parts=64); ETOK = aal("ETOK", [4, 4], F32, parts=64)
        CTR = aal("CTR", [TT], F32, parts=4)
        CMX = aal("CMX", [4], F32, parts=4); BL = aal("BL", [4], F32, parts=4)
        MPV = aal("MPV", [5], F32, parts=4); MCT = aal("MCT", [4], F32, parts=4); DA = aal("DA", [4], F32, parts=4)
        MNEW = aal("MNEW", [4], F32, parts=4)
        ZZ = aal("ZZ", [4, 4], F32, parts=4)
        DEC = aal("DEC", [16], F32)
        VS = aal("VS", [4, 257], BF16, parts=64); KTOK = aal("KTOK", [512], BF16, parts=64)
        QKM = aal("QKM", [4, 64], BF16, parts=64); HN = aal("HN", [4, 256], BF16, parts=64)
        CBF = aal("CBF", [4, 257], BF16)
        AD = aal("AD", [4], F32, parts=64); RD = aal("RD", [4], F32, parts=64); SS = aal("SS", [4], F32, parts=64)
        SC = aal("SC", [4], F32, parts=64); JK = aal("JK", [256], BF16, parts=64)
        HTMP = aal("HTMP", [8, 64], F32)
        CNS = aal("CNS", [4, 257], F32)
        for h in range(4):
            b = projW(O_MQ + h * 128, 128)
            A("act", lambda e, b=b, h=h: e.activation(out=QT.t[:, h, TK], in_=b.t[:, TK], func=AF.Identity,
                                                      bias=bias(l, O_MQ + h * 128), scale=1.0), r=[b.r, BIN.r], w=[QT.r])
            b = projW(O_MK + h * 128, 128)
            A("dve", lambda e, b=b, h=h: e.tensor_scalar(out=KT.t[:, h, TK], in0=b.t[:, TK], scalar1=bias(l, O_MK + h * 128),
                                                         scalar2=128.0 ** -0.5, op0=ALU.add, op1=ALU.mult), r=[b.r, BIN.r], w=[KT.r])
        for c in range(8):
            b = projW(O_MV + c * 128, 128)
            A("act", lambda e, b=b, c=c: e.activation(out=VT.t[:, c, TK], in_=b.t[:, TK], func=AF.Identity,
                                                      bias=bias(l, O_MV + c * 128), scale=1.0), r=[b.r, BIN.r], w=[VT.r])
            bo = projW(O_MO + c * 128, 128)
            bz = projW(O_MZ + c * 128, 128)
            A("act", lambda e, bo=bo, c=c: e.activation(out=T1.t[:, TK], in_=bo.t[:, TK], func=AF.Sigmoid,
                                                        bias=bias(l, O_MO + c * 128), scale=1.0), r=[bo.r, BIN.r], w=[T1.r])
            A("act", lambda e, bz=bz, c=c: e.activation(out=T2.t[:, TK], in_=bz.t[:, TK], func=AF.Silu,
                                                        bias=bias(l, O_MZ + c * 128), scale=1.0), r=[bz.r, BIN.r], w=[T2.r])
            A("dve", lambda e, c=c: e.tensor_tensor(out=OG.t[:, c, TK], in0=T1.t[:, TK], in1=T2.t[:, TK], op=ALU.mult),
              r=[T1.r, T2.r], w=[OG.r])
        b = projW(O_MIF, 8)
        A("act", lambda e, b=b: e.activation(out=GT.t[0:8, TK], in_=b.t[0:8, TK], func=AF.Identity, bias=bias(l, O_MIF, 8), scale=1.0),
          r=[b.r, BIN.r], w=[GT.r])
        b = bank()
        for j in range(nch):
            tr(b.t[0:L, j * 8:(j + 1) * 8], GT.t[0:8, j * L:(j + 1) * L], IDF.t[0:8, 0:8], [GT.r, IDF.r], [b.r])
        A("dve", lambda e, b=b: e.tensor_copy(out=GTOK.t[0:L], in_=b.t[0:L, 0:32].rearrange("p (j g) -> p j g", j=4)), r=[b.r], w=[GTOK.r])
        A("act", lambda e: e.activation(out=SP_.t[0:L], in_=GTOK.t[0:L, :, 4:8], func=AF.Exp, scale=-1.0), r=[GTOK.r], w=[SP_.r])
        A("act", lambda e: e.activation(out=SP_.t[0:L], in_=SP_.t[0:L], func=AF.Ln, bias=1.0, scale=1.0), r=[SP_.r], w=[SP_.r])
        b1 = bank()
        mm(b1.t[0:L, 0:16], UTF.t[0:L, 0:L], SP_.t[0:L].rearrange("p j h -> p (j h)"), True, True, [UTF.r, SP_.r], [b1.r])
        A("dve", lambda e: e.tensor_copy(out=NBT.t[0:L].rearrange("p j h -> p (j h)"), in_=b1.t[0:L, 0:16]), r=[b1.r], w=[NBT.r])
        A("dve", lambda e: e.tensor_tensor(out=CTOK.t[0:L], in0=GTOK.t[0:L, :, 0:4], in1=NBT.t[0:L], op=ALU.add),
          r=[GTOK.r, NBT.r], w=[CTOK.r])
        b2 = bank()
        for j in range(nch):
            mm(b2.t[0:4, j * L:(j + 1) * L], SP_.t[0:L, j, :], UTF.t[0:L, 0:L], True, True, [SP_.r, UTF.r], [b2.r])
        A("dve", lambda e: e.tensor_tensor(out=CTR.t[0:4, TK], in0=GT.t[0:4, TK], in1=b2.t[0:4, TK], op=ALU.add), r=[GT.r, b2.r], w=[CTR.r])
        A("dve", lambda e: e.tensor_reduce(out=CMX.t[0:4, :], in_=CTR.t[0:4, TK].rearrange("p (j t) -> p j t", j=4), axis=AX.X, op=ALU.max),
          r=[CTR.r], w=[CMX.r])
        A("dve", lambda e: e.tensor_scalar(out=BL.t[0:4, :], in0=b2.t[0:4, TK].rearrange("p (j t) -> p j t", j=4)[:, :, L - 1],
                                           scalar1=-1.0, scalar2=None, op0=ALU.mult), r=[b2.r], w=[BL.r])
        if kind == "p":
            A("dve", lambda e: e.tensor_copy(out=MPV.t[0:4, 0:1], in_=MRP[l].t[:]), r=[MRP[l].r], w=[MPV.r])
            for j in range(nch):
                A("dve", lambda e, j=j: e.tensor_tensor(out=MCT.t[0:4, j:j + 1], in0=MPV.t[0:4, j:j + 1], in1=CMX.t[0:4, j:j + 1], op=ALU.max),
                  r=[MPV.r, CMX.r], w=[MCT.r])
                A("dve", lambda e, j=j: e.tensor_tensor(out=MPV.t[0:4, j + 1:j + 2], in0=BL.t[0:4, j:j + 1], in1=MCT.t[0:4, j:j + 1], op=ALU.add),
                  r=[BL.r, MCT.r], w=[MPV.r])
            A("dve", lambda e: e.tensor_copy(out=MRP[l].t[:], in_=MPV.t[0:4, 4:5]), r=[MPV.r], w=[MRP[l].r])
            if last:
                dma_out(m_p[l].rearrange("(h o) -> h o", o=1), MRP[l].t[:], MRP[l].r, slow=True)
        else:
            dma_in(MPV.t[0:4, 0:4], stm[l].rearrange("s h -> h s"), MPV.r, slow=True)
            A("dve", lambda e: e.tensor_tensor(out=MCT.t[0:4, :], in0=MPV.t[0:4, 0:4], in1=CMX.t[0:4, :], op=ALU.max),
              r=[MPV.r, CMX.r], w=[MCT.r])
            A("dve", lambda e: e.tensor_tensor(out=MNEW.t[0:4, :], in0=BL.t[0:4, :], in1=MCT.t[0:4, :], op=ALU.add),
              r=[BL.r, MCT.r], w=[MNEW.r])
            dma_out(m_s[l].rearrange("s h -> h s"), MNEW.t[0:4, :], MNEW.r, slow=True)
        A("dve", lambda e: e.tensor_tensor(out=DA.t[0:4, :], in0=MPV.t[0:4, 0:4], in1=MCT.t[0:4, :], op=ALU.subtract),
          r=[MPV.r, MCT.r], w=[DA.r])
        A("act", lambda e: e.activation(out=DA.t[0:4, :], in_=DA.t[0:4, :], func=AF.Exp), r=[DA.r], w=[DA.r])

        def bcast_rows(src_tb, nparts):
            A("dve", lambda e: e.tensor_tensor(out=ZZ.t[0:4], in0=src_tb.t[0:4, :].unsqueeze(2).broadcast_to([4, 4, 4]),
                                               in1=EYE4.t[0:4, :].unsqueeze(1).broadcast_to([4, 4, 4]), op=ALU.mult),
              r=[src_tb.r, EYE4.r], w=[ZZ.r])
            bb = bank()
            mm(bb.t[0:nparts, 0:16], ONF.t[0:4, 0:nparts], ZZ.t[0:4].rearrange("p j h -> p (j h)"), True, True, [ONF.r, ZZ.r], [bb.r])
            return bb

        if kind == "s" and l == 0:
            dbg("GT", GT.t[0:8, TK], GT.r, [8, 128]); dbg("GTOK", GTOK.t[0:L], GTOK.r, [32, 4, 8]); dbg("SP", SP_.t[0:L], SP_.r, [32, 4, 4])
            dbg("NBT", NBT.t[0:L], NBT.r, [32, 4, 4]); dbg("CTR", CTR.t[0:4, TK], CTR.r, [4, 128]); dbg("CMX", CMX.t[0:4, :], CMX.r, [4, 4])
            dbg("BL", BL.t[0:4, :], BL.r, [4, 4]); dbg("MCT", MCT.t[0:4, :], MCT.r, [4, 4]); dbg("MPV", MPV.t[0:4, 0:4], MPV.r, [4, 4])
        bm = bcast_rows(MCT, 64)
        A("dve", lambda e: e.tensor_tensor(out=WSK.t[0:L].rearrange("p j h -> p (j h)"), in0=CTOK.t[0:L].rearrange("p j h -> p (j h)"),
                                           in1=bm.t[0:L, 0:16], op=ALU.subtract), r=[CTOK.r, bm.r], w=[WSK.r])
        A("act", lambda e: e.activation(out=WSK.t[0:L], in_=WSK.t[0:L], func=AF.Exp), r=[WSK.r], w=[WSK.r])
        A("dve", lambda e: e.tensor_tensor(out=ETOK.t[0:L].rearrange("p j h -> p (j h)"), in0=NBT.t[0:L].rearrange("p j h -> p (j h)"),
                                           in1=bm.t[0:L, 0:16], op=ALU.subtract), r=[NBT.r, bm.r], w=[ETOK.r])
        A("act", lambda e: e.activation(out=ETOK.t[0:L], in_=ETOK.t[0:L], func=AF.Exp), r=[ETOK.r], w=[ETOK.r])
        bd = bcast_rows(DA, 128)
        A("dve", lambda e: e.tensor_copy(out=DEC.t[:, :], in_=bd.t[:, 0:16]), r=[bd.r], w=[DEC.r])

        for j in range(nch):
            cs_ = slice(j * L, (j + 1) * L)
            if kind == "p":
                S = CNP[l]
            else:
                S = CNS
                dma_in(S.t[:, :, 0:256], stC[l, j].rearrange("h k v -> k h v"), S.r)
                dma_in(S.t[:, :, 256], stn[l, j].rearrange("h k -> k h"), S.r, slow=True)
            A("dve", lambda e, S=S, j=j: e.tensor_tensor(out=S.t[:], in0=S.t[:],
                                                         in1=DEC.t[:, j * 4:(j + 1) * 4].unsqueeze(2).broadcast_to([128, 4, 257]), op=ALU.mult),
              r=[S.r, DEC.r], w=[S.r])
            A("act", lambda e, S=S: e.copy(out=CBF.t[:], in_=S.t[:]), r=[S.r], w=[CBF.r])
            bq = bank()
            for h in range(4):
                mm(bq.t[0:L, h * L:(h + 1) * L], KT.t[:, h, cs_], QT.t[:, h, cs_], True, True, [KT.r, QT.r], [bq.r])
            A("dve", lambda e, bq=bq: e.tensor_tensor(out=QKM.t[0:L, :, 0:L], in0=bq.t[0:L, 0:4 * L].rearrange("p (h t) -> p h t", h=4),
                                                      in1=UTF.t[0:L, 0:L].unsqueeze(1).broadcast_to([L, 4, L]), op=ALU.mult),
              r=[bq.r, UTF.r], w=[QKM.r])
            bv = bank()
            for c in range(8):
                tr(bfv(bv)[0:L, c * 128:(c + 1) * 128], VT.t[:, c, cs_], IDB.t[:], [VT.r, IDB.r], [bv.r])
            A("dve", lambda e, bv=bv, j=j: e.tensor_tensor(out=VS.t[0:L, :, 0:256], in0=bfv(bv)[0:L, 0:1024].rearrange("p (h v) -> p h v", h=4),
                                                           in1=WSK.t[0:L, j, :].unsqueeze(2).broadcast_to([L, 4, 256]), op=ALU.mult),
              r=[bv.r, WSK.r], w=[VS.r])
            A("dve", lambda e, j=j: e.tensor_copy(out=VS.t[0:L, :, 256], in_=WSK.t[0:L, j, :]), r=[WSK.r], w=[VS.r])
            bk = bank()
            for h in range(4):
                tr(bfv(bk)[0:L, h * 128:(h + 1) * 128], KT.t[:, h, cs_], IDB.t[:], [KT.r, IDB.r], [bk.r])
            A("act", lambda e, bk=bk: e.copy(out=KTOK.t[0:L, :], in_=bfv(bk)[0:L, 0:512]), r=[bk.r], w=[KTOK.r])
            bn = [bank(), bank()]
            bden = bank()
            for h in range(4):
                o_ = bn[h // 2].t[0:L, (h % 2) * 256:(h % 2) * 256 + 256]
                mm(o_, QKM.t[0:L, h, 0:L], VS.t[0:L, h, 0:256], True, False, [QKM.r, VS.r], [bn[h // 2].r])
                mm(o_, QT.t[:, h, cs_], CBF.t[:, h, 0:256], False, True, [QT.r, CBF.r], [bn[h // 2].r])
                mm(bden.t[0:L, h:h + 1], QKM.t[0:L, h, 0:L], VS.t[0:L, h, 256:257], True, False, [QKM.r, VS.r], [bden.r])
                mm(bden.t[0:L, h:h + 1], QT.t[:, h, cs_], CBF.t[:, h, 256:257], False, True, [QT.r, CBF.r], [bden.r])
            A("act", lambda e, bden=bden: e.activation(out=AD.t[0:L, :], in_=bden.t[0:L, 0:4], func=AF.Abs), r=[bden.r], w=[AD.r])
            A("dve", lambda e, j=j: e.tensor_tensor(out=AD.t[0:L, :], in0=AD.t[0:L, :], in1=ETOK.t[0:L, j, :], op=ALU.max),
              r=[AD.r, ETOK.r], w=[AD.r])
            A("dve", lambda e: e.reciprocal(out=RD.t[0:L, :], in_=AD.t[0:L, :]), r=[AD.r], w=[RD.r])
            for h in range(4):
                A("act", lambda e, h=h, bn=bn: e.activation(out=JK.t[0:L, :], in_=bn[h // 2].t[0:L, (h % 2) * 256:(h % 2) * 256 + 256],
                                                            func=AF.Square, scale=RD.t[0:L, h:h + 1], accum_out=SS.t[0:L, h:h + 1]),
                  r=[bn[h // 2].r, RD.r], w=[JK.r, SS.r])
            A("dve", lambda e: e.tensor_scalar(out=SS.t[0:L, :], in0=SS.t[0:L, :], scalar1=1.0 / 256, scalar2=EPS, op0=ALU.mult, op1=ALU.add),
              r=[SS.r], w=[SS.r])
            A("act", lambda e: e.activation(out=SS.t[0:L, :], in_=SS.t[0:L, :], func=AF.Sqrt), r=[SS.r], w=[SS.r])
            A("dve", lambda e: e.reciprocal(out=SS.t[0:L, :], in_=SS.t[0:L, :]), r=[SS.r], w=[SS.r])
            A("dve", lambda e: e.tensor_tensor(out=SC.t[0:L, :], in0=SS.t[0:L, :], in1=RD.t[0:L, :], op=ALU.mult), r=[SS.r, RD.r], w=[SC.r])
            for g in range(2):
                A("dve", lambda e, g=g, bn=bn: e.tensor_tensor(out=HN.t[0:L, 2 * g:2 * g + 2, :],
                                                               in0=bn[g].t[0:L, 0:512].rearrange("p (h v) -> p h v", h=2),
                                                               in1=SC.t[0:L, 2 * g:2 * g + 2].unsqueeze(2).broadcast_to([L, 2, 256]), op=ALU.mult),
                  r=[bn[g].r, SC.r], w=[HN.r])
            bt_ = bank()
            HNf = HN.t[0:L].rearrange("p h v -> p (h v)")
            for c in range(8):
                tr(bfv(bt_)[:, c * L:(c + 1) * L], HNf[:, c * 128:(c + 1) * 128], IDB.t[0:L, 0:L], [HN.r, IDB.r], [bt_.r])
            A("dve", lambda e, bt_=bt_: e.tensor_tensor(out=HTMP.t[:, :, 0:L], in0=bfv(bt_)[:, 0:8 * L].rearrange("p (c t) -> p c t", c=8),
                                                        in1=MNG.t[:, l, :].unsqueeze(2).broadcast_to([128, 8, L]), op=ALU.mult),
              r=[bt_.r, MNG.r], w=[HTMP.r])
            A("dve", lambda e, cs_=cs_: e.tensor_tensor(out=HMG.t[:, :, cs_], in0=HTMP.t[:, :, 0:L], in1=OG.t[:, :, cs_], op=ALU.mult),
              r=[HTMP.r, OG.r], w=[HMG.r])
            bs = [bank(), bank()]
            bsn = bank()
            for h in range(4):
                mm(bs[h // 2].t[:, (h % 2) * 256:(h % 2) * 256 + 256], KTOK.t[0:L, h * 128:(h + 1) * 128], VS.t[0:L, h, 0:256], True, True,
                   [KTOK.r, VS.r], [bs[h // 2].r])
                mm(bsn.t[:, h:h + 1], KTOK.t[0:L, h * 128:(h + 1) * 128], VS.t[0:L, h, 256:257], True, True, [KTOK.r, VS.r], [bsn.r])
            for g in range(2):
                A("dve", lambda e, g=g, bs=bs, S=S: e.tensor_tensor(out=S.t[:, 2 * g:2 * g + 2, 0:256], in0=S.t[:, 2 * g:2 * g + 2, 0:256],
                                                                    in1=bs[g].t[:, 0:512].rearrange("p (h v) -> p h v", h=2), op=ALU.add),
                  r=[S.r, bs[g].r], w=[S.r])
            A("dve", lambda e, bsn=bsn, S=S: e.tensor_tensor(out=S.t[:, :, 256], in0=S.t[:, :, 256], in1=bsn.t[:, 0:4], op=ALU.add),
              r=[S.r, bsn.r], w=[S.r])
            if kind == "s":
                dma_out(C_s[l, j].rearrange("h k v -> k h v"), S.t[:, :, 0:256], S.r)
                dma_out(n_s[l, j].rearrange("h k -> k h"), S.t[:, :, 256], S.r, slow=True)
        if kind == "p" and last:
            dma_out(C_p[l].rearrange("h k v -> k h v"), CNP[l].t[:, :, 0:256], CNP[l].r)
            dma_out(n_p[l].rearrange("h k -> k h"), CNP[l].t[:, :, 256], CNP[l].r, slow=True)

        areset()
        ACQ = aal("ACQ", [8, TT], BF16)
        CKV32 = aal("CKV32", [4, TT]); CKVB = aal("CKVB", [4, TT], BF16)
        KR32 = aal("KR32", [TT], F32, parts=64); KRB = aal("KRB", [TT], BF16, parts=64)
        SQ = aal("SQ", [512], BF16); SQF = aal("SQF", [TT], F32, parts=64)
        R1 = aal("R1", [512]); R2 = aal("R2", [TT], F32, parts=64)
        TA = aal("TA", [TT], F32, parts=64); TBb = aal("TBb", [TT], F32, parts=64)
        LATO = aal("LATO", [512]); KRO = aal("KRO", [64])
        QR32 = aal("QR32", [TT], F32, parts=64); QRNB = aal("QRNB", [TT], BF16, parts=64)
        PTs = [aal(f"PT{i}", [TT], BF16) for i in range(3)]
        RDEN = aal("RDEN", [TT]); TO = aal("TO", [TT])
        KRP = aal("KRP", [SEQ], BF16, parts=64)
        if kind == "s":
            LPB = aal("LPB", [8, 512], BF16); LATT = aal("LATT", [4, PAST], BF16); KRPB = aal("KRPB", [8, 64], BF16)
        if kind == "p":
            dma_in(CS.t[:, :, :], c_csp[:, :, pos0:pos0 + TT], CS.r)
        else:
            dma_in(CS.t[:, :, 0:128], c_css[:, :, :], CS.r)
        COS = CS.t[:, 0, TK]; SIN = CS.t[:, 1, TK]

        def rms_bcast(src_bank, nparts, n, gain_ap, dst, dst_r, ncols=TK, fp32sq=False):
            sq = SQF if fp32sq else SQ
            A("act", lambda e: e.activation(out=sq.t[0:nparts, ncols], in_=src_bank.t[0:nparts, ncols], func=AF.Square), r=[src_bank.r], w=[sq.r])
            b2_ = bank()
            on = ONF if fp32sq else ONB
            mm(b2_.t[0:nparts, ncols], on.t[0:nparts, 0:nparts], sq.t[0:nparts, ncols], True, True, [on.r, sq.r], [b2_.r])
            rr = R2 if nparts == 64 else R1
            A("dve", lambda e: e.tensor_scalar(out=rr.t[0:nparts, ncols], in0=b2_.t[0:nparts, ncols], scalar1=1.0 / n, scalar2=EPS,
                                               op0=ALU.mult, op1=ALU.add), r=[b2_.r], w=[rr.r])
            A("act", lambda e: e.activation(out=rr.t[0:nparts, ncols], in_=rr.t[0:nparts, ncols], func=AF.Sqrt), r=[rr.r], w=[rr.r])
            A("dve", lambda e: e.reciprocal(out=rr.t[0:nparts, ncols], in_=rr.t[0:nparts, ncols]), r=[rr.r], w=[rr.r])
            A("dve", lambda e: e.scalar_tensor_tensor(out=dst, in0=src_bank.t[0:nparts, ncols], scalar=gain_ap, in1=rr.t[0:nparts, ncols],
                                                      op0=ALU.mult, op1=ALU.mult), r=[src_bank.r, rr.r], w=[dst_r])

        bR = abank(0)
        for c in range(8):
            b = projW(O_ACQ + c * 128, 128)
            A("act", lambda e, b=b, c=c: e.activation(out=ACQ.t[:, c, TK], in_=b.t[:, TK], func=AF.Identity,
                                                      bias=bias(l, O_ACQ + c * 128), scale=1.0), r=[b.r, BIN.r], w=[ACQ.r])
            A("act", lambda e, b=b, c=c: e.activation(out=SQ.t[:, TK], in_=b.t[:, TK], func=AF.Square,
                                                      bias=bias(l, O_ACQ + c * 128), scale=1.0), r=[b.r, BIN.r], w=[SQ.r])
            mm(bR.t[:, TK], ONB.t[:], SQ.t[:, TK], c == 0, c == 7, [ONB.r, SQ.r], [bR.r])
        rstd_from(TB(R1.t[:, TK], R1.r), bR.t[:, TK], 1.0 / 1024, bR.r)
        for c in range(8):
            A("dve", lambda e, c=c: e.scalar_tensor_tensor(out=ACQ.t[:, c, TK], in0=ACQ.t[:, c, TK], scalar=CQG.t[:, l, c:c + 1],
                                                           in1=R1.t[:, TK], op0=ALU.mult, op1=ALU.mult), r=[ACQ.r, R1.r, CQG.r], w=[ACQ.r])
        bR = abank(1)
        for c in range(4):
            b = projW(O_ACKV + c * 128, 128)
            A("act", lambda e, b=b, c=c: e.activation(out=CKV32.t[:, c, TK], in_=b.t[:, TK], func=AF.Identity,
                                                      bias=bias(l, O_ACKV + c * 128), scale=1.0), r=[b.r, BIN.r], w=[CKV32.r])
            A("act", lambda e, c=c: e.activation(out=SQ.t[:, TK], in_=CKV32.t[:, c, TK], func=AF.Square), r=[CKV32.r], w=[SQ.r])
            mm(bR.t[:, TK], ONB.t[:], SQ.t[:, TK], c == 0, c == 3, [ONB.r, SQ.r], [bR.r])
        rstd_from(TB(R1.t[:, TK], R1.r), bR.t[:, TK], 1.0 / 512, bR.r)
        for c in range(4):
            A("dve", lambda e, c=c: e.scalar_tensor_tensor(out=CKV32.t[:, c, TK], in0=CKV32.t[:, c, TK], scalar=CKVG.t[:, l, c:c + 1],
                                                           in1=R1.t[:, TK], op0=ALU.mult, op1=ALU.mult), r=[CKV32.r, R1.r, CKVG.r], w=[CKV32.r])
        A("act", lambda e: e.copy(out=CKVB.t[:, :, TK], in_=CKV32.t[:, :, TK]), r=[CKV32.r], w=[CKVB.r])
        for j in range(nsub):
            b = bank()
            for c in range(4):
                tr(b.t[:, c * 128:(c + 1) * 128], CKV32.t[:, c, j * 128:(j + 1) * 128], IDF.t[:], [CKV32.r, IDF.r], [b.r])
            A("act", lambda e, b=b: e.copy(out=LATO.t[:, :], in_=b.t[:, :]), r=[b.r], w=[LATO.r])
            dst = lat_p[l, pos0 + j * 128:pos0 + (j + 1) * 128, :] if kind == "p" else lat_s[l]
            dma_out(dst, LATO.t[:, :], LATO.r)
        b = projW(O_AKR, 64)
        A("act", lambda e, b=b: e.activation(out=KR32.t[0:64, TK], in_=b.t[0:64, TK], func=AF.Identity, bias=bias(l, O_AKR, 64), scale=1.0),
          r=[b.r, BIN.r], w=[KR32.r])
        A("act", lambda e: e.activation(out=SQF.t[0:64, TK], in_=KR32.t[0:64, TK], func=AF.Square), r=[KR32.r], w=[SQF.r])
        b2 = bank()
        mm(b2.t[0:64, TK], ONF.t[0:64, 0:64], SQF.t[0:64, TK], True, True, [ONF.r, SQF.r], [b2.r])
        rstd_from(TB(R2.t[0:64, TK], R2.r), b2.t[0:64, TK], 1.0 / 64, b2.r)
        A("dve", lambda e: e.scalar_tensor_tensor(out=KR32.t[0:64, TK], in0=KR32.t[0:64, TK], scalar=KRG.t[:, l:l + 1], in1=R2.t[0:64, TK],
                                                  op0=ALU.mult, op1=ALU.mult), r=[KR32.r, R2.r, KRG.r], w=[KR32.r])
        b3 = bank()
        mm(b3.t[0:64, TK], ROTF.t[:], KR32.t[0:64, TK], True, True, [ROTF.r, KR32.r], [b3.r])
        A("dve", lambda e: e.tensor_tensor(out=TA.t[0:64, TK], in0=KR32.t[0:64, TK], in1=COS, op=ALU.mult), r=[KR32.r, CS.r], w=[TA.r])
        A("dve", lambda e: e.tensor_tensor(out=TBb.t[0:64, TK], in0=b3.t[0:64, TK], in1=SIN, op=ALU.mult), r=[b3.r, CS.r], w=[TBb.r])
        A("dve", lambda e: e.tensor_tensor(out=KR32.t[0:64, TK], in0=TA.t[0:64, TK], in1=TBb.t[0:64, TK], op=ALU.add), r=[TA.r, TBb.r], w=[KR32.r])
        A("act", lambda e: e.copy(out=KRB.t[0:64, TK], in_=KR32.t[0:64, TK]), r=[KR32.r], w=[KRB.r])
        for j in range(nsub):
            b = bank()
            tr(b.t[:, 0:64], KR32.t[0:64, j * 128:(j + 1) * 128], IDF.t[0:64, 0:64], [KR32.r, IDF.r], [b.r])
            A("act", lambda e, b=b: e.copy(out=KRO.t[:, :], in_=b.t[:, 0:64]), r=[b.r], w=[KRO.r])
            dst = kr_p[l, pos0 + j * 128:pos0 + (j + 1) * 128, :] if kind == "p" else kr_s[l]
            dma_out(dst, KRO.t[:, :], KRO.r)
        RKR = krc_res[l]
        if kind == "p":
            if not last:
                dma_out(KRC[l][:, pos0:pos0 + TT], KRB.t[0:64, TK], KRB.r, final=False, extra_w=[RKR])
            if ti > 0:
                P.op("sp", lambda e: e.dma_start(out=KRP.t[0:64, 0:pos0], in_=KRC[l][:, 0:pos0]), r=[RKR], w=[KRP.r], dma=True, semres=KRP.r)

        SETS = []
        for i_ in range(2):
            SETS.append(dict(QNB=aal(f"QNB{i_}", [TT], BF16), QRB=aal(f"QRB{i_}", [TT], BF16, parts=64), KNB=aal(f"KNB{i_}", [TT], BF16),
                             VB=aal(f"VB{i_}", [4, 128], BF16), SZ=aal(f"SZ{i_}", [TT]),
                             KP=aal(f"KP{i_}", [SEQ if kind == "p" else PAST], BF16),
                             VP=aal(f"VP{i_}", [16 if kind == "p" else 8, 128], BF16)))
        SQ2 = aal("SQ2", [512], BF16); R1b = aal("R1b", [512])

        def rms2(src_bank, nparts, n, gain_ap, dst, dst_r, ncols, sq, rr, on):
            A("act", lambda e: e.activation(out=sq.t[0:nparts, ncols], in_=src_bank.t[0:nparts, ncols], func=AF.Square), r=[src_bank.r], w=[sq.r])
            b2_ = bank()
            mm(b2_.t[0:nparts, ncols], on.t[0:nparts, 0:nparts], sq.t[0:nparts, ncols], True, True, [on.r, sq.r], [b2_.r])
            A("dve", lambda e: e.tensor_scalar(out=rr.t[0:nparts, ncols], in0=b2_.t[0:nparts, ncols], scalar1=1.0 / n, scalar2=EPS,
                                               op0=ALU.mult, op1=ALU.add), r=[b2_.r], w=[rr.r])
            A("act", lambda e: e.activation(out=rr.t[0:nparts, ncols], in_=rr.t[0:nparts, ncols], func=AF.Sqrt), r=[rr.r], w=[rr.r])
            A("dve", lambda e: e.reciprocal(out=rr.t[0:nparts, ncols], in_=rr.t[0:nparts, ncols]), r=[rr.r], w=[rr.r])
            A("dve", lambda e: e.scalar_tensor_tensor(out=dst, in0=src_bank.t[0:nparts, ncols], scalar=gain_ap, in1=rr.t[0:nparts, ncols],
                                                      op0=ALU.mult, op1=ALU.mult), r=[src_bank.r, rr.r], w=[dst_r])

        def stageA(h, cols, S):
            QNB_, QRB_, KNB_, VB_, SZ_, KP_, VP_ = S["QNB"], S["QRB"], S["KNB"], S["VB"], S["SZ"], S["KP"], S["VP"]
            uq, uqr = wload(w_uq[l][:, h * 192:(h + 1) * 192], 8, 192, ("w_uq", l, h))
            ukv, ukvr = wload(w_ukv[l][:, h * 256:(h + 1) * 256], 4, 256, ("w_ukv", l, h))
            COSc = CS.t[:, 0, cols]; SINc = CS.t[:, 1, cols]
            blocks = []
            if kind == "p" and ti > 0:
                RC = kvc_res[l][h]
                P.op("sp", lambda e: e.dma_start(out=KP_.t[:, 0:pos0], in_=KC[l, h][:, 0:pos0]), r=[RC], w=[KP_.r], dma=True, semres=KP_.r)
                P.op("sp", lambda e: e.dma_start(out=VP_.t[:, 0:pos0 // 128, :],
                                                 in_=VC[l, h][0:pos0, :].rearrange("(j p) v -> p j v", p=128)),
                     r=[RC], w=[VP_.r], dma=True, semres=VP_.r)
            bqn = bank(pin=True)
            for kc in range(8):
                mm(bqn.t[:, cols], uq[:, kc, 0:128], ACQ.t[:, kc, cols], kc == 0, kc == 7, [uqr, ACQ.r], [bqn.r])
            bqr = bank(pin=True)
            for kc in range(8):
                mm(bqr.t[0:64, cols], uq[:, kc, 128:192], ACQ.t[:, kc, cols], kc == 0, kc == 7, [uqr, ACQ.r], [bqr.r])
            bkn = bank(pin=True)
            for kc in range(4):
                mm(bkn.t[:, cols], ukv[:, kc, 0:128], CKVB.t[:, kc, cols], kc == 0, kc == 3, [ukvr, CKVB.r], [bkn.r])
            bv = bank()
            if kind == "p":
                for j in range(nsub):
                    for kc in range(4):
                        mm(bv.t[:, j * 128:(j + 1) * 128], CKVB.t[:, kc, j * 128:(j + 1) * 128], ukv[:, kc, 128:256], kc == 0, kc == 3,
                           [ukvr, CKVB.r], [bv.r])
                A("act", lambda e: e.copy(out=VB_.t[:, 0:nsub, :], in_=bv.t[:, 0:nsub * 128].rearrange("p (j v) -> p j v", j=nsub)),
                  r=[bv.r], w=[VB_.r])
            else:
                for kc in range(4):
                    mm(bv.t[0:32, 0:128], CKVB.t[:, kc, cols], ukv[:, kc, 128:256], kc == 0, kc == 3, [ukvr, CKVB.r], [bv.r])
                A("act", lambda e: e.copy(out=VB_.t[0:32, 0, :], in_=bv.t[0:32, 0:128]), r=[bv.r], w=[VB_.r])
            bz = projW(O_AZ + h * 128, 128)
            A("act", lambda e: e.activation(out=SZ_.t[:, TK], in_=bz.t[:, TK], func=AF.Silu, bias=bias(l, O_AZ + h * 128), scale=1.0),
              r=[bz.r, BIN.r], w=[SZ_.r])
            rms2(bqn, 128, 128, QNG.t[:, l:l + 1], QNB_.t[:, cols], QNB_.r, cols, SQ, R1, ONB)
            unpin(bqn)
            rms2(bkn, 128, 128, KNG.t[:, l:l + 1], KNB_.t[:, cols], KNB_.r, cols, SQ2, R1b, ONB)
            unpin(bkn)
            rms2(bqr, 64, 64, QRG.t[:, l:l + 1], QR32.t[0:64, cols], QR32.r, cols, SQF, R2, ONF)
            unpin(bqr)
            A("act", lambda e: e.copy(out=QRNB.t[0:64, cols], in_=QR32.t[0:64, cols]), r=[QR32.r], w=[QRNB.r])
            return (h, cols, S, uq, uqr, ukv, ukvr)

        def stageA3(h, cols, S, uq, uqr, ukv, ukvr):
            QNB_, QRB_, KNB_, VB_, SZ_, KP_, VP_ = S["QNB"], S["QRB"], S["KNB"], S["VB"], S["SZ"], S["KP"], S["VP"]
            COSc = CS.t[:, 0, cols]; SINc = CS.t[:, 1, cols]
            blocks = []
            b3 = bank()
            mm(b3.t[0:64, cols], ROTB.t[:], QRNB.t[0:64, cols], True, True, [ROTB.r, QRNB.r], [b3.r])
            A("dve", lambda e: e.tensor_tensor(out=TA.t[0:64, cols], in0=QR32.t[0:64, cols], in1=COSc, op=ALU.mult), r=[QR32.r, CS.r], w=[TA.r])
            A("dve", lambda e: e.tensor_tensor(out=TBb.t[0:64, cols], in0=b3.t[0:64, cols], in1=SINc, op=ALU.mult), r=[b3.r, CS.r], w=[TBb.r])
            A("dve", lambda e: e.tensor_tensor(out=QRB_.t[0:64, cols], in0=TA.t[0:64, cols], in1=TBb.t[0:64, cols], op=ALU.add),
              r=[TA.r, TBb.r], w=[QRB_.r])
            if kind == "p":
                RC = kvc_res[l][h]
                if not last:
                    dma_out(KC[l, h][:, pos0:pos0 + TT], KNB_.t[:, TK], KNB_.r, final=False, extra_w=[RC])
                    dma_out(VC[l, h][pos0:pos0 + TT, :].rearrange("(j p) v -> p j v", p=128), VB_.t[:, 0:nsub, :], VB_.r, final=False, extra_w=[RC])
                for pb in range(pos0 // 128):
                    blocks.append((KP_.t[:, pb * 128:(pb + 1) * 128], KRP.t[0:64, pb * 128:(pb + 1) * 128], VP_.t[:, pb, :], 128, 0, None,
                                   [KP_.r, KRP.r, VP_.r]))
                for kb in range(nsub):
                    blocks.append((KNB_.t[:, kb * 128:(kb + 1) * 128], KRB.t[0:64, kb * 128:(kb + 1) * 128], VB_.t[:, kb, :], 128, kb * 128,
                                   AMB.t[:, kb, kb * 128:TT], [KNB_.r, KRB.r, VB_.r]))
            else:
                for half in range(2):
                    b = bank(pin=True)
                    for kc in range(4):
                        mm(b.t[:, :], ukv[:, kc, 0:128], LATT.t[:, kc, half * 512:(half + 1) * 512], kc == 0, kc == 3, [ukvr, LATT.r], [b.r])
                    rms2(b, 128, 128, KNG.t[:, l:l + 1], KP_.t[:, half * 512:(half + 1) * 512], KP_.r, slice(0, 512),
                         SQ if half == 0 else SQ2, R1 if half == 0 else R1b, ONB)
                    unpin(b)
                for g in range(2):
                    b = bank()
                    for q in range(4):
                        kb = 4 * g + q
                        for kc in range(4):
                            mm(b.t[:, q * 128:(q + 1) * 128], LATT.t[:, kc, kb * 128:(kb + 1) * 128], ukv[:, kc, 128:256], kc == 0, kc == 3,
                               [ukvr, LATT.r], [b.r])
                    A("act", lambda e, b=b, g=g: e.copy(out=VP_.t[:, 4 * g:4 * g + 4, :], in_=b.t[:, :].rearrange("p (q v) -> p q v", q=4)),
                      r=[b.r], w=[VP_.r])
                for pb in range(8):
                    blocks.append((KP_.t[:, pb * 128:(pb + 1) * 128], KRP.t[0:64, pb * 128:(pb + 1) * 128], VP_.t[:, pb, :], 128, 0, None,
                                   [KP_.r, KRP.r, VP_.r]))
                blocks.append((KNB_.t[:, cols], KRB.t[0:64, cols], VB_.t[0:32, 0, :], 32, 0, None, [KNB_.r, KRB.r, VB_.r]))
            return blocks

        def stageB(h, blocks, cols, S):
            QNB_, QRB_, SZ_ = S["QNB"], S["QRB"], S["SZ"]
            qs = cols.start
            nq = cols.stop - cols.start
            bO = abank(0); bD = abank(1)
            nb = len(blocks)
            pts = {}

            def s1(i):
                (kap, krap, vap, nk, q0, mk, rds) = blocks[i]
                qsl = slice(qs + q0, qs + nq)
                osl = slice(q0, nq)
                bS = bank()
                mm(bS.t[0:nk, osl], kap, QNB_.t[:, qsl], True, False, rds + [QNB_.r], [bS.r])
                mm(bS.t[0:nk, osl], krap, QRB_.t[0:64, qsl], False, True, rds + [QRB_.r], [bS.r])
                PT = PTs[i % 3]
                A("act", lambda e: e.activation(out=PT.t[0:nk, osl], in_=bS.t[0:nk, osl], func=AF.Exp, scale=ATTN_SCALE), r=[bS.r], w=[PT.r])
                if mk is not None:
                    A("dve", lambda e: e.tensor_tensor(out=PT.t[0:nk, osl], in0=PT.t[0:nk, osl], in1=mk, op=ALU.mult), r=[PT.r, AMB.r], w=[PT.r])
                pts[i] = PT

            def s2(i):
                (kap, krap, vap, nk, q0, mk, rds) = blocks[i]
                osl = slice(q0, nq)
                PT = pts[i]
                mm(bO.t[:, osl], vap, PT.t[0:nk, osl], i == 0, i == nb - 1, rds + [PT.r], [bO.r])
                mm(bD.t[:, osl], ONB.t[0:nk, :], PT.t[0:nk, osl], i == 0, i == nb - 1, [ONB.r, PT.r], [bD.r])

            s1(0)
            for i in range(nb):
                if i + 1 < nb:
                    s1(i + 1)
                s2(i)
            osl = slice(0, nq)
            A("dve", lambda e: e.reciprocal(out=RDEN.t[:, osl], in_=bD.t[:, osl]), r=[bD.r], w=[RDEN.r])
            A("dve", lambda e: e.tensor_tensor(out=TO.t[:, osl], in0=bO.t[:, osl], in1=RDEN.t[:, osl], op=ALU.mult), r=[bO.r, RDEN.r], w=[TO.r])
            A("dve", lambda e: e.tensor_tensor(out=AOG.t[:, h, qs:qs + nq], in0=TO.t[:, osl], in1=SZ_.t[:, qs:qs + nq], op=ALU.mult),
              r=[TO.r, SZ_.r], w=[AOG.r])

        def run_heads(cols):
            prev = None
            for h in range(16):
                S = SETS[h % 2]
                st_ = stageA(h, cols, S)
                if prev is not None:
                    stageB(*prev)
                blocks = stageA3(*st_)
                prev = (h, blocks, cols, S)
            stageB(*prev)

        if kind == "p":
            run_heads(TK)
        else:
            for s in range(4):
                P.op("pool", lambda e, s=s: e.dma_start(out=LPB.t[:, :, :], in_=latp_d[l, s].rearrange("(b p) f -> p b f", p=128)),
                     w=[LPB.r], dma=True)
                P.op("pool", lambda e, s=s: e.dma_start(out=KRPB.t[:, :, :], in_=krp_d[l, s].rearrange("(b p) f -> p b f", p=128)),
                     w=[KRPB.r], dma=True)
                for c in range(4):
                    b = bank()
                    for kb in range(8):
                        tr(bfv(b)[:, kb * 128:(kb + 1) * 128], LPB.t[:, kb, c * 128:(c + 1) * 128], IDB.t[:], [LPB.r, IDB.r], [b.r])
                    A("act", lambda e, b=b, c=c: e.copy(out=LATT.t[:, c, :], in_=bfv(b)[:, 0:1024]), r=[b.r], w=[LATT.r])
                b = bank()
                for kb in range(8):
                    tr(bfv(b)[0:64, kb * 128:(kb + 1) * 128], KRPB.t[:, kb, :], IDB.t[:], [KRPB.r, IDB.r], [b.r])
                A("act", lambda e, b=b: e.copy(out=KRP.t[0:64, 0:PAST], in_=bfv(b)[0:64, 0:1024]), r=[b.r], w=[KRP.r])
                run_heads(slice(s * 32, (s + 1) * 32))

        areset()
        MT = aal("MT", [32, TT], BF16)
        SG = [aal(f"SG{i}", [TT]) for i in range(3)]
        TM = aal("TM", [TT]); TM2 = aal("TM2", [TT])
        for dc in range(32):
            bg = []
            for br in range(3):
                b = projW(O_G + br * D + dc * 128, 128)
                A("act", lambda e, b=b, br=br, dc=dc: e.activation(out=SG[br].t[:, TK], in_=b.t[:, TK], func=AF.Sigmoid,
                                                                   bias=bias(l, O_G + br * D + dc * 128), scale=1.0),
                  r=[b.r, BIN.r], w=[SG[br].r])
            ys = []
            for (wi, (wsrc, nk, act)) in enumerate(((w_pc, 8, CVG), (w_pm, 8, HMG), (w_pa, 16, AOG))):
                wv, wr = wload(wsrc[l][:, dc * 128:(dc + 1) * 128], nk, 128, ("w_p", wi, l, dc))
                b = bank()
                for kc in range(nk):
                    mm(b.t[:, TK], wv[:, kc, :], act.t[:, kc, TK], kc == 0, kc == nk - 1, [wr, act.r], [b.r])
                ys.append(b)
            A("dve", lambda e, ys=ys: e.tensor_tensor(out=TM.t[:, TK], in0=ys[0].t[:, TK], in1=SG[0].t[:, TK], op=ALU.mult), r=[ys[0].r, SG[0].r], w=[TM.r])
            A("dve", lambda e, ys=ys: e.tensor_tensor(out=TM2.t[:, TK], in0=ys[1].t[:, TK], in1=SG[1].t[:, TK], op=ALU.mult), r=[ys[1].r, SG[1].r], w=[TM2.r])
            A("dve", lambda e: e.tensor_tensor(out=TM.t[:, TK], in0=TM.t[:, TK], in1=TM2.t[:, TK], op=ALU.add), r=[TM.r, TM2.r], w=[TM.r])
            A("dve", lambda e, ys=ys: e.tensor_tensor(out=TM2.t[:, TK], in0=ys[2].t[:, TK], in1=SG[2].t[:, TK], op=ALU.mult), r=[ys[2].r, SG[2].r], w=[TM2.r])
            A("dve", lambda e, dc=dc: e.tensor_tensor(out=MT.t[:, dc, TK], in0=TM.t[:, TK], in1=TM2.t[:, TK], op=ALU.add), r=[TM.r, TM2.r], w=[MT.r])

        if kind == "s" and l == 0:
            dbg("CVG", CVG.t[:, :, TK], CVG.r, [128, 8, 128], BF16); dbg("HMG", HMG.t[:, :, TK], HMG.r, [128, 8, 128], BF16)
            dbg("AOG", AOG.t[:, :, TK], AOG.r, [128, 16, 128], BF16); dbg("MT", MT.t[:, :, TK], MT.r, [128, 32, 128], BF16)
        for cg in range(8):
            bo = [abank(j) for j in range(nsub)]
            for kq in range(4):
                wv, wr = wload(w_out[l][kq * 1024:(kq + 1) * 1024, cg * 512:(cg + 1) * 512], 8, 512, ("w_out", l, kq, cg))
                for j in range(nsub):
                    for k8 in range(8):
                        kc = kq * 8 + k8
                        mm(bo[j].t[:, :], MT.t[:, kc, j * 128:(j + 1) * 128], wv[:, k8, :], kc == 0, kc == 31, [wr, MT.r], [bo[j].r])
            for j in range(nsub):
                A("dve", lambda e, j=j, bo=bo, cg=cg: e.tensor_tensor(out=Xs[j].t[:, cg * 512:(cg + 1) * 512], in0=Xs[j].t[:, cg * 512:(cg + 1) * 512],
                                                                      in1=bo[j].t[:, :], op=ALU.add), r=[Xs[j].r, bo[j].r], w=[Xs[j].r])
        if l == NLAYER - 1:
            for j in range(nsub):
                dst = y_p[pos0 + j * 128:pos0 + (j + 1) * 128, :] if kind == "p" else y_s[:, :]
                dma_out(dst, Xs[j].t[:], Xs[j].r)

    kvc_res = [[P.res(f"kvc{l}_{h}") for h in range(16)] for l in range(2)]
    krc_res = [P.res(f"krc{l}") for l in range(2)]
    tiles = []
    for i in range(NTILES_P):
        tiles.append(dict(kind="p", ntok=TT, pos0=i * TT, idx=i, last=(i == NPT - 1)))
    if DO_SAMPLE:
        tiles.append(dict(kind="s", ntok=128, pos0=PAST, idx=0, last=True))
    for tile in tiles:
        for l in range(NLAYER):
            process(tile, l)
    P.emit()
    st.close()
    return nc


def host_consts():
    ident = np.eye(128, dtype=np.float32)
    rot = np.zeros((64, 64), np.float32)
    for i in range(32):
        rot[i + 32, i] = -1.0
        rot[i, i + 32] = 1.0
    utri = np.triu(np.ones((64, 64), np.float32))
    half = 32
    freqs = (np.float32(10000.0) ** (-np.arange(half, dtype=np.float32) / np.float32(half))).astype(np.float32)

    def cs(pos):
        ang = pos.astype(np.float32)[None, :] * freqs[:, None]
        c = np.cos(ang).astype(np.float32)
        s = np.sin(ang).astype(np.float32)
        return np.stack([np.concatenate([c, c], 0), np.concatenate([s, s], 0)], axis=1)
    csp = cs(np.arange(SEQ))
    css = np.tile(cs(PAST + np.arange(32)), (1, 1, 4))
    am = np.zeros((128, 2, TT), np.float32)
    for kb in range(2):
        for p in range(128):
            kk = kb * 128 + p
            am[p, kb, :] = ((np.arange(TT) // 64) >= (kk // 64)).astype(np.float32)
    return dict(c_ident=ident, c_rot=rot, c_utri=utri, c_csp=np.ascontiguousarray(csp), c_css=np.ascontiguousarray(css),
                c_am=am, c_eye4=np.eye(4, dtype=np.float32))


_WNAMES = ["w_in", "w_pc", "w_pm", "w_uq", "w_ukv", "w_pa", "w_out"]


def host_params(inp):
    f = lambda a: np.asarray(a, dtype=np.float32)
    pc = lambda a, n: np.ascontiguousarray(f(a).reshape(2, n, 128).transpose(2, 0, 1))
    b_in = f(inp["b_in"])
    hb = np.zeros((128, 2, NBCH), np.float32)
    for l in range(2):
        hb[:, l, 0:40] = b_in[l, 0:5120].reshape(40, 128).T
        hb[0:8, l, 40] = b_in[l, 5120:5128]
        hb[:, l, 41:69] = b_in[l, 5128:8712].reshape(28, 128).T
        hb[0:64, l, 69] = b_in[l, 8712:8776]
        hb[:, l, 70:182] = b_in[l, 8776:NIN].reshape(112, 128).T
    return dict(h_lng=pc(inp["ln_g"], 32), h_bin=hb,
                h_cw=np.ascontiguousarray(f(inp["conv_w"]).reshape(2, 31, 8, 128).transpose(3, 0, 2, 1)),
                h_cb=pc(inp["conv_b"], 8), h_clg=pc(inp["conv_ln_g"], 8), h_clb=pc(inp["conv_ln_b"], 8), h_mng=pc(inp["m_norm_g"], 8),
                h_cqg=pc(inp["cq_g"], 8), h_ckvg=pc(inp["ckv_g"], 4),
                h_qng=np.ascontiguousarray(f(inp["qn_g"]).T), h_kng=np.ascontiguousarray(f(inp["kn_g"]).T),
                h_qrg=np.ascontiguousarray(f(inp["qr_g"]).T), h_krg=np.ascontiguousarray(f(inp["kr_g"]).T))


def kernel(**inp):
    cfg = {}
    nc = build(cfg)
    consts = host_consts()
    f = lambda a: np.ascontiguousarray(np.asarray(a, dtype=np.float32))
    W = {k: f(inp[k]) for k in _WNAMES}
    W.update(host_params(inp))
    in_maps = []
    for c in range(8):
        sl = slice(4 * c, 4 * c + 4)
        m = dict(W)
        m.update(consts)
        m["xp"] = f(inp["x_prompt"][c])
        m["xs"] = f(inp["x_sample"][sl]).reshape(128, D)
        m["latc"] = f(inp["cache_kv_latent"][:, sl])
        m["krc"] = f(inp["cache_k_rope"][:, sl])
        m["stconv"] = f(inp["state_conv"][:, sl])
        m["stC"] = f(inp["state_mlstm_C"][:, sl])
        m["stn"] = f(inp["state_mlstm_n"][:, sl])
        m["stm"] = f(inp["state_mlstm_m"][:, sl])
        in_maps.append(m)
    res = run_bass_kernel_spmd(nc, in_maps, core_ids=list(range(8)))
    R = res.results
    cat = lambda k, ax: np.concatenate([np.asarray(r[k]) for r in R], axis=ax)
    stk = lambda k, ax: np.stack([np.asarray(r[k]) for r in R], axis=ax)
    y_prompt = stk("y_p", 0)
    y_sample = cat("y_s", 0).reshape(32, 32, D)
    return (y_prompt, y_sample,
            stk("cs_p", 1), cat("cs_s", 1),
            stk("C_p", 1), stk("n_p", 1), stk("m_p", 1),
            cat("C_s", 1), cat("n_s", 1), cat("m_s", 1),
            stk("lat_p", 1), stk("kr_p", 1),
            cat("lat_s", 1).reshape(2, 32, 32, 512), cat("kr_s", 1).reshape(2, 32, 32, 64))
```

```python
import numpy as np
from contextlib import ExitStack
import concourse.bass as bass
import concourse.mybir as mybir
from concourse.bass_utils import run_bass_kernel_spmd

F32 = mybir.dt.float32
BF16 = mybir.dt.bfloat16
AF = mybir.ActivationFunctionType
ALU = mybir.AluOpType
AX = mybir.AxisListType

D = 4096
NIN = 23112
SEQ = 2048
TT = 256
NPT = SEQ // TT
PAST = 1024
EPS = 1e-6
ATTN_SCALE = 192.0 ** -0.5
O_CA, O_CB, O_CZ, O_MQ, O_MK, O_MV, O_MIF, O_MO, O_MZ = 0, 1024, 2048, 3072, 3584, 4096, 5120, 5128, 6152
O_ACQ, O_ACKV, O_AKR, O_AZ, O_G = 7176, 8200, 8712, 8776, 10824
NBCH = 182


def cid(col):
    if col < 5120:
        return col // 128
    if col == 5120:
        return 40
    if col < 8712:
        return 41 + (col - 5128) // 128
    if col == 8712:
        return 69
    return 70 + (col - 8776) // 128


ENGS = ("pe", "act", "dve", "pool", "sp")
SAME_ENGINE_SYNC = True
EPOCH = 30000


class Res:
    __slots__ = ("name", "last_w", "readers", "sem", "semcnt", "arena")

    def __init__(self, name, arena=False):
        self.name = name
        self.last_w = None
        self.readers = {}
        self.sem = None
        self.semcnt = 0
        self.arena = arena


_SEQ = [0]


class Op:
    __slots__ = ("eng", "fn", "deps", "dma", "flag", "sem", "val", "seq")

    def __init__(self, eng, fn, dma):
        _SEQ[0] += 1
        self.seq = _SEQ[0]
        self.eng = eng
        self.fn = fn
        self.deps = []
        self.dma = dma
        self.flag = False
        self.sem = None
        self.val = 0


class _Rec:
    def __getattr__(self, name):
        return lambda *a, **k: (name, a, k)


_REC = _Rec()


class Prog:
    def __init__(self, nc, stack):
        self.nc = nc
        self.stack = stack
        self.ops = {e: [] for e in ENGS}
        self.out_dmas = []
        self.nres = 0
        self.token = Res("arena_token")
        self.dsem = {}
        self.verbose = False

    def res(self, name=None, arena=False):
        self.nres += 1
        return Res(name or f"r{self.nres}", arena)

    def _newsem(self, name):
        return self.stack.enter_context(self.nc.semaphore(name))

    def op(self, eng, fn, r=(), w=(), dma=False, out=False, semres=None):
        o = Op(eng, fn(_REC) if fn is not None else None, dma)
        deps = o.deps
        r = list(r)
        w = list(w)
        if any(R.arena for R in r) or any(R.arena for R in w):
            r.append(self.token)
        if dma:
            R0 = semres if semres is not None else (w + r)[0]
            ent = self.dsem.get(R0.name)
            if ent is None:
                ent = [self._newsem("d" + R0.name), 0]
                self.dsem[R0.name] = ent
            ent[1] += 16
            o.sem = ent[0]
            o.val = ent[1]
            if out:
                self.out_dmas.append(o)
        o_sem_key = o.sem
        for R in r:
            if R.last_w is not None:
                deps.append((R.last_w, 0))
        for R in w:
            if R.last_w is not None:
                deps.append((R.last_w, 1))
            for rd in R.readers.values():
                deps.append((rd, 2))
        key = (eng, id(o_sem_key)) if dma else (eng, None)
        for R in r:
            R.readers[key] = o
        for R in w:
            R.last_w = o
            R.readers = {}
        self.ops[eng].append(o)
        return o

    def barrier(self):
        self.op("act", lambda e: e.nop(), w=[self.token])

    def _need(self, o, d, kind):
        if d.dma:
            return True
        if d.eng == o.eng and not o.dma:
            if d.eng == "pe":
                return False
            return kind == 0 and SAME_ENGINE_SYNC
        return True

    def emit(self):
        nc = self.nc
        for e in ENGS:
            for o in self.ops[e]:
                for (d, kind) in o.deps:
                    if not d.dma and self._need(o, d, kind):
                        d.flag = True
        fin = Op("sp", None, False)
        fin.deps = [(d, 0) for d in self.out_dmas]
        self.ops["sp"].append(fin)
        for e in ENGS:
            cnt = 0
            sems = {}
            for o in self.ops[e]:
                if o.dma or not o.flag:
                    continue
                ep = cnt // EPOCH
                if ep not in sems:
                    sems[ep] = self._newsem(f"e{e}{ep}")
                o.sem = sems[ep]
                o.val = cnt % EPOCH + 1
                cnt += 1

        if self.verbose:
            mx = {e: max([o.val for o in self.ops[e] if o.sem is not None] + [0]) for e in ENGS}
            print("ops per engine", {e: len(self.ops[e]) for e in ENGS}, "max sem val per engine", mx,
                  "dma sems", len(self.dsem), "max dma val", max([v[1] for v in self.dsem.values()] + [0]), flush=True)

        def run(engname):
            def body(eng):
                waited = {}
                for o in self.ops[engname]:
                    for (d, kind) in o.deps:
                        if not self._need(o, d, kind):
                            continue
                        k = id(d.sem)
                        if waited.get(k, 0) >= d.val:
                            continue
                        eng.wait_ge(d.sem, d.val)
                        waited[k] = d.val
                    if o.fn is None:
                        continue
                    name, a, k = o.fn
                    ins = getattr(eng, name)(*a, **k)
                    if o.dma:
                        ins.then_inc(o.sem, 16)
                    elif o.flag:
                        ins.then_inc(o.sem, 1)
            return body

        with nc.Block() as block:
            block.tensor(run("pe"))
            block.scalar(run("act"))
            block.vector(run("dve"))
            block.gpsimd(run("pool"))
            block.sync(run("sp"))


def build(cfg):
    NTILES_P = cfg.get("ntiles_p", NPT)
    DO_SAMPLE = cfg.get("sample", True)
    NLAYER = cfg.get("nlayer", 2)
    USE_WC = cfg.get("use_wc", True)
    nc = bass.Bass("TRN2", target_bir_lowering=False)

    def din(name, shape, dt=F32):
        return nc.dram_tensor(name, list(shape), dt, kind="ExternalInput").ap()

    def dout(name, shape):
        return nc.dram_tensor(name, list(shape), F32, kind="ExternalOutput").ap()

    xp = din("xp", [SEQ, D]); xs = din("xs", [128, D])
    latp_d = din("latc", [2, 4, PAST, 512]); krp_d = din("krc", [2, 4, PAST, 64])
    stconv = din("stconv", [2, 4, 30, 1024]); stC = din("stC", [2, 4, 4, 128, 256])
    stn = din("stn", [2, 4, 4, 128]); stm = din("stm", [2, 4, 4])
    w_in = din("w_in", [2, D, NIN])
    w_pc = din("w_pc", [2, 1024, D]); w_pm = din("w_pm", [2, 1024, D])
    w_uq = din("w_uq", [2, 1024, 3072]); w_ukv = din("w_ukv", [2, 512, D]); w_pa = din("w_pa", [2, 2048, D])
    w_out = din("w_out", [2, D, D])
    h_lng = din("h_lng", [128, 2, 32]); h_bin = din("h_bin", [128, 2, NBCH]); h_cw = din("h_cw", [128, 2, 8, 31])
    h_cb = din("h_cb", [128, 2, 8]); h_clg = din("h_clg", [128, 2, 8]); h_clb = din("h_clb", [128, 2, 8]); h_mng = din("h_mng", [128, 2, 8])
    h_cqg = din("h_cqg", [128, 2, 8]); h_ckvg = din("h_ckvg", [128, 2, 4]); h_qng = din("h_qng", [128, 2]); h_kng = din("h_kng", [128, 2])
    h_qrg = din("h_qrg", [64, 2]); h_krg = din("h_krg", [64, 2])
    c_ident = din("c_ident", [128, 128]); c_rot = din("c_rot", [64, 64]); c_utri = din("c_utri", [64, 64])
    c_csp = din("c_csp", [64, 2, SEQ]); c_css = din("c_css", [64, 2, 128]); c_am = din("c_am", [128, 2, TT])
    c_eye4 = din("c_eye4", [4, 4])

    y_p = dout("y_p", [SEQ, D]); y_s = dout("y_s", [128, D])
    cs_p = dout("cs_p", [2, 30, 1024]); cs_s = dout("cs_s", [2, 4, 30, 1024])
    C_p = dout("C_p", [2, 4, 128, 256]); n_p = dout("n_p", [2, 4, 128]); m_p = dout("m_p", [2, 4])
    C_s = dout("C_s", [2, 4, 4, 128, 256]); n_s = dout("n_s", [2, 4, 4, 128]); m_s = dout("m_s", [2, 4, 4])
    lat_p = dout("lat_p", [2, SEQ, 512]); kr_p = dout("kr_p", [2, SEQ, 64])
    lat_s = dout("lat_s", [2, 128, 512]); kr_s = dout("kr_s", [2, 128, 64])
    KC = nc.dram_tensor("KC", [2, 16, 128, SEQ], BF16).ap()
    VC = nc.dram_tensor("VC", [2, 16, SEQ, 128], BF16).ap()
    KRC = nc.dram_tensor("KRC", [2, 64, SEQ], BF16).ap()

    st = ExitStack()
    P = Prog(nc, st)
    P.verbose = cfg.get("verbose", False)

    def sbt(name, shape, dt=F32):
        return st.enter_context(nc.sbuf_tensor(name, list(shape), dt))

    class TB:
        def __init__(self, t, res):
            self.t = t
            self.r = res

    def pers(name, shape, dt=F32):
        return TB(sbt(name, shape, dt), P.res(name))

    NSUB = TT // 128
    Xs = [pers(f"X{j}", [128, D]) for j in range(NSUB)]
    HT = pers("HT", [128, 32, TT], BF16)
    CVG = pers("CVG", [128, 8, TT], BF16)
    HMG = pers("HMG", [128, 8, TT], BF16)
    AOG = pers("AOG", [128, 16, TT], BF16)
    NW = 4
    WS = [pers(f"WS{i}", [128, 4096], BF16) for i in range(NW)]
    IDF = pers("IDF", [128, 128]); IDB = pers("IDB", [128, 128], BF16)
    ONF = pers("ONF", [128, 128]); ONB = pers("ONB", [128, 128], BF16)
    ROTF = pers("ROTF", [64, 64]); ROTB = pers("ROTB", [64, 64], BF16)
    UTF = pers("UTF", [64, 64]); UTB = pers("UTB", [64, 64], BF16)
    EYE4 = pers("EYE4", [4, 4])
    AMF = pers("AMF", [128, 2, TT]); AMB = pers("AMB", [128, 2, TT], BF16)
    CS = pers("CS", [64, 2, TT])
    LNG = pers("LNG", [128, 2, 32]); BIN = pers("BIN", [128, 2, NBCH])
    CW = pers("CW", [128, 2, 8, 31]); CB_ = pers("CB", [128, 2, 8]); CLG = pers("CLG", [128, 2, 8]); CLB = pers("CLB", [128, 2, 8])
    MNG = pers("MNG", [128, 2, 8]); CQG = pers("CQG", [128, 2, 8]); CKVG = pers("CKVG", [128, 2, 4])
    QNG = pers("QNG", [128, 2]); KNG = pers("KNG", [128, 2]); QRG = pers("QRG", [64, 2]); KRG = pers("KRG", [64, 2])
    CT = [pers(f"CT{l}", [128, 8, 30]) for l in range(2)]
    CNP = [pers(f"CNP{l}", [128, 4, 257]) for l in range(2)]
    MRP = [pers(f"MRP{l}", [4, 1]) for l in range(2)]
    ARENA_BYTES = 72 * 1024
    ARENA = sbt("ARENA", [128, ARENA_BYTES // 2], BF16)
    astate = dict(lo=0, hi=ARENA_BYTES, side=0, phase=0)
    alive = []
    ancient = Res("ancient")

    def _merge(dst, k, op):
        cur = dst.get(k)
        if cur is None or cur.seq < op.seq:
            dst[k] = op

    def _inherit(new, old):
        for k, op in old.readers.items():
            _merge(new.readers, k, op)
        lw = old.last_w
        if lw is not None:
            _merge(new.readers, (lw.eng, id(lw.sem) if lw.dma else None), lw)

    def areset():
        astate["phase"] += 1
        astate["side"] ^= 1
        astate["lo"] = 0
        astate["hi"] = ARENA_BYTES
        ph = astate["phase"]
        keep = []
        for ent in alive:
            if ent[3] < ph - 4:
                _inherit(ancient, ent[2])
            else:
                keep.append(ent)
        alive[:] = keep

    def aal(name, shape, dt=F32, parts=128):
        esz = 4 if dt == F32 else 2
        n = int(np.prod(shape))
        nb = (n * esz + 63) // 64 * 64
        if astate["side"] == 0:
            o = astate["lo"]
            astate["lo"] = o + nb
        else:
            astate["hi"] -= nb
            o = astate["hi"]
        assert astate["lo"] <= astate["hi"], (name, astate)
        ap = ARENA[0:parts, o // 2:(o + n * esz) // 2]
        if dt == F32:
            ap = ap.bitcast(F32)
        if len(shape) == 2:
            ap = ap.rearrange("p (a b) -> p a b", a=shape[0])
        elif len(shape) == 3:
            ap = ap.rearrange("p (a b c) -> p a b c", a=shape[0], b=shape[1])
        R = P.res(name)
        _inherit(R, ancient)
        for (s0, e0, R0, ph0) in alive:
            if s0 < o + nb and o < e0:
                _inherit(R, R0)
        alive.append((o, o + nb, R, astate["phase"]))
        return TB(ap, R)

    banks = [TB(st.enter_context(nc.psum_tensor(f"pb{i}", [128, 512], F32)), P.res(f"pb{i}")) for i in range(8)]
    bcnt = [0]

    pinned = set()

    def bank(pin=False):
        while True:
            i = bcnt[0] % 6
            bcnt[0] += 1
            if i not in pinned:
                break
        if pin:
            pinned.add(i)
        return banks[i]

    def unpin(b):
        pinned.discard(banks.index(b))

    def abank(i):
        return banks[6 + i]

    def bfv(b):
        return b.t.bitcast(BF16)

    wcnt = [0]
    NIMG = 720
    WPER = 180
    WCS = [nc.dram_tensor(f"WC{i}", [WPER, 128, 4096], BF16).ap() for i in range(NIMG // WPER)]
    wc_index = {}
    WST = [P.res(f"WSst{i}") for i in range(NW)]

    def wload(src, nk, ncols, key):
        i = wcnt[0] % NW
        s = WS[i]
        wcnt[0] += 1
        n = nk * ncols
        v = s.t[:, 0:n].rearrange("p (k c) -> p k c", k=nk)
        ent = wc_index.get(key)
        if ent is None:
            idx = len(wc_index)
            assert idx < NIMG
            R = P.res(f"wc{idx}")
            wc_index[key] = (idx, R)
            P.op("pool", lambda e: e.dma_start(out=v, in_=src.rearrange("(k p) c -> p k c", p=128)), w=[s.r], dma=True)
            if USE_WC:
                P.op("sp", lambda e: e.dma_start(out=WCS[idx // WPER][idx % WPER][:, 0:n], in_=s.t[:, 0:n]), r=[s.r], w=[R], dma=True, semres=WST[i])
        else:
            idx, R = ent
            if USE_WC:
                P.op("sp", lambda e: e.dma_start(out=s.t[:, 0:n], in_=WCS[idx // WPER][idx % WPER][:, 0:n]), r=[R], w=[s.r], dma=True)
            else:
                P.op("pool", lambda e: e.dma_start(out=v, in_=src.rearrange("(k p) c -> p k c", p=128)), w=[s.r], dma=True)
        return v, s.r

    def dma_in(dst, src, R, slow=False, eng="sp"):
        P.op(eng, lambda e: e.dma_start(out=dst, in_=src, allow_slow_non_contiguous=slow), w=[R], dma=True)

    def dma_out(dst, src, R, slow=False, final=True, extra_w=()):
        P.op("sp", lambda e: e.dma_start(out=dst, in_=src, allow_slow_non_contiguous=slow), r=[R], w=list(extra_w),
             dma=True, out=final, semres=R)

    A = lambda eng, fn, r=(), w=(): P.op(eng, fn, r=r, w=w)
    DBG = cfg.get("dbg", False)

    def dbg(name, ap, R, shape, dt=F32):
        if DBG:
            d = nc.dram_tensor("dbg_" + name, list(shape), dt, kind="ExternalOutput").ap()
            dma_out(d, ap, R, slow=True)

    def mm(out, lhsT, rhs, start, stop, r, w):
        P.op("pe", lambda e: e.matmul(out, lhsT, rhs, start=start, stop=stop), r=r, w=w)

    def tr(out, in_, ident, r, w):
        P.op("pe", lambda e: e.transpose(out, in_, ident), r=r, w=w)

    dma_in(IDF.t[:], c_ident, IDF.r); dma_in(ROTF.t[:], c_rot, ROTF.r); dma_in(UTF.t[:], c_utri, UTF.r)
    dma_in(EYE4.t[:], c_eye4, EYE4.r); dma_in(AMF.t[:], c_am, AMF.r)
    A("dve", lambda e: e.tensor_copy(out=IDB.t[:], in_=IDF.t[:]), r=[IDF.r], w=[IDB.r])
    A("dve", lambda e: e.tensor_copy(out=ROTB.t[:], in_=ROTF.t[:]), r=[ROTF.r], w=[ROTB.r])
    A("dve", lambda e: e.tensor_copy(out=UTB.t[:], in_=UTF.t[:]), r=[UTF.r], w=[UTB.r])
    A("dve", lambda e: e.tensor_copy(out=AMB.t[:], in_=AMF.t[:]), r=[AMF.r], w=[AMB.r])
    A("dve", lambda e: e.memset(ONF.t[:], 1.0), w=[ONF.r])
    A("dve", lambda e: e.memset(ONB.t[:], 1.0), w=[ONB.r])
    for (dst, src) in ((LNG, h_lng), (BIN, h_bin), (CW, h_cw), (CB_, h_cb), (CLG, h_clg), (CLB, h_clb), (MNG, h_mng),
                       (CQG, h_cqg), (CKVG, h_ckvg), (QNG, h_qng), (KNG, h_kng), (QRG, h_qrg), (KRG, h_krg)):
        dma_in(dst.t[:], src, dst.r)
    for l in range(2):
        A("dve", lambda e, l=l: e.memset(CT[l].t[:], 0.0), w=[CT[l].r])
        A("dve", lambda e, l=l: e.memset(CNP[l].t[:], 0.0), w=[CNP[l].r])
        A("dve", lambda e, l=l: e.memset(MRP[l].t[:], 0.0), w=[MRP[l].r])

    def bias(l, col, n=128):
        return BIN.t[0:n, l, cid(col):cid(col) + 1]

    def rstd_from(dst, src_ap, scale, R_src, parts=128):
        A("dve", lambda e: e.tensor_scalar(out=dst.t, in0=src_ap, scalar1=scale, scalar2=EPS, op0=ALU.mult, op1=ALU.add),
          r=[R_src], w=[dst.r])
        A("act", lambda e: e.activation(out=dst.t, in_=dst.t, func=AF.Sqrt), r=[dst.r], w=[dst.r])
        A("dve", lambda e: e.reciprocal(out=dst.t, in_=dst.t), r=[dst.r], w=[dst.r])

    def process(tile, l):
        kind = tile["kind"]
        ntok = tile["ntok"]
        nsub = ntok // 128
        pos0 = tile["pos0"]
        ti = tile["idx"]
        last = tile["last"]
        L = 64 if kind == "p" else 32
        nch = 4
        TK = slice(0, ntok)

        def w_in_cols(c0, ncols):
            return wload(w_in[l][:, c0:c0 + ncols], 32, ncols, ("w_in", l, c0, ncols))

        def projW(c0, ncols):
            wv, wr = w_in_cols(c0, ncols)
            b = bank()
            for kc in range(32):
                mm(b.t[0:ncols, TK], wv[:, kc, :], HT.t[:, kc, TK], kc == 0, kc == 31, [wr, HT.r], [b.r])
            return b

        areset()
        XS = aal("XS", [D], BF16)
        SSQ = aal("SSQ", [4]); RSTD0 = aal("RSTD0", [4])
        for j in range(nsub):
            Xj = Xs[j]
            if l == 0:
                src = (xp[pos0 + j * 128: pos0 + (j + 1) * 128, :] if kind == "p" else xs[:, :])
                dma_in(Xj.t[:], src, Xj.r)
            A("act", lambda e, Xj=Xj, j=j: e.activation(out=XS.t, in_=Xj.t[:], func=AF.Square, accum_out=SSQ.t[:, j:j + 1]),
              r=[Xj.r], w=[XS.r, SSQ.r])
            A("dve", lambda e, j=j: e.tensor_scalar(out=RSTD0.t[:, j:j + 1], in0=SSQ.t[:, j:j + 1], scalar1=1.0 / D, scalar2=EPS,
                                                    op0=ALU.mult, op1=ALU.add), r=[SSQ.r], w=[RSTD0.r])
            A("act", lambda e, j=j: e.activation(out=RSTD0.t[:, j:j + 1], in_=RSTD0.t[:, j:j + 1], func=AF.Sqrt), r=[RSTD0.r], w=[RSTD0.r])
            A("dve", lambda e, j=j: e.reciprocal(out=RSTD0.t[:, j:j + 1], in_=RSTD0.t[:, j:j + 1]), r=[RSTD0.r], w=[RSTD0.r])
            A("act", lambda e, Xj=Xj, j=j: e.activation(out=XS.t, in_=Xj.t[:], func=AF.Copy, scale=RSTD0.t[:, j:j + 1]),
              r=[Xj.r, RSTD0.r], w=[XS.r])
            for g in range(8):
                b = bank()
                for q in range(4):
                    c = 4 * g + q
                    tr(bfv(b)[:, q * 128:(q + 1) * 128], XS.t[:, c * 128:(c + 1) * 128], IDB.t[:], [XS.r, IDB.r], [b.r])
                A("dve", lambda e, b=b, g=g, j=j: e.tensor_tensor(
                    out=HT.t[:, 4 * g:4 * g + 4, j * 128:(j + 1) * 128],
                    in0=bfv(b)[:, 0:512].rearrange("p (q t) -> p q t", q=4),
                    in1=LNG.t[:, l, 4 * g:4 * g + 4].unsqueeze(2).broadcast_to([128, 4, 128]), op=ALU.mult),
                  r=[b.r, LNG.r], w=[HT.r])

        areset()
        UF = aal("UF", [8, 30 + TT])
        ACC = aal("ACC", [8, TT])
        SIG = aal("SIG", [TT]); SQC = aal("SQC", [TT]); MEAN = aal("MEAN", [TT]); RSTDC = aal("RSTDC", [TT]); SZC = aal("SZC", [TT])
        CSO = aal("CSO", [1024], F32, parts=32)
        STG = aal("STG", [1024], F32, parts=32)
        if kind == "p":
            def ufnew(cc): return UF.t[:, cc, 30:30 + TT]
            def ufwin(cc, k): return UF.t[:, cc, k:k + TT]
            def accv(cc): return ACC.t[:, cc, 0:TT]
            def psv(ap): return ap
            def v2(t): return t.t[:, 0:TT]
            A("dve", lambda e: e.tensor_copy(out=UF.t[:, :, 0:30], in_=CT[l].t[:]), r=[CT[l].r], w=[UF.r])
        else:
            UFs = UF.t[:, :, 0:248].rearrange("p c (s t) -> p c s t", s=4)
            def ufnew(cc): return UFs[:, cc, :, 30:62]
            def ufwin(cc, k): return UFs[:, cc, :, k:k + 32]
            def accv(cc): return ACC.t[:, cc, 0:128].rearrange("p (s t) -> p s t", s=4)
            def psv(ap): return ap.rearrange("p (s t) -> p s t", s=4)
            def v2(t): return t.t[:, 0:128].rearrange("p (s t) -> p s t", s=4)
            for s in range(4):
                dma_in(STG.t[0:30, :], stconv[l, s], STG.r)
                for g in range(2):
                    b = bank()
                    for q in range(4):
                        cc = 4 * g + q
                        tr(b.t[:, q * 32:q * 32 + 30], STG.t[0:30, cc * 128:(cc + 1) * 128], IDF.t[0:30, 0:30], [STG.r, IDF.r], [b.r])
                    A("dve", lambda e, b=b, g=g, s=s: e.tensor_copy(
                        out=UFs[:, 4 * g:4 * g + 4, s, 0:30],
                        in_=b.t[:, 0:128].rearrange("p (q t) -> p q t", q=4)[:, :, 0:30]), r=[b.r], w=[UF.r])
        bS1 = abank(0); bS2 = abank(1)
        UFr = [P.res(f"UF{c}") for c in range(8)]
        ACCr = [P.res(f"ACC{c}") for c in range(8)]
        for c in range(8):
            _inherit(UFr[c], UF.r)
            _inherit(ACCr[c], ACC.r)
        for ent in list(alive):
            if ent[2] is UF.r:
                alive.extend((ent[0], ent[1], R_, ent[3]) for R_ in UFr)
            if ent[2] is ACC.r:
                alive.extend((ent[0], ent[1], R_, ent[3]) for R_ in ACCr)
        SQCs = [SQC, aal("SQC2", [TT])]
        SIGs = [SIG, aal("SIG2", [TT])]
        for cp in range(4):
            pair = (2 * cp, 2 * cp + 1)
            for i_, cc in enumerate(pair):
                ba = projW(O_CA + cc * 128, 128)
                bb = projW(O_CB + cc * 128, 128)
                SG_ = SIGs[i_]
                A("act", lambda e, bb=bb, cc=cc, SG_=SG_: e.activation(out=SG_.t[:, TK], in_=bb.t[:, TK], func=AF.Sigmoid,
                                                                        bias=bias(l, O_CB + cc * 128), scale=1.0), r=[bb.r, BIN.r], w=[SG_.r])
                A("dve", lambda e, ba=ba, cc=cc, SG_=SG_: e.scalar_tensor_tensor(out=ufnew(cc), in0=psv(ba.t[:, TK]), scalar=bias(l, O_CA + cc * 128),
                                                                                 in1=psv(SG_.t[:, TK]), op0=ALU.add, op1=ALU.mult),
                  r=[ba.r, SG_.r, BIN.r], w=[UFr[cc]])
                A("dve", lambda e, cc=cc: e.tensor_scalar(out=accv(cc), in0=ufwin(cc, 0), scalar1=CW.t[:, l, cc, 0:1],
                                                          scalar2=CB_.t[:, l, cc:cc + 1], op0=ALU.mult, op1=ALU.add),
                  r=[UFr[cc], CW.r, CB_.r], w=[ACCr[cc]])
            for k in range(1, 31):
                for cc in pair:
                    A("dve", lambda e, cc=cc, k=k: e.scalar_tensor_tensor(out=accv(cc), in0=ufwin(cc, k), scalar=CW.t[:, l, cc, k:k + 1],
                                                                          in1=accv(cc), op0=ALU.mult, op1=ALU.add),
                      r=[UFr[cc], CW.r, ACCr[cc]], w=[ACCr[cc]])
            for i_, cc in enumerate(pair):
                SQ_ = SQCs[i_]
                A("act", lambda e, cc=cc, SQ_=SQ_: e.activation(out=SQ_.t[:, TK], in_=ACC.t[:, cc, TK], func=AF.Square), r=[ACCr[cc]], w=[SQ_.r])
                mm(bS1.t[:, TK], ONF.t[:], ACC.t[:, cc, TK], cc == 0, cc == 7, [ONF.r, ACCr[cc]], [bS1.r])
                mm(bS2.t[:, TK], ONF.t[:], SQ_.t[:, TK], cc == 0, cc == 7, [ONF.r, SQ_.r], [bS2.r])
        A("dve", lambda e: e.tensor_scalar(out=MEAN.t[:, TK], in0=bS1.t[:, TK], scalar1=1.0 / 1024, scalar2=None, op0=ALU.mult),
          r=[bS1.r], w=[MEAN.r])
        A("dve", lambda e: e.tensor_tensor(out=SQC.t[:, TK], in0=MEAN.t[:, TK], in1=MEAN.t[:, TK], op=ALU.mult), r=[MEAN.r], w=[SQC.r])
        A("dve", lambda e: e.scalar_tensor_tensor(out=RSTDC.t[:, TK], in0=bS2.t[:, TK], scalar=1.0 / 1024, in1=SQC.t[:, TK],
                                                  op0=ALU.mult, op1=ALU.subtract), r=[bS2.r, SQC.r], w=[RSTDC.r])
        A("dve", lambda e: e.tensor_scalar(out=RSTDC.t[:, TK], in0=RSTDC.t[:, TK], scalar1=EPS, scalar2=None, op0=ALU.add),
          r=[RSTDC.r], w=[RSTDC.r])
        A("act", lambda e: e.activation(out=RSTDC.t[:, TK], in_=RSTDC.t[:, TK], func=AF.Sqrt), r=[RSTDC.r], w=[RSTDC.r])
        A("dve", lambda e: e.reciprocal(out=RSTDC.t[:, TK], in_=RSTDC.t[:, TK]), r=[RSTDC.r], w=[RSTDC.r])
        for cc in range(8):
            bz = projW(O_CZ + cc * 128, 128)
            A("act", lambda e, bz=bz, cc=cc: e.activation(out=SZC.t[:, TK], in_=bz.t[:, TK], func=AF.Silu,
                                                          bias=bias(l, O_CZ + cc * 128), scale=1.0), r=[bz.r, BIN.r], w=[SZC.r])
            A("dve", lambda e, cc=cc: e.tensor_tensor(out=ACC.t[:, cc, TK], in0=ACC.t[:, cc, TK], in1=MEAN.t[:, TK], op=ALU.subtract),
              r=[ACCr[cc], MEAN.r], w=[ACCr[cc]])
            A("dve", lambda e, cc=cc: e.tensor_tensor(out=ACC.t[:, cc, TK], in0=ACC.t[:, cc, TK], in1=RSTDC.t[:, TK], op=ALU.mult),
              r=[ACCr[cc], RSTDC.r], w=[ACCr[cc]])
            A("act", lambda e, cc=cc: e.activation(out=ACC.t[:, cc, TK], in_=ACC.t[:, cc, TK], func=AF.Silu,
                                                   bias=CLB.t[:, l, cc:cc + 1], scale=CLG.t[:, l, cc:cc + 1]),
              r=[ACCr[cc], CLB.r, CLG.r], w=[ACCr[cc]])
            A("dve", lambda e, cc=cc: e.tensor_tensor(out=CVG.t[:, cc, TK], in0=ACC.t[:, cc, TK], in1=SZC.t[:, TK], op=ALU.mult),
              r=[ACCr[cc], SZC.r], w=[CVG.r])
        if kind == "p":
            A("dve", lambda e: e.tensor_copy(out=CT[l].t[:], in_=UF.t[:, :, TT:TT + 30]), r=UFr, w=[CT[l].r])
            if last:
                for g in range(2):
                    b = bank()
                    for q in range(4):
                        cc = 4 * g + q
                        tr(b.t[0:30, q * 128:(q + 1) * 128], UF.t[:, cc, TT:TT + 30], IDF.t[:], [UFr[cc], IDF.r], [b.r])
                    A("act", lambda e, b=b, g=g: e.copy(out=CSO.t[0:30, g * 512:(g + 1) * 512], in_=b.t[0:30, :]), r=[b.r], w=[CSO.r])
                dma_out(cs_p[l], CSO.t[0:30, :], CSO.r)
        else:
            for s in range(4):
                for g in range(2):
                    b = bank()
                    for q in range(4):
                        cc = 4 * g + q
                        tr(b.t[0:30, q * 128:(q + 1) * 128], UFs[:, cc, s, 32:62], IDF.t[:], [UFr[cc], IDF.r], [b.r])
                    A("act", lambda e, b=b, g=g: e.copy(out=CSO.t[0:30, g * 512:(g + 1) * 512], in_=b.t[0:30, :]), r=[b.r], w=[CSO.r])
                dma_out(cs_s[l, s], CSO.t[0:30, :], CSO.r)

        areset()
        QT = aal("QT", [4, TT], BF16); KT = aal("KT", [4, TT], BF16); VT = aal("VT", [8, TT], BF16)
        OG = aal("OG", [8, TT], BF16)
        T1 = aal("T1", [TT]); T2 = aal("T2", [TT])
        GT = aal("GT", [TT], F32, parts=8)
        GTOK = aal("GTOK", [4, 8], F32, parts=64)
        SP_ = aal("SPt", [4, 4], F32, parts=64)
        NBT = aal("NBT", [4, 4], F32, parts=64)
        CTOK = aal("CTOK", [4, 4], F32, parts=64)
        WSK = aal("WSK", [4, 4], F32, parts=64); ETOK = aal("ETOK", [4, 4], F32, parts=64)
        CTR = aal("CTR", [TT], F32, parts=4)
        CMX = aal("CMX", [4], F32, parts=4); BL = aal("BL", [4], F32, parts=4)
        MPV = aal("MPV", [5], F32, parts=4); MCT = aal("MCT", [4], F32, parts=4); DA = aal("DA", [4], F32, parts=4)
        MNEW = aal("MNEW", [4], F32, parts=4)
        ZZ = aal("ZZ", [4, 4], F32, parts=4)
        DEC = aal("DEC", [16], F32)
        VS = aal("VS", [4, 257], BF16, parts=64); KTOK = aal("KTOK", [512], BF16, parts=64)
        QKM = aal("QKM", [4, 64], BF16, parts=64); HN = aal("HN", [4, 256], BF16, parts=64)
        CBF = aal("CBF", [4, 257], BF16)
        AD = aal("AD", [4], F32, parts=64); RD = aal("RD", [4], F32, parts=64); SS = aal("SS", [4], F32, parts=64)
        SC = aal("SC", [4], F32, parts=64); JK = aal("JK", [256], BF16, parts=64)
        HTMP = aal("HTMP", [8, 64], F32)
        CNS = aal("CNS", [4, 257], F32)
        for h in range(4):
            b = projW(O_MQ + h * 128, 128)
            A("act", lambda e, b=b, h=h: e.activation(out=QT.t[:, h, TK], in_=b.t[:, TK], func=AF.Identity,
                                                      bias=bias(l, O_MQ + h * 128), scale=1.0), r=[b.r, BIN.r], w=[QT.r])
            b = projW(O_MK + h * 128, 128)
            A("dve", lambda e, b=b, h=h: e.tensor_scalar(out=KT.t[:, h, TK], in0=b.t[:, TK], scalar1=bias(l, O_MK + h * 128),
                                                         scalar2=128.0 ** -0.5, op0=ALU.add, op1=ALU.mult), r=[b.r, BIN.r], w=[KT.r])
        for c in range(8):
            b = projW(O_MV + c * 128, 128)
            A("act", lambda e, b=b, c=c: e.activation(out=VT.t[:, c, TK], in_=b.t[:, TK], func=AF.Identity,
                                                      bias=bias(l, O_MV + c * 128), scale=1.0), r=[b.r, BIN.r], w=[VT.r])
            bo = projW(O_MO + c * 128, 128)
            bz = projW(O_MZ + c * 128, 128)
            A("act", lambda e, bo=bo, c=c: e.activation(out=T1.t[:, TK], in_=bo.t[:, TK], func=AF.Sigmoid,
                                                        bias=bias(l, O_MO + c * 128), scale=1.0), r=[bo.r, BIN.r], w=[T1.r])
            A("act", lambda e, bz=bz, c=c: e.activation(out=T2.t[:, TK], in_=bz.t[:, TK], func=AF.Silu,
                                                        bias=bias(l, O_MZ + c * 128), scale=1.0), r=[bz.r, BIN.r], w=[T2.r])
            A("dve", lambda e, c=c: e.tensor_tensor(out=OG.t[:, c, TK], in0=T1.t[:, TK], in1=T2.t[:, TK], op=ALU.mult),
              r=[T1.r, T2.r], w=[OG.r])
        b = projW(O_MIF, 8)
        A("act", lambda e, b=b: e.activation(out=GT.t[0:8, TK], in_=b.t[0:8, TK], func=AF.Identity, bias=bias(l, O_MIF, 8), scale=1.0),
          r=[b.r, BIN.r], w=[GT.r])
        b = bank()
        for j in range(nch):
            tr(b.t[0:L, j * 8:(j + 1) * 8], GT.t[0:8, j * L:(j + 1) * L], IDF.t[0:8, 0:8], [GT.r, IDF.r], [b.r])
        A("dve", lambda e, b=b: e.tensor_copy(out=GTOK.t[0:L], in_=b.t[0:L, 0:32].rearrange("p (j g) -> p j g", j=4)), r=[b.r], w=[GTOK.r])
        A("act", lambda e: e.activation(out=SP_.t[0:L], in_=GTOK.t[0:L, :, 4:8], func=AF.Exp, scale=-1.0), r=[GTOK.r], w=[SP_.r])
        A("act", lambda e: e.activation(out=SP_.t[0:L], in_=SP_.t[0:L], func=AF.Ln, bias=1.0, scale=1.0), r=[SP_.r], w=[SP_.r])
        b1 = bank()
        mm(b1.t[0:L, 0:16], UTF.t[0:L, 0:L], SP_.t[0:L].rearrange("p j h -> p (j h)"), True, True, [UTF.r, SP_.r], [b1.r])
        A("dve", lambda e: e.tensor_copy(out=NBT.t[0:L].rearrange("p j h -> p (j h)"), in_=b1.t[0:L, 0:16]), r=[b1.r], w=[NBT.r])
        A("dve", lambda e: e.tensor_tensor(out=CTOK.t[0:L], in0=GTOK.t[0:L, :, 0:4], in1=NBT.t[0:L], op=ALU.add),
          r=[GTOK.r, NBT.r], w=[CTOK.r])
        b2 = bank()
        for j in range(nch):
            mm(b2.t[0:4, j * L:(j + 1) * L], SP_.t[0:L, j, :], UTF.t[0:L, 0:L], True, True, [SP_.r, UTF.r], [b2.r])
        A("dve", lambda e: e.tensor_tensor(out=CTR.t[0:4, TK], in0=GT.t[0:4, TK], in1=b2.t[0:4, TK], op=ALU.add), r=[GT.r, b2.r], w=[CTR.r])
        A("dve", lambda e: e.tensor_reduce(out=CMX.t[0:4, :], in_=CTR.t[0:4, TK].rearrange("p (j t) -> p j t", j=4), axis=AX.X, op=ALU.max),
          r=[CTR.r], w=[CMX.r])
        A("dve", lambda e: e.tensor_scalar(out=BL.t[0:4, :], in0=b2.t[0:4, TK].rearrange("p (j t) -> p j t", j=4)[:, :, L - 1],
                                           scalar1=-1.0, scalar2=None, op0=ALU.mult), r=[b2.r], w=[BL.r])
        if kind == "p":
            A("dve", lambda e: e.tensor_copy(out=MPV.t[0:4, 0:1], in_=MRP[l].t[:]), r=[MRP[l].r], w=[MPV.r])
            for j in range(nch):
                A("dve", lambda e, j=j: e.tensor_tensor(out=MCT.t[0:4, j:j + 1], in0=MPV.t[0:4, j:j + 1], in1=CMX.t[0:4, j:j + 1], op=ALU.max),
                  r=[MPV.r, CMX.r], w=[MCT.r])
                A("dve", lambda e, j=j: e.tensor_tensor(out=MPV.t[0:4, j + 1:j + 2], in0=BL.t[0:4, j:j + 1], in1=MCT.t[0:4, j:j + 1], op=ALU.add),
                  r=[BL.r, MCT.r], w=[MPV.r])
            A("dve", lambda e: e.tensor_copy(out=MRP[l].t[:], in_=MPV.t[0:4, 4:5]), r=[MPV.r], w=[MRP[l].r])
            if last:
                dma_out(m_p[l].rearrange("(h o) -> h o", o=1), MRP[l].t[:], MRP[l].r, slow=True)
        else:
            dma_in(MPV.t[0:4, 0:4], stm[l].rearrange("s h -> h s"), MPV.r, slow=True)
            A("dve", lambda e: e.tensor_tensor(out=MCT.t[0:4, :], in0=MPV.t[0:4, 0:4], in1=CMX.t[0:4, :], op=ALU.max),
              r=[MPV.r, CMX.r], w=[MCT.r])
            A("dve", lambda e: e.tensor_tensor(out=MNEW.t[0:4, :], in0=BL.t[0:4, :], in1=MCT.t[0:4, :], op=ALU.add),
              r=[BL.r, MCT.r], w=[MNEW.r])
            dma_out(m_s[l].rearrange("s h -> h s"), MNEW.t[0:4, :], MNEW.r, slow=True)
        A("dve", lambda e: e.tensor_tensor(out=DA.t[0:4, :], in0=MPV.t[0:4, 0:4], in1=MCT.t[0:4, :], op=ALU.subtract),
          r=[MPV.r, MCT.r], w=[DA.r])
        A("act", lambda e: e.activation(out=DA.t[0:4, :], in_=DA.t[0:4, :], func=AF.Exp), r=[DA.r], w=[DA.r])

        def bcast_rows(src_tb, nparts):
            A("dve", lambda e: e.tensor_tensor(out=ZZ.t[0:4], in0=src_tb.t[0:4, :].unsqueeze(2).broadcast_to([4, 4, 4]),
                                               in1=EYE4.t[0:4, :].unsqueeze(1).broadcast_to([4, 4, 4]), op=ALU.mult),
              r=[src_tb.r, EYE4.r], w=[ZZ.r])
            bb = bank()
            mm(bb.t[0:nparts, 0:16], ONF.t[0:4, 0:nparts], ZZ.t[0:4].rearrange("p j h -> p (j h)"), True, True, [ONF.r, ZZ.r], [bb.r])
            return bb

        if kind == "s" and l == 0:
            dbg("GT", GT.t[0:8, TK], GT.r, [8, 128]); dbg("GTOK", GTOK.t[0:L], GTOK.r, [32, 4, 8]); dbg("SP", SP_.t[0:L], SP_.r, [32, 4, 4])
            dbg("NBT", NBT.t[0:L], NBT.r, [32, 4, 4]); dbg("CTR", CTR.t[0:4, TK], CTR.r, [4, 128]); dbg("CMX", CMX.t[0:4, :], CMX.r, [4, 4])
            dbg("BL", BL.t[0:4, :], BL.r, [4, 4]); dbg("MCT", MCT.t[0:4, :], MCT.r, [4, 4]); dbg("MPV", MPV.t[0:4, 0:4], MPV.r, [4, 4])
        bm = bcast_rows(MCT, 64)
        A("dve", lambda e: e.tensor_tensor(out=WSK.t[0:L].rearrange("p j h -> p (j h)"), in0=CTOK.t[0:L].rearrange("p j h -> p (j h)"),
                                           in1=bm.t[0:L, 0:16], op=ALU.subtract), r=[CTOK.r, bm.r], w=[WSK.r])
        A("act", lambda e: e.activation(out=WSK.t[0:L], in_=WSK.t[0:L], func=AF.Exp), r=[WSK.r], w=[WSK.r])
        A("dve", lambda e: e.tensor_tensor(out=ETOK.t[0:L].rearrange("p j h -> p (j h)"), in0=NBT.t[0:L].rearrange("p j h -> p (j h)"),
                                           in1=bm.t[0:L, 0:16], op=ALU.subtract), r=[NBT.r, bm.r], w=[ETOK.r])
        A("act", lambda e: e.activation(out=ETOK.t[0:L], in_=ETOK.t[0:L], func=AF.Exp), r=[ETOK.r], w=[ETOK.r])
        bd = bcast_rows(DA, 128)
        A("dve", lambda e: e.tensor_copy(out=DEC.t[:, :], in_=bd.t[:, 0:16]), r=[bd.r], w=[DEC.r])

        for j in range(nch):
            cs_ = slice(j * L, (j + 1) * L)
            if kind == "p":
                S = CNP[l]
            else:
                S = CNS
                dma_in(S.t[:, :, 0:256], stC[l, j].rearrange("h k v -> k h v"), S.r)
                dma_in(S.t[:, :, 256], stn[l, j].rearrange("h k -> k h"), S.r, slow=True)
            A("dve", lambda e, S=S, j=j: e.tensor_tensor(out=S.t[:], in0=S.t[:],
                                                         in1=DEC.t[:, j * 4:(j + 1) * 4].unsqueeze(2).broadcast_to([128, 4, 257]), op=ALU.mult),
              r=[S.r, DEC.r], w=[S.r])
            A("act", lambda e, S=S: e.copy(out=CBF.t[:], in_=S.t[:]), r=[S.r], w=[CBF.r])
            bq = bank()
            for h in range(4):
                mm(bq.t[0:L, h * L:(h + 1) * L], KT.t[:, h, cs_], QT.t[:, h, cs_], True, True, [KT.r, QT.r], [bq.r])
            A("dve", lambda e, bq=bq: e.tensor_tensor(out=QKM.t[0:L, :, 0:L], in0=bq.t[0:L, 0:4 * L].rearrange("p (h t) -> p h t", h=4),
                                                      in1=UTF.t[0:L, 0:L].unsqueeze(1).broadcast_to([L, 4, L]), op=ALU.mult),
              r=[bq.r, UTF.r], w=[QKM.r])
            bv = bank()
            for c in range(8):
                tr(bfv(bv)[0:L, c * 128:(c + 1) * 128], VT.t[:, c, cs_], IDB.t[:], [VT.r, IDB.r], [bv.r])
            A("dve", lambda e, bv=bv, j=j: e.tensor_tensor(out=VS.t[0:L, :, 0:256], in0=bfv(bv)[0:L, 0:1024].rearrange("p (h v) -> p h v", h=4),
                                                           in1=WSK.t[0:L, j, :].unsqueeze(2).broadcast_to([L, 4, 256]), op=ALU.mult),
              r=[bv.r, WSK.r], w=[VS.r])
            A("dve", lambda e, j=j: e.tensor_copy(out=VS.t[0:L, :, 256], in_=WSK.t[0:L, j, :]), r=[WSK.r], w=[VS.r])
            bk = bank()
            for h in range(4):
                tr(bfv(bk)[0:L, h * 128:(h + 1) * 128], KT.t[:, h, cs_], IDB.t[:], [KT.r, IDB.r], [bk.r])
            A("act", lambda e, bk=bk: e.copy(out=KTOK.t[0:L, :], in_=bfv(bk)[0:L, 0:512]), r=[bk.r], w=[KTOK.r])
            bn = [bank(), bank()]
            bden = bank()
            for h in range(4):
                o_ = bn[h // 2].t[0:L, (h % 2) * 256:(h % 2) * 256 + 256]
                mm(o_, QKM.t[0:L, h, 0:L], VS.t[0:L, h, 0:256], True, False, [QKM.r, VS.r], [bn[h // 2].r])
                mm(o_, QT.t[:, h, cs_], CBF.t[:, h, 0:256], False, True, [QT.r, CBF.r], [bn[h // 2].r])
                mm(bden.t[0:L, h:h + 1], QKM.t[0:L, h, 0:L], VS.t[0:L, h, 256:257], True, False, [QKM.r, VS.r], [bden.r])
                mm(bden.t[0:L, h:h + 1], QT.t[:, h, cs_], CBF.t[:, h, 256:257], False, True, [QT.r, CBF.r], [bden.r])
            A("act", lambda e, bden=bden: e.activation(out=AD.t[0:L, :], in_=bden.t[0:L, 0:4], func=AF.Abs), r=[bden.r], w=[AD.r])
            A("dve", lambda e, j=j: e.tensor_tensor(out=AD.t[0:L, :], in0=AD.t[0:L, :], in1=ETOK.t[0:L, j, :], op=ALU.max),
              r=[AD.r, ETOK.r], w=[AD.r])
            A("dve", lambda e: e.reciprocal(out=RD.t[0:L, :], in_=AD.t[0:L, :]), r=[AD.r], w=[RD.r])
            for h in range(4):
                A("act", lambda e, h=h, bn=bn: e.activation(out=JK.t[0:L, :], in_=bn[h // 2].t[0:L, (h % 2) * 256:(h % 2) * 256 + 256],
                                                            func=AF.Square, scale=RD.t[0:L, h:h + 1], accum_out=SS.t[0:L, h:h + 1]),
                  r=[bn[h // 2].r, RD.r], w=[JK.r, SS.r])
            A("dve", lambda e: e.tensor_scalar(out=SS.t[0:L, :], in0=SS.t[0:L, :], scalar1=1.0 / 256, scalar2=EPS, op0=ALU.mult, op1=ALU.add),
              r=[SS.r], w=[SS.r])
            A("act", lambda e: e.activation(out=SS.t[0:L, :], in_=SS.t[0:L, :], func=AF.Sqrt), r=[SS.r], w=[SS.r])
            A("dve", lambda e: e.reciprocal(out=SS.t[0:L, :], in_=SS.t[0:L, :]), r=[SS.r], w=[SS.r])
            A("dve", lambda e: e.tensor_tensor(out=SC.t[0:L, :], in0=SS.t[0:L, :], in1=RD.t[0:L, :], op=ALU.mult), r=[SS.r, RD.r], w=[SC.r])
            for g in range(2):
                A("dve", lambda e, g=g, bn=bn: e.tensor_tensor(out=HN.t[0:L, 2 * g:2 * g + 2, :],
                                                               in0=bn[g].t[0:L, 0:512].rearrange("p (h v) -> p h v", h=2),
                                                               in1=SC.t[0:L, 2 * g:2 * g + 2].unsqueeze(2).broadcast_to([L, 2, 256]), op=ALU.mult),
                  r=[bn[g].r, SC.r], w=[HN.r])
            bt_ = bank()
            HNf = HN.t[0:L].rearrange("p h v -> p (h v)")
            for c in range(8):
                tr(bfv(bt_)[:, c * L:(c + 1) * L], HNf[:, c * 128:(c + 1) * 128], IDB.t[0:L, 0:L], [HN.r, IDB.r], [bt_.r])
            A("dve", lambda e, bt_=bt_: e.tensor_tensor(out=HTMP.t[:, :, 0:L], in0=bfv(bt_)[:, 0:8 * L].rearrange("p (c t) -> p c t", c=8),
                                                        in1=MNG.t[:, l, :].unsqueeze(2).broadcast_to([128, 8, L]), op=ALU.mult),
              r=[bt_.r, MNG.r], w=[HTMP.r])
            A("dve", lambda e, cs_=cs_: e.tensor_tensor(out=HMG.t[:, :, cs_], in0=HTMP.t[:, :, 0:L], in1=OG.t[:, :, cs_], op=ALU.mult),
              r=[HTMP.r, OG.r], w=[HMG.r])
            bs = [bank(), bank()]
            bsn = bank()
            for h in range(4):
                mm(bs[h // 2].t[:, (h % 2) * 256:(h % 2) * 256 + 256], KTOK.t[0:L, h * 128:(h + 1) * 128], VS.t[0:L, h, 0:256], True, True,
                   [KTOK.r, VS.r], [bs[h // 2].r])
                mm(bsn.t[:, h:h + 1], KTOK.t[0:L, h * 128:(h + 1) * 128], VS.t[0:L, h, 256:257], True, True, [KTOK.r, VS.r], [bsn.r])
            for g in range(2):
                A("dve", lambda e, g=g, bs=bs, S=S: e.tensor_tensor(out=S.t[:, 2 * g:2 * g + 2, 0:256], in0=S.t[:, 2 * g:2 * g + 2, 0:256],
                                                                    in1=bs[g].t[:, 0:512].rearrange("p (h v) -> p h v", h=2), op=ALU.add),
                  r=[S.r, bs[g].r], w=[S.r])
            A("dve", lambda e, bsn=bsn, S=S: e.tensor_tensor(out=S.t[:, :, 256], in0=S.t[:, :, 256], in1=bsn.t[:, 0:4], op=ALU.add),
              r=[S.r, bsn.r], w=[S.r])
            if kind == "s":
                dma_out(C_s[l, j].rearrange("h k v -> k h v"), S.t[:, :, 0:256], S.r)
                dma_out(n_s[l, j].rearrange("h k -> k h"), S.t[:, :, 256], S.r, slow=True)
        if kind == "p" and last:
            dma_out(C_p[l].rearrange("h k v -> k h v"), CNP[l].t[:, :, 0:256], CNP[l].r)
            dma_out(n_p[l].rearrange("h k -> k h"), CNP[l].t[:, :, 256], CNP[l].r, slow=True)

        areset()
        ACQ = aal("ACQ", [8, TT], BF16)
        CKV32 = aal("CKV32", [4, TT]); CKVB = aal("CKVB", [4, TT], BF16)
        KR32 = aal("KR32", [TT], F32, parts=64); KRB = aal("KRB", [TT], BF16, parts=64)
        SQ = aal("SQ", [512], BF16); SQF = aal("SQF", [TT], F32, parts=64)
        R1 = aal("R1", [512]); R2 = aal("R2", [TT], F32, parts=64)
        TA = aal("TA", [TT], F32, parts=64); TBb = aal("TBb", [TT], F32, parts=64)
        LATO = aal("LATO", [512]); KRO = aal("KRO", [64])
        QR32 = aal("QR32", [TT], F32, parts=64); QRNB = aal("QRNB", [TT], BF16, parts=64)
        PTs = [aal(f"PT{i}", [TT], BF16) for i in range(3)]
        RDEN = aal("RDEN", [TT]); TO = aal("TO", [TT])
        KRP = aal("KRP", [SEQ], BF16, parts=64)
        if kind == "s":
            LPB = aal("LPB", [8, 512], BF16); LATT = aal("LATT", [4, PAST], BF16); KRPB = aal("KRPB", [8, 64], BF16)
        if kind == "p":
            dma_in(CS.t[:, :, :], c_csp[:, :, pos0:pos0 + TT], CS.r)
        else:
            dma_in(CS.t[:, :, 0:128], c_css[:, :, :], CS.r)
        COS = CS.t[:, 0, TK]; SIN = CS.t[:, 1, TK]

        def rms_bcast(src_bank, nparts, n, gain_ap, dst, dst_r, ncols=TK, fp32sq=False):
            sq = SQF if fp32sq else SQ
            A("act", lambda e: e.activation(out=sq.t[0:nparts, ncols], in_=src_bank.t[0:nparts, ncols], func=AF.Square), r=[src_bank.r], w=[sq.r])
            b2_ = bank()
            on = ONF if fp32sq else ONB
            mm(b2_.t[0:nparts, ncols], on.t[0:nparts, 0:nparts], sq.t[0:nparts, ncols], True, True, [on.r, sq.r], [b2_.r])
            rr = R2 if nparts == 64 else R1
            A("dve", lambda e: e.tensor_scalar(out=rr.t[0:nparts, ncols], in0=b2_.t[0:nparts, ncols], scalar1=1.0 / n, scalar2=EPS,
                                               op0=ALU.mult, op1=ALU.add), r=[b2_.r], w=[rr.r])
            A("act", lambda e: e.activation(out=rr.t[0:nparts, ncols], in_=rr.t[0:nparts, ncols], func=AF.Sqrt), r=[rr.r], w=[rr.r])
            A("dve", lambda e: e.reciprocal(out=rr.t[0:nparts, ncols], in_=rr.t[0:nparts, ncols]), r=[rr.r], w=[rr.r])
            A("dve", lambda e: e.scalar_tensor_tensor(out=dst, in0=src_bank.t[0:nparts, ncols], scalar=gain_ap, in1=rr.t[0:nparts, ncols],
                                                      op0=ALU.mult, op1=ALU.mult), r=[src_bank.r, rr.r], w=[dst_r])

        bR = abank(0)
        for c in range(8):
            b = projW(O_ACQ + c * 128, 128)
            A("act", lambda e, b=b, c=c: e.activation(out=ACQ.t[:, c, TK], in_=b.t[:, TK], func=AF.Identity,
                                                      bias=bias(l, O_ACQ + c * 128), scale=1.0), r=[b.r, BIN.r], w=[ACQ.r])
            A("act", lambda e, b=b, c=c: e.activation(out=SQ.t[:, TK], in_=b.t[:, TK], func=AF.Square,
                                                      bias=bias(l, O_ACQ + c * 128), scale=1.0), r=[b.r, BIN.r], w=[SQ.r])
            mm(bR.t[:, TK], ONB.t[:], SQ.t[:, TK], c == 0, c == 7, [ONB.r, SQ.r], [bR.r])
        rstd_from(TB(R1.t[:, TK], R1.r), bR.t[:, TK], 1.0 / 1024, bR.r)
        for c in range(8):
            A("dve", lambda e, c=c: e.scalar_tensor_tensor(out=ACQ.t[:, c, TK], in0=ACQ.t[:, c, TK], scalar=CQG.t[:, l, c:c + 1],
                                                           in1=R1.t[:, TK], op0=ALU.mult, op1=ALU.mult), r=[ACQ.r, R1.r, CQG.r], w=[ACQ.r])
        bR = abank(1)
        for c in range(4):
            b = projW(O_ACKV + c * 128, 128)
            A("act", lambda e, b=b, c=c: e.activation(out=CKV32.t[:, c, TK], in_=b.t[:, TK], func=AF.Identity,
                                                      bias=bias(l, O_ACKV + c * 128), scale=1.0), r=[b.r, BIN.r], w=[CKV32.r])
            A("act", lambda e, c=c: e.activation(out=SQ.t[:, TK], in_=CKV32.t[:, c, TK], func=AF.Square), r=[CKV32.r], w=[SQ.r])
            mm(bR.t[:, TK], ONB.t[:], SQ.t[:, TK], c == 0, c == 3, [ONB.r, SQ.r], [bR.r])
        rstd_from(TB(R1.t[:, TK], R1.r), bR.t[:, TK], 1.0 / 512, bR.r)
        for c in range(4):
            A("dve", lambda e, c=c: e.scalar_tensor_tensor(out=CKV32.t[:, c, TK], in0=CKV32.t[:, c, TK], scalar=CKVG.t[:, l, c:c + 1],
                                                           in1=R1.t[:, TK], op0=ALU.mult, op1=ALU.mult), r=[CKV32.r, R1.r, CKVG.r], w=[CKV32.r])
        A("act", lambda e: e.copy(out=CKVB.t[:, :, TK], in_=CKV32.t[:, :, TK]), r=[CKV32.r], w=[CKVB.r])
        for j in range(nsub):
            b = bank()
            for c in range(4):
                tr(b.t[:, c * 128:(c + 1) * 128], CKV32.t[:, c, j * 128:(j + 1) * 128], IDF.t[:], [CKV32.r, IDF.r], [b.r])
            A("act", lambda e, b=b: e.copy(out=LATO.t[:, :], in_=b.t[:, :]), r=[b.r], w=[LATO.r])
            dst = lat_p[l, pos0 + j * 128:pos0 + (j + 1) * 128, :] if kind == "p" else lat_s[l]
            dma_out(dst, LATO.t[:, :], LATO.r)
        b = projW(O_AKR, 64)
        A("act", lambda e, b=b: e.activation(out=KR32.t[0:64, TK], in_=b.t[0:64, TK], func=AF.Identity, bias=bias(l, O_AKR, 64), scale=1.0),
          r=[b.r, BIN.r], w=[KR32.r])
        A("act", lambda e: e.activation(out=SQF.t[0:64, TK], in_=KR32.t[0:64, TK], func=AF.Square), r=[KR32.r], w=[SQF.r])
        b2 = bank()
        mm(b2.t[0:64, TK], ONF.t[0:64, 0:64], SQF.t[0:64, TK], True, True, [ONF.r, SQF.r], [b2.r])
        rstd_from(TB(R2.t[0:64, TK], R2.r), b2.t[0:64, TK], 1.0 / 64, b2.r)
        A("dve", lambda e: e.scalar_tensor_tensor(out=KR32.t[0:64, TK], in0=KR32.t[0:64, TK], scalar=KRG.t[:, l:l + 1], in1=R2.t[0:64, TK],
                                                  op0=ALU.mult, op1=ALU.mult), r=[KR32.r, R2.r, KRG.r], w=[KR32.r])
        b3 = bank()
        mm(b3.t[0:64, TK], ROTF.t[:], KR32.t[0:64, TK], True, True, [ROTF.r, KR32.r], [b3.r])
        A("dve", lambda e: e.tensor_tensor(out=TA.t[0:64, TK], in0=KR32.t[0:64, TK], in1=COS, op=ALU.mult), r=[KR32.r, CS.r], w=[TA.r])
        A("dve", lambda e: e.tensor_tensor(out=TBb.t[0:64, TK], in0=b3.t[0:64, TK], in1=SIN, op=ALU.mult), r=[b3.r, CS.r], w=[TBb.r])
        A("dve", lambda e: e.tensor_tensor(out=KR32.t[0:64, TK], in0=TA.t[0:64, TK], in1=TBb.t[0:64, TK], op=ALU.add), r=[TA.r, TBb.r], w=[KR32.r])
        A("act", lambda e: e.copy(out=KRB.t[0:64, TK], in_=KR32.t[0:64, TK]), r=[KR32.r], w=[KRB.r])
        for j in range(nsub):
            b = bank()
            tr(b.t[:, 0:64], KR32.t[0:64, j * 128:(j + 1) * 128], IDF.t[0:64, 0:64], [KR32.r, IDF.r], [b.r])
            A("act", lambda e, b=b: e.copy(out=KRO.t[:, :], in_=b.t[:, 0:64]), r=[b.r], w=[KRO.r])
            dst = kr_p[l, pos0 + j * 128:pos0 + (j + 1) * 128, :] if kind == "p" else kr_s[l]
            dma_out(dst, KRO.t[:, :], KRO.r)
        RKR = krc_res[l]
        if kind == "p":
            if not last:
                dma_out(KRC[l][:, pos0:pos0 + TT], KRB.t[0:64, TK], KRB.r, final=False, extra_w=[RKR])
            if ti > 0:
                P.op("sp", lambda e: e.dma_start(out=KRP.t[0:64, 0:pos0], in_=KRC[l][:, 0:pos0]), r=[RKR], w=[KRP.r], dma=True, semres=KRP.r)

        SETS = []
        for i_ in range(2):
            SETS.append(dict(QNB=aal(f"QNB{i_}", [TT], BF16), QRB=aal(f"QRB{i_}", [TT], BF16, parts=64), KNB=aal(f"KNB{i_}", [TT], BF16),
                             VB=aal(f"VB{i_}", [4, 128], BF16), SZ=aal(f"SZ{i_}", [TT]),
                             KP=aal(f"KP{i_}", [SEQ if kind == "p" else PAST], BF16),
                             VP=aal(f"VP{i_}", [16 if kind == "p" else 8, 128], BF16)))
        SQ2 = aal("SQ2", [512], BF16); R1b = aal("R1b", [512])

        def rms2a(src_bank, nparts, n, ncols, sq, rr, on):
            A("act", lambda e: e.activation(out=sq.t[0:nparts, ncols], in_=src_bank.t[0:nparts, ncols], func=AF.Square), r=[src_bank.r], w=[sq.r])
            b2_ = bank()
            mm(b2_.t[0:nparts, ncols], on.t[0:nparts, 0:nparts], sq.t[0:nparts, ncols], True, True, [on.r, sq.r], [b2_.r])
            A("dve", lambda e: e.tensor_scalar(out=rr.t[0:nparts, ncols], in0=b2_.t[0:nparts, ncols], scalar1=1.0 / n, scalar2=EPS,
                                               op0=ALU.mult, op1=ALU.add), r=[b2_.r], w=[rr.r])

        def rms2b(src_bank, nparts, gain_ap, dst, dst_r, ncols, rr):
            A("act", lambda e: e.activation(out=rr.t[0:nparts, ncols], in_=rr.t[0:nparts, ncols], func=AF.Sqrt), r=[rr.r], w=[rr.r])
            A("dve", lambda e: e.reciprocal(out=rr.t[0:nparts, ncols], in_=rr.t[0:nparts, ncols]), r=[rr.r], w=[rr.r])
            A("dve", lambda e: e.scalar_tensor_tensor(out=dst, in0=src_bank.t[0:nparts, ncols], scalar=gain_ap, in1=rr.t[0:nparts, ncols],
                                                      op0=ALU.mult, op1=ALU.mult), r=[src_bank.r, rr.r], w=[dst_r])

        def rms2(src_bank, nparts, n, gain_ap, dst, dst_r, ncols, sq, rr, on):
            rms2a(src_bank, nparts, n, ncols, sq, rr, on)
            rms2b(src_bank, nparts, gain_ap, dst, dst_r, ncols, rr)

        def stageA(h, cols, S):
            QNB_, QRB_, KNB_, VB_, SZ_, KP_, VP_ = S["QNB"], S["QRB"], S["KNB"], S["VB"], S["SZ"], S["KP"], S["VP"]
            uq, uqr = wload(w_uq[l][:, h * 192:(h + 1) * 192], 8, 192, ("w_uq", l, h))
            ukv, ukvr = wload(w_ukv[l][:, h * 256:(h + 1) * 256], 4, 256, ("w_ukv", l, h))
            COSc = CS.t[:, 0, cols]; SINc = CS.t[:, 1, cols]
            blocks = []
            if kind == "p" and ti > 0:
                RC = kvc_res[l][h]
                P.op("sp", lambda e: e.dma_start(out=KP_.t[:, 0:pos0], in_=KC[l, h][:, 0:pos0]), r=[RC], w=[KP_.r], dma=True, semres=KP_.r)
                P.op("sp", lambda e: e.dma_start(out=VP_.t[:, 0:pos0 // 128, :],
                                                 in_=VC[l, h][0:pos0, :].rearrange("(j p) v -> p j v", p=128)),
                     r=[RC], w=[VP_.r], dma=True, semres=VP_.r)
            bqn = bank(pin=True)
            for kc in range(8):
                mm(bqn.t[:, cols], uq[:, kc, 0:128], ACQ.t[:, kc, cols], kc == 0, kc == 7, [uqr, ACQ.r], [bqn.r])
            bqr = bank(pin=True)
            for kc in range(8):
                mm(bqr.t[0:64, cols], uq[:, kc, 128:192], ACQ.t[:, kc, cols], kc == 0, kc == 7, [uqr, ACQ.r], [bqr.r])
            bkn = bank(pin=True)
            for kc in range(4):
                mm(bkn.t[:, cols], ukv[:, kc, 0:128], CKVB.t[:, kc, cols], kc == 0, kc == 3, [ukvr, CKVB.r], [bkn.r])
            bv = bank()
            if kind == "p":
                for j in range(nsub):
                    for kc in range(4):
                        mm(bv.t[:, j * 128:(j + 1) * 128], CKVB.t[:, kc, j * 128:(j + 1) * 128], ukv[:, kc, 128:256], kc == 0, kc == 3,
                           [ukvr, CKVB.r], [bv.r])
                A("act", lambda e: e.copy(out=VB_.t[:, 0:nsub, :], in_=bv.t[:, 0:nsub * 128].rearrange("p (j v) -> p j v", j=nsub)),
                  r=[bv.r], w=[VB_.r])
            else:
                for kc in range(4):
                    mm(bv.t[0:32, 0:128], CKVB.t[:, kc, cols], ukv[:, kc, 128:256], kc == 0, kc == 3, [ukvr, CKVB.r], [bv.r])
                A("act", lambda e: e.copy(out=VB_.t[0:32, 0, :], in_=bv.t[0:32, 0:128]), r=[bv.r], w=[VB_.r])
            bz = projW(O_AZ + h * 128, 128)
            A("act", lambda e: e.activation(out=SZ_.t[:, TK], in_=bz.t[:, TK], func=AF.Silu, bias=bias(l, O_AZ + h * 128), scale=1.0),
              r=[bz.r, BIN.r], w=[SZ_.r])
            rms2a(bqn, 128, 128, cols, SQ, R1, ONB)
            rms2a(bkn, 128, 128, cols, SQ2, R1b, ONB)
            rms2a(bqr, 64, 64, cols, SQF, R2, ONF)
            rms2b(bqn, 128, QNG.t[:, l:l + 1], QNB_.t[:, cols], QNB_.r, cols, R1)
            unpin(bqn)
            rms2b(bkn, 128, KNG.t[:, l:l + 1], KNB_.t[:, cols], KNB_.r, cols, R1b)
            unpin(bkn)
            rms2b(bqr, 64, QRG.t[:, l:l + 1], QR32.t[0:64, cols], QR32.r, cols, R2)
            unpin(bqr)
            A("act", lambda e: e.copy(out=QRNB.t[0:64, cols], in_=QR32.t[0:64, cols]), r=[QR32.r], w=[QRNB.r])
            return (h, cols, S, uq, uqr, ukv, ukvr)

        def stageA3(h, cols, S, uq, uqr, ukv, ukvr):
            QNB_, QRB_, KNB_, VB_, SZ_, KP_, VP_ = S["QNB"], S["QRB"], S["KNB"], S["VB"], S["SZ"], S["KP"], S["VP"]
            COSc = CS.t[:, 0, cols]; SINc = CS.t[:, 1, cols]
            blocks = []
            b3 = bank()
            mm(b3.t[0:64, cols], ROTB.t[:], QRNB.t[0:64, cols], True, True, [ROTB.r, QRNB.r], [b3.r])
            A("dve", lambda e: e.tensor_tensor(out=TA.t[0:64, cols], in0=QR32.t[0:64, cols], in1=COSc, op=ALU.mult), r=[QR32.r, CS.r], w=[TA.r])
            A("dve", lambda e: e.tensor_tensor(out=TBb.t[0:64, cols], in0=b3.t[0:64, cols], in1=SINc, op=ALU.mult), r=[b3.r, CS.r], w=[TBb.r])
            A("dve", lambda e: e.tensor_tensor(out=QRB_.t[0:64, cols], in0=TA.t[0:64, cols], in1=TBb.t[0:64, cols], op=ALU.add),
              r=[TA.r, TBb.r], w=[QRB_.r])
            if kind == "p":
                RC = kvc_res[l][h]
                if not last:
                    dma_out(KC[l, h][:, pos0:pos0 + TT], KNB_.t[:, TK], KNB_.r, final=False, extra_w=[RC])
                    dma_out(VC[l, h][pos0:pos0 + TT, :].rearrange("(j p) v -> p j v", p=128), VB_.t[:, 0:nsub, :], VB_.r, final=False, extra_w=[RC])
                for pb in range(pos0 // 128):
                    blocks.append((KP_.t[:, pb * 128:(pb + 1) * 128], KRP.t[0:64, pb * 128:(pb + 1) * 128], VP_.t[:, pb, :], 128, 0, None,
                                   [KP_.r, KRP.r, VP_.r]))
                for kb in range(nsub):
                    blocks.append((KNB_.t[:, kb * 128:(kb + 1) * 128], KRB.t[0:64, kb * 128:(kb + 1) * 128], VB_.t[:, kb, :], 128, kb * 128,
                                   AMB.t[:, kb, kb * 128:TT], [KNB_.r, KRB.r, VB_.r]))
            else:
                for half in range(2):
                    b = bank(pin=True)
                    for kc in range(4):
                        mm(b.t[:, :], ukv[:, kc, 0:128], LATT.t[:, kc, half * 512:(half + 1) * 512], kc == 0, kc == 3, [ukvr, LATT.r], [b.r])
                    rms2(b, 128, 128, KNG.t[:, l:l + 1], KP_.t[:, half * 512:(half + 1) * 512], KP_.r, slice(0, 512),
                         SQ if half == 0 else SQ2, R1 if half == 0 else R1b, ONB)
                    unpin(b)
                for g in range(2):
                    b = bank()
                    for q in range(4):
                        kb = 4 * g + q
                        for kc in range(4):
                            mm(b.t[:, q * 128:(q + 1) * 128], LATT.t[:, kc, kb * 128:(kb + 1) * 128], ukv[:, kc, 128:256], kc == 0, kc == 3,
                               [ukvr, LATT.r], [b.r])
                    A("act", lambda e, b=b, g=g: e.copy(out=VP_.t[:, 4 * g:4 * g + 4, :], in_=b.t[:, :].rearrange("p (q v) -> p q v", q=4)),
                      r=[b.r], w=[VP_.r])
                for pb in range(8):
                    blocks.append((KP_.t[:, pb * 128:(pb + 1) * 128], KRP.t[0:64, pb * 128:(pb + 1) * 128], VP_.t[:, pb, :], 128, 0, None,
                                   [KP_.r, KRP.r, VP_.r]))
                blocks.append((KNB_.t[:, cols], KRB.t[0:64, cols], VB_.t[0:32, 0, :], 32, 0, None, [KNB_.r, KRB.r, VB_.r]))
            return blocks

        def stageB(h, blocks, cols, S):
            QNB_, QRB_, SZ_ = S["QNB"], S["QRB"], S["SZ"]
            qs = cols.start
            nq = cols.stop - cols.start
            bO = abank(0); bD = abank(1)
            nb = len(blocks)
            pts = {}

            def s1(i):
                (kap, krap, vap, nk, q0, mk, rds) = blocks[i]
                qsl = slice(qs + q0, qs + nq)
                osl = slice(q0, nq)
                bS = bank()
                mm(bS.t[0:nk, osl], kap, QNB_.t[:, qsl], True, False, rds + [QNB_.r], [bS.r])
                mm(bS.t[0:nk, osl], krap, QRB_.t[0:64, qsl], False, True, rds + [QRB_.r], [bS.r])
                PT = PTs[i % 3]
                A("act", lambda e: e.activation(out=PT.t[0:nk, osl], in_=bS.t[0:nk, osl], func=AF.Exp, scale=ATTN_SCALE), r=[bS.r], w=[PT.r])
                if mk is not None:
                    A("dve", lambda e: e.tensor_tensor(out=PT.t[0:nk, osl], in0=PT.t[0:nk, osl], in1=mk, op=ALU.mult), r=[PT.r, AMB.r], w=[PT.r])
                pts[i] = PT

            def s2(i):
                (kap, krap, vap, nk, q0, mk, rds) = blocks[i]
                osl = slice(q0, nq)
                PT = pts[i]
                mm(bO.t[:, osl], vap, PT.t[0:nk, osl], i == 0, i == nb - 1, rds + [PT.r], [bO.r])
                mm(bD.t[:, osl], ONB.t[0:nk, :], PT.t[0:nk, osl], i == 0, i == nb - 1, [ONB.r, PT.r], [bD.r])

            s1(0)
            if nb > 1:
                s1(1)
            for i in range(nb):
                if i + 2 < nb:
                    s1(i + 2)
                s2(i)
            osl = slice(0, nq)
            A("dve", lambda e: e.reciprocal(out=RDEN.t[:, osl], in_=bD.t[:, osl]), r=[bD.r], w=[RDEN.r])
            A("dve", lambda e: e.tensor_tensor(out=TO.t[:, osl], in0=bO.t[:, osl], in1=RDEN.t[:, osl], op=ALU.mult), r=[bO.r, RDEN.r], w=[TO.r])
            A("dve", lambda e: e.tensor_tensor(out=AOG.t[:, h, qs:qs + nq], in0=TO.t[:, osl], in1=SZ_.t[:, qs:qs + nq], op=ALU.mult),
              r=[TO.r, SZ_.r], w=[AOG.r])

        def run_heads(cols):
            prev = None
            for h in range(16):
                S = SETS[h % 2]
                st_ = stageA(h, cols, S)
                if prev is not None:
                    stageB(*prev)
                blocks = stageA3(*st_)
                prev = (h, blocks, cols, S)
            stageB(*prev)

        if kind == "p":
            run_heads(TK)
        else:
            for s in range(4):
                P.op("pool", lambda e, s=s: e.dma_start(out=LPB.t[:, :, :], in_=latp_d[l, s].rearrange("(b p) f -> p b f", p=128)),
                     w=[LPB.r], dma=True)
                P.op("pool", lambda e, s=s: e.dma_start(out=KRPB.t[:, :, :], in_=krp_d[l, s].rearrange("(b p) f -> p b f", p=128)),
                     w=[KRPB.r], dma=True)
                for c in range(4):
                    b = bank()
                    for kb in range(8):
                        tr(bfv(b)[:, kb * 128:(kb + 1) * 128], LPB.t[:, kb, c * 128:(c + 1) * 128], IDB.t[:], [LPB.r, IDB.r], [b.r])
                    A("act", lambda e, b=b, c=c: e.copy(out=LATT.t[:, c, :], in_=bfv(b)[:, 0:1024]), r=[b.r], w=[LATT.r])
                b = bank()
                for kb in range(8):
                    tr(bfv(b)[0:64, kb * 128:(kb + 1) * 128], KRPB.t[:, kb, :], IDB.t[:], [KRPB.r, IDB.r], [b.r])
                A("act", lambda e, b=b: e.copy(out=KRP.t[0:64, 0:PAST], in_=bfv(b)[0:64, 0:1024]), r=[b.r], w=[KRP.r])
                run_heads(slice(s * 32, (s + 1) * 32))

        areset()
        MT = aal("MT", [32, TT], BF16)
        SG = [aal(f"SG{i}", [TT]) for i in range(3)]
        TM = aal("TM", [TT]); TM2 = aal("TM2", [TT])
        for dc in range(32):
            bg = []
            for br in range(3):
                b = projW(O_G + br * D + dc * 128, 128)
                A("act", lambda e, b=b, br=br, dc=dc: e.activation(out=SG[br].t[:, TK], in_=b.t[:, TK], func=AF.Sigmoid,
                                                                   bias=bias(l, O_G + br * D + dc * 128), scale=1.0),
                  r=[b.r, BIN.r], w=[SG[br].r])
            ys = []
            for (wi, (wsrc, nk, act)) in enumerate(((w_pc, 8, CVG), (w_pm, 8, HMG), (w_pa, 16, AOG))):
                wv, wr = wload(wsrc[l][:, dc * 128:(dc + 1) * 128], nk, 128, ("w_p", wi, l, dc))
                b = bank()
                for kc in range(nk):
                    mm(b.t[:, TK], wv[:, kc, :], act.t[:, kc, TK], kc == 0, kc == nk - 1, [wr, act.r], [b.r])
                ys.append(b)
            A("dve", lambda e, ys=ys: e.tensor_tensor(out=TM.t[:, TK], in0=ys[0].t[:, TK], in1=SG[0].t[:, TK], op=ALU.mult), r=[ys[0].r, SG[0].r], w=[TM.r])
            A("dve", lambda e, ys=ys: e.tensor_tensor(out=TM2.t[:, TK], in0=ys[1].t[:, TK], in1=SG[1].t[:, TK], op=ALU.mult), r=[ys[1].r, SG[1].r], w=[TM2.r])
            A("dve", lambda e: e.tensor_tensor(out=TM.t[:, TK], in0=TM.t[:, TK], in1=TM2.t[:, TK], op=ALU.add), r=[TM.r, TM2.r], w=[TM.r])
            A("dve", lambda e, ys=ys: e.tensor_tensor(out=TM2.t[:, TK], in0=ys[2].t[:, TK], in1=SG[2].t[:, TK], op=ALU.mult), r=[ys[2].r, SG[2].r], w=[TM2.r])
            A("dve", lambda e, dc=dc: e.tensor_tensor(out=MT.t[:, dc, TK], in0=TM.t[:, TK], in1=TM2.t[:, TK], op=ALU.add), r=[TM.r, TM2.r], w=[MT.r])

        if kind == "s" and l == 0:
            dbg("CVG", CVG.t[:, :, TK], CVG.r, [128, 8, 128], BF16); dbg("HMG", HMG.t[:, :, TK], HMG.r, [128, 8, 128], BF16)
            dbg("AOG", AOG.t[:, :, TK], AOG.r, [128, 16, 128], BF16); dbg("MT", MT.t[:, :, TK], MT.r, [128, 32, 128], BF16)
        for cg in range(8):
            bo = [abank(j) for j in range(nsub)]
            for kq in range(4):
                wv, wr = wload(w_out[l][kq * 1024:(kq + 1) * 1024, cg * 512:(cg + 1) * 512], 8, 512, ("w_out", l, kq, cg))
                for j in range(nsub):
                    for k8 in range(8):
                        kc = kq * 8 + k8
                        mm(bo[j].t[:, :], MT.t[:, kc, j * 128:(j + 1) * 128], wv[:, k8, :], kc == 0, kc == 31, [wr, MT.r], [bo[j].r])
            for j in range(nsub):
                A("dve", lambda e, j=j, bo=bo, cg=cg: e.tensor_tensor(out=Xs[j].t[:, cg * 512:(cg + 1) * 512], in0=Xs[j].t[:, cg * 512:(cg + 1) * 512],
                                                                      in1=bo[j].t[:, :], op=ALU.add), r=[Xs[j].r, bo[j].r], w=[Xs[j].r])
        if l == NLAYER - 1:
            for j in range(nsub):
                dst = y_p[pos0 + j * 128:pos0 + (j + 1) * 128, :] if kind == "p" else y_s[:, :]
                dma_out(dst, Xs[j].t[:], Xs[j].r)

    kvc_res = [[P.res(f"kvc{l}_{h}") for h in range(16)] for l in range(2)]
    krc_res = [P.res(f"krc{l}") for l in range(2)]
    tiles = []
    for i in range(NTILES_P):
        tiles.append(dict(kind="p", ntok=TT, pos0=i * TT, idx=i, last=(i == NPT - 1)))
    if DO_SAMPLE:
        tiles.append(dict(kind="s", ntok=128, pos0=PAST, idx=0, last=True))
    for tile in tiles:
        for l in range(NLAYER):
            process(tile, l)
    P.emit()
    st.close()
    return nc


def host_consts():
    ident = np.eye(128, dtype=np.float32)
    rot = np.zeros((64, 64), np.float32)
    for i in range(32):
        rot[i + 32, i] = -1.0
        rot[i, i + 32] = 1.0
    utri = np.triu(np.ones((64, 64), np.float32))
    half = 32
    freqs = (np.float32(10000.0) ** (-np.arange(half, dtype=np.float32) / np.float32(half))).astype(np.float32)

    def cs(pos):
        ang = pos.astype(np.float32)[None, :] * freqs[:, None]
        c = np.cos(ang).astype(np.float32)
        s = np.sin(ang).astype(np.float32)
        return np.stack([np.concatenate([c, c], 0), np.concatenate([s, s], 0)], axis=1)
    csp = cs(np.arange(SEQ))
    css = np.tile(cs(PAST + np.arange(32)), (1, 1, 4))
    am = np.zeros((128, 2, TT), np.float32)
    for kb in range(2):
        for p in range(128):
            kk = kb * 128 + p
            am[p, kb, :] = ((np.arange(TT) // 64) >= (kk // 64)).astype(np.float32)
    return dict(c_ident=ident, c_rot=rot, c_utri=utri, c_csp=np.ascontiguousarray(csp), c_css=np.ascontiguousarray(css),
                c_am=am, c_eye4=np.eye(4, dtype=np.float32))


_WNAMES = ["w_in", "w_pc", "w_pm", "w_uq", "w_ukv", "w_pa", "w_out"]


def host_params(inp):
    f = lambda a: np.asarray(a, dtype=np.float32)
    pc = lambda a, n: np.ascontiguousarray(f(a).reshape(2, n, 128).transpose(2, 0, 1))
    b_in = f(inp["b_in"])
    hb = np.zeros((128, 2, NBCH), np.float32)
    for l in range(2):
        hb[:, l, 0:40] = b_in[l, 0:5120].reshape(40, 128).T
        hb[0:8, l, 40] = b_in[l, 5120:5128]
        hb[:, l, 41:69] = b_in[l, 5128:8712].reshape(28, 128).T
        hb[0:64, l, 69] = b_in[l, 8712:8776]
        hb[:, l, 70:182] = b_in[l, 8776:NIN].reshape(112, 128).T
    return dict(h_lng=pc(inp["ln_g"], 32), h_bin=hb,
                h_cw=np.ascontiguousarray(f(inp["conv_w"]).reshape(2, 31, 8, 128).transpose(3, 0, 2, 1)),
                h_cb=pc(inp["conv_b"], 8), h_clg=pc(inp["conv_ln_g"], 8), h_clb=pc(inp["conv_ln_b"], 8), h_mng=pc(inp["m_norm_g"], 8),
                h_cqg=pc(inp["cq_g"], 8), h_ckvg=pc(inp["ckv_g"], 4),
                h_qng=np.ascontiguousarray(f(inp["qn_g"]).T), h_kng=np.ascontiguousarray(f(inp["kn_g"]).T),
                h_qrg=np.ascontiguousarray(f(inp["qr_g"]).T), h_krg=np.ascontiguousarray(f(inp["kr_g"]).T))


def kernel(**inp):
    cfg = {}
    nc = build(cfg)
    consts = host_consts()
    f = lambda a: np.ascontiguousarray(np.asarray(a, dtype=np.float32))
    W = {k: f(inp[k]) for k in _WNAMES}
    W.update(host_params(inp))
    in_maps = []
    for c in range(8):
        sl = slice(4 * c, 4 * c + 4)
        m = dict(W)
        m.update(consts)
        m["xp"] = f(inp["x_prompt"][c])
        m["xs"] = f(inp["x_sample"][sl]).reshape(128, D)
        m["latc"] = f(inp["cache_kv_latent"][:, sl])
        m["krc"] = f(inp["cache_k_rope"][:, sl])
        m["stconv"] = f(inp["state_conv"][:, sl])
        m["stC"] = f(inp["state_mlstm_C"][:, sl])
        m["stn"] = f(inp["state_mlstm_n"][:, sl])
        m["stm"] = f(inp["state_mlstm_m"][:, sl])
        in_maps.append(m)
    res = run_bass_kernel_spmd(nc, in_maps, core_ids=list(range(8)))
    R = res.results
    cat = lambda k, ax: np.concatenate([np.asarray(r[k]) for r in R], axis=ax)
    stk = lambda k, ax: np.stack([np.asarray(r[k]) for r in R], axis=ax)
    y_prompt = stk("y_p", 0)
    y_sample = cat("y_s", 0).reshape(32, 32, D)
    return (y_prompt, y_sample,
            stk("cs_p", 1), cat("cs_s", 1),
            stk("C_p", 1), stk("n_p", 1), stk("m_p", 1),
            cat("C_s", 1), cat("n_s", 1), cat("m_s", 1),
            stk("lat_p", 1), stk("kr_p", 1),
            cat("lat_s", 1).reshape(2, 32, 32, 512), cat("kr_s", 1).reshape(2, 32, 32, 64))
```
